# Optimizing a Trainium2 kernel written in Bass

```python
import math
import jax, jax.numpy as jnp
from jax import lax
import numpy as np

D_MODEL = 1024
BATCH = 4
SEQ = 8192
DEPTH = 1
DEC_BATCH = 8
DEC_SEQ = 2048
PAST_LEN = 128

MIX_WIDTH = D_MODEL
ATTN_WIDTH = MIX_WIDTH // 2
RET_WIDTH = MIX_WIDTH - ATTN_WIDTH
ATTN_HEAD_DIM = 64
N_ATTN_HEADS = ATTN_WIDTH // ATTN_HEAD_DIM
N_KV_HEADS = 2
GQA_GROUP = N_ATTN_HEADS // N_KV_HEADS
WINDOW = 128
ATTN_BLOCK = 128
ROT_DIM = ATTN_HEAD_DIM // 4
ROPE_THETA = 500000.0
RET_HEAD_DIM = 128
N_RET_HEADS = RET_WIDTH // RET_HEAD_DIM
RET_CHUNK = 128
RET_ROT_THETA = 10000.0
D_FF = 2816
EPS = 1e-6
NEG_BIG = -1e30
D_IN = (N_ATTN_HEADS * ATTN_HEAD_DIM + 2 * N_KV_HEADS * ATTN_HEAD_DIM
        + 4 * RET_WIDTH)

kernel_name = "hymba_swa_retention_macaron_encoder"


def _rmsnorm(x, g):
    xf = x.astype(jnp.float32)
    y = xf * lax.rsqrt(jnp.mean(xf * xf, axis=-1, keepdims=True) + EPS)
    return (y * g.astype(jnp.float32)).astype(x.dtype)


def _swiglu(x, w_gate, w_up, w_down):
    return (jax.nn.silu(x @ w_gate) * (x @ w_up)) @ w_down


def _rotate(x, cos, sin):
    half = x.shape[-1] // 2
    x1, x2 = x[..., :half], x[..., half:]
    return jnp.concatenate([x1 * cos - x2 * sin, x2 * cos + x1 * sin], axis=-1)


def _banded_sink_attention(q, k, v, sink):
    B, S = q.shape[0], q.shape[1]
    C = ATTN_BLOCK
    NB = S // C
    d = ATTN_HEAD_DIM
    qb = q.reshape(B, NB, C, N_KV_HEADS, GQA_GROUP, d) * (d ** -0.5)
    pad = ((0, 0), (C, C), (0, 0), (0, 0))
    kp = jnp.pad(k, pad).reshape(B, NB + 2, C, N_KV_HEADS, d)
    vp = jnp.pad(v, pad).reshape(B, NB + 2, C, N_KV_HEADS, d)
    kw = jnp.concatenate([kp[:, :-2], kp[:, 1:-1], kp[:, 2:]], axis=2)
    vw = jnp.concatenate([vp[:, :-2], vp[:, 1:-1], vp[:, 2:]], axis=2)
    scores = jnp.einsum('bnqhgd,bnkhd->bnhgqk', qb, kw,
                        preferred_element_type=jnp.float32)
    qpos = jnp.arange(NB)[:, None] * C + jnp.arange(C)[None, :]
    kpos = jnp.arange(NB)[:, None] * C - C + jnp.arange(3 * C)[None, :]
    rel = kpos[:, None, :] - qpos[:, :, None]
    valid = (jnp.abs(rel) <= WINDOW) & (kpos >= 0)[:, None, :] & (kpos < S)[:, None, :]
    scores = jnp.where(valid[None, :, None, None], scores, NEG_BIG)
    sink_l = sink.astype(jnp.float32).reshape(N_KV_HEADS, GQA_GROUP)[None, None, :, :, None, None]
    m = jnp.maximum(jnp.max(scores, axis=-1, keepdims=True), sink_l)
    p = jnp.exp(scores - m)
    p = p / (jnp.sum(p, axis=-1, keepdims=True) + jnp.exp(sink_l - m))
    out = jnp.einsum('bnhgqk,bnkhd->bnqhgd', p.astype(v.dtype), vw)
    return out.reshape(B, S, N_ATTN_HEADS * d)


def _retention_one_direction(q, k, v, log_decay):
    B, H, S, dk = q.shape
    dv = v.shape[-1]
    C = RET_CHUNK
    NC = S // C
    ld = log_decay.astype(jnp.float32)
    qc = q.astype(jnp.float32).reshape(B, H, NC, C, dk)
    kc = k.astype(jnp.float32).reshape(B, H, NC, C, dk)
    vc = v.astype(jnp.float32).reshape(B, H, NC, C, dv)
    idx = jnp.arange(C, dtype=jnp.float32)
    diff = idx[:, None] - idx[None, :]
    D = jnp.where(diff >= 0, jnp.exp(ld[:, None, None] * jnp.maximum(diff, 0.0)), 0.0)
    inner = jnp.einsum('bhnid,bhnjd->bhnij', qc, kc) * D[None, :, None]
    o_inner = jnp.einsum('bhnij,bhnje->bhnie', inner, vc)
    k_to_end = kc * jnp.exp(ld[:, None] * (C - 1.0 - idx))[None, :, None, :, None]
    kv_chunk = jnp.einsum('bhnjd,bhnje->nbhde', k_to_end, vc)
    chunk_decay = jnp.exp(ld * C)[None, :, None, None]

    def step(state, kv_n):
        return state * chunk_decay + kv_n, state

    _, states_prev = lax.scan(step, jnp.zeros((B, H, dk, dv), jnp.float32), kv_chunk)
    q_from_start = qc * jnp.exp(ld[:, None] * (idx + 1.0))[None, :, None, :, None]
    o_cross = jnp.einsum('bhnid,nbhde->bhnie', q_from_start, states_prev)
    return (o_inner + o_cross).reshape(B, H, S, dv)


def _layer(x, ffn1_norm, ffn1_w_gate, ffn1_w_up, ffn1_w_down, mix_norm, w_in,
           attn_sink, attn_out_norm, ret_log_decay_fwd, ret_log_decay_bwd, w_out,
           ffn2_norm, ffn2_w_gate, ffn2_w_up, ffn2_w_down):
    B, S, _ = x.shape
    h = x + 0.5 * _swiglu(_rmsnorm(x, ffn1_norm), ffn1_w_gate, ffn1_w_up, ffn1_w_down)

    u = _rmsnorm(h, mix_norm)
    proj = u @ w_in
    o1 = N_ATTN_HEADS * ATTN_HEAD_DIM
    o2 = o1 + N_KV_HEADS * ATTN_HEAD_DIM
    o3 = o2 + N_KV_HEADS * ATTN_HEAD_DIM
    o4 = o3 + RET_WIDTH
    o5 = o4 + RET_WIDTH
    o6 = o5 + RET_WIDTH
    aq, ak, av, rq, rk, rv, rg = jnp.split(proj, [o1, o2, o3, o4, o5, o6], axis=-1)

    pos = jnp.arange(S, dtype=jnp.float32)

    inv_a = ROPE_THETA ** (-jnp.arange(0, ROT_DIM, 2, dtype=jnp.float32) / ROT_DIM)
    ang_a = pos[:, None] * inv_a[None, :]
    cos_a = jnp.cos(ang_a)[None, :, None, :].astype(x.dtype)
    sin_a = jnp.sin(ang_a)[None, :, None, :].astype(x.dtype)
    aq = aq.reshape(B, S, N_ATTN_HEADS, ATTN_HEAD_DIM)
    ak = ak.reshape(B, S, N_KV_HEADS, ATTN_HEAD_DIM)
    av = av.reshape(B, S, N_KV_HEADS, ATTN_HEAD_DIM)
    aq = jnp.concatenate([_rotate(aq[..., :ROT_DIM], cos_a, sin_a), aq[..., ROT_DIM:]], axis=-1)
    ak = jnp.concatenate([_rotate(ak[..., :ROT_DIM], cos_a, sin_a), ak[..., ROT_DIM:]], axis=-1)
    attn_out = _rmsnorm(_banded_sink_attention(aq, ak, av, attn_sink), attn_out_norm)

    inv_r = RET_ROT_THETA ** (-jnp.linspace(0.0, 1.0, RET_HEAD_DIM // 2, dtype=jnp.float32))
    ang_r = pos[:, None] * inv_r[None, :]
    cos_r = jnp.cos(ang_r)[None, None].astype(x.dtype)
    sin_r = jnp.sin(ang_r)[None, None].astype(x.dtype)
    rq = rq.reshape(B, S, N_RET_HEADS, RET_HEAD_DIM).transpose(0, 2, 1, 3)
    rk = rk.reshape(B, S, N_RET_HEADS, RET_HEAD_DIM).transpose(0, 2, 1, 3)
    rv = rv.reshape(B, S, N_RET_HEADS, RET_HEAD_DIM).transpose(0, 2, 1, 3)
    rq = _rotate(rq, cos_r, sin_r)
    rk = _rotate(rk, cos_r, sin_r) * (RET_HEAD_DIM ** -0.5)
    ret_f = _retention_one_direction(rq, rk, rv, ret_log_decay_fwd)
    ret_b = jnp.flip(_retention_one_direction(jnp.flip(rq, 2), jnp.flip(rk, 2), jnp.flip(rv, 2),
                                              ret_log_decay_bwd), 2)
    ret = ret_f + ret_b
    mu = jnp.mean(ret, axis=-1, keepdims=True)
    var = jnp.mean(jnp.square(ret - mu), axis=-1, keepdims=True)
    ret = ((ret - mu) * lax.rsqrt(var + EPS)).astype(x.dtype)
    ret = ret.transpose(0, 2, 1, 3).reshape(B, S, RET_WIDTH)
    ret_out = jax.nn.silu(rg) * ret

    h = h + jnp.concatenate([attn_out, ret_out], axis=-1) @ w_out

    h = h + 0.5 * _swiglu(_rmsnorm(h, ffn2_norm), ffn2_w_gate, ffn2_w_up, ffn2_w_down)
    return h


def _trunk(x, ffn1_norm, ffn1_w_gate, ffn1_w_up, ffn1_w_down, mix_norm, w_in,
           attn_sink, attn_out_norm, ret_log_decay_fwd, ret_log_decay_bwd, w_out,
           ffn2_norm, ffn2_w_gate, ffn2_w_up, ffn2_w_down, final_norm):
    h = x
    for l in range(DEPTH):
        h = _layer(h, ffn1_norm[l], ffn1_w_gate[l], ffn1_w_up[l], ffn1_w_down[l],
                   mix_norm[l], w_in[l], attn_sink[l], attn_out_norm[l],
                   ret_log_decay_fwd[l], ret_log_decay_bwd[l], w_out[l],
                   ffn2_norm[l], ffn2_w_gate[l], ffn2_w_up[l], ffn2_w_down[l])
    return _rmsnorm(h, final_norm)


def setup_inputs(seed: int = 0) -> dict:
    key = jax.random.key(seed)
    ks = jax.random.split(key, 20)
    f32 = jnp.float32

    def w(k, shape, fan_in):
        return jax.random.normal(k, shape, f32) * (fan_in ** -0.5)

    def gain(k, shape):
        return 1.0 + 0.05 * jax.random.normal(k, shape, f32)

    base_decay = jnp.log1p(-jnp.exp2(-5.0 - jnp.arange(N_RET_HEADS, dtype=f32)))
    return {
        "x_prompt": jax.random.normal(ks[0], (BATCH, SEQ, D_MODEL), f32),
        "x_sample": jax.random.normal(ks[1], (DEC_BATCH, DEC_SEQ, D_MODEL), f32),
        "ffn1_norm": gain(ks[2], (DEPTH, D_MODEL)),
        "ffn1_w_gate": w(ks[3], (DEPTH, D_MODEL, D_FF), D_MODEL),
        "ffn1_w_up": w(ks[4], (DEPTH, D_MODEL, D_FF), D_MODEL),
        "ffn1_w_down": w(ks[5], (DEPTH, D_FF, D_MODEL), D_FF),
        "mix_norm": gain(ks[6], (DEPTH, D_MODEL)),
        "w_in": w(ks[7], (DEPTH, D_MODEL, D_IN), D_MODEL),
        "attn_sink": 0.5 * jax.random.normal(ks[8], (DEPTH, N_ATTN_HEADS), f32),
        "attn_out_norm": gain(ks[9], (DEPTH, ATTN_WIDTH)),
        "ret_log_decay_fwd": base_decay[None, :] * (1.0 + 0.05 * jax.random.normal(ks[10], (DEPTH, N_RET_HEADS), f32)),
        "ret_log_decay_bwd": base_decay[None, :] * (1.0 + 0.05 * jax.random.normal(ks[11], (DEPTH, N_RET_HEADS), f32)),
        "w_out": w(ks[12], (DEPTH, MIX_WIDTH, D_MODEL), MIX_WIDTH),
        "ffn2_norm": gain(ks[13], (DEPTH, D_MODEL)),
        "ffn2_w_gate": w(ks[14], (DEPTH, D_MODEL, D_FF), D_MODEL),
        "ffn2_w_up": w(ks[15], (DEPTH, D_MODEL, D_FF), D_MODEL),
        "ffn2_w_down": w(ks[16], (DEPTH, D_FF, D_MODEL), D_FF),
        "final_norm": gain(ks[17], (D_MODEL,)),
    }


def reference(x_prompt, x_sample, ffn1_norm, ffn1_w_gate, ffn1_w_up, ffn1_w_down,
              mix_norm, w_in, attn_sink, attn_out_norm, ret_log_decay_fwd,
              ret_log_decay_bwd, w_out, ffn2_norm, ffn2_w_gate, ffn2_w_up,
              ffn2_w_down, final_norm):
    y_prompt = _trunk(x_prompt, ffn1_norm, ffn1_w_gate, ffn1_w_up, ffn1_w_down, mix_norm,
                      w_in, attn_sink, attn_out_norm, ret_log_decay_fwd, ret_log_decay_bwd,
                      w_out, ffn2_norm, ffn2_w_gate, ffn2_w_up, ffn2_w_down, final_norm)
    y_sample = _trunk(x_sample, ffn1_norm, ffn1_w_gate, ffn1_w_up, ffn1_w_down, mix_norm,
                      w_in, attn_sink, attn_out_norm, ret_log_decay_fwd, ret_log_decay_bwd,
                      w_out, ffn2_norm, ffn2_w_gate, ffn2_w_up, ffn2_w_down, final_norm)
    return (y_prompt, y_sample)
```

```python
import contextlib
import numpy as np
import ml_dtypes
import concourse.bass as bass
import concourse.mybir as mybir
from concourse.bass_utils import run_bass_kernel_spmd

F32 = mybir.dt.float32
BF16 = mybir.dt.bfloat16
AF = mybir.ActivationFunctionType
ALU = mybir.AluOpType
AX = mybir.AxisListType

D = 1024
DFF = 2816
NFF = 22
C = 128
EPS = 1e-6
QS = [(0, 6), (6, 12), (12, 17), (17, 22)]


def _q_of(j):
    for qi, (a, b) in enumerate(QS):
        if a <= j < b:
            return qi


class Prog:
    EPOCH = 3000

    DEFCOST = {"pe": 0.6, "act": 0.6, "dve": 0.5, "pool": 1.15, "sp": 0.15}

    def __init__(self, nc):
        self.nc = nc
        self.ops = []
        self.sched = False
        self.extra_reads = []

    def add(self, eng, fn, reads=(), writes=(), dma=None, inc=16, cost=None):
        if cost is None:
            cost = 0.15 if dma is not None else self.DEFCOST[eng]
        tset = None
        if eng == "act" and self.sched and dma is None:
            import inspect
            try:
                src = inspect.getsource(fn)
            except Exception:
                src = ""
            if "AF.Silu" in src:
                tset = "silu"
            elif "AF.Ln" in src or "AF.Exp" in src:
                tset = "exp"
        self.ops.append(dict(eng=eng, fn=fn, reads=tuple(reads) + tuple(self.extra_reads), writes=tuple(writes), dma=dma, inc=inc,
                             cost=cost, sched=self.sched, tset=tset))

    def list_schedule(self, a, b):
        import heapq
        ops = self.ops
        n = b - a
        last_w = {}
        readers = {}
        preds = [set() for _ in range(n)]
        for i in range(a, b):
            op = ops[i]
            ps = preds[i - a]
            for r in op["reads"]:
                if r in last_w:
                    ps.add(last_w[r])
                if isinstance(r, tuple) and r[0] == "ps":
                    for x in readers.get(r, ()):
                        if ops[x]["eng"] != op["eng"]:
                            ps.add(x)
            for w in op["writes"]:
                if w in last_w:
                    ps.add(last_w[w])
                for x in readers.get(w, ()):
                    ps.add(x)
            ps.discard(i)
            for r in op["reads"]:
                readers.setdefault(r, []).append(i)
            for w in op["writes"]:
                last_w[w] = i
                readers[w] = []
        succs = [[] for _ in range(n)]
        indeg = [0] * n
        for i in range(n):
            for p in preds[i]:
                succs[p - a].append(i)
            indeg[i] = len(preds[i])
        LAT = 0.25
        DMA_LAT = 2.5
        finish = [0.0] * n
        tready = [0.0] * n
        heaps = {}
        eng_free = {}
        for i in range(n):
            if indeg[i] == 0:
                heapq.heappush(heaps.setdefault(ops[a + i]["eng"], []), (0.0, i))
        order = []
        act_set = [None]
        while len(order) < n:
            best = None
            for eng, hp in heaps.items():
                if not hp:
                    continue
                tr, i = hp[0]
                st = max(tr, eng_free.get(eng, 0.0))
                if best is None or (st, i) < best[:2]:
                    best = (st, i, eng)
            st, i, eng = best
            heapq.heappop(heaps[eng])
            op = ops[a + i]
            extra = 0.0
            if eng == "act":
                ts = op.get("tset")
                if ts is not None and ts != act_set[0]:
                    hp = heaps[eng]
                    popped = []
                    alt = None
                    while hp and len(popped) < 8:
                        tr2, i2 = heapq.heappop(hp)
                        if max(tr2, eng_free.get(eng, 0.0)) <= st + 0.3 and ops[a + i2].get("tset") in (None, act_set[0]):
                            alt = (tr2, i2)
                            break
                        popped.append((tr2, i2))
                    for it in popped:
                        heapq.heappush(hp, it)
                    if alt is not None:
                        heapq.heappush(hp, (tready[i], i))
                        i = alt[1]
                        op = ops[a + i]
                        st = max(alt[0], eng_free.get(eng, 0.0))
                        ts = op.get("tset")
                if ts is not None and ts != act_set[0]:
                    extra = 1.3
                    act_set[0] = ts
            eng_free[eng] = st + op["cost"] + extra
            finish[i] = st + op["cost"] + extra + (DMA_LAT if op["dma"] is not None else 0.0)
            order.append(a + i)
            for j in succs[i]:
                tready[j] = max(tready[j], finish[i] + LAT)
                indeg[j] -= 1
                if indeg[j] == 0:
                    heapq.heappush(heaps.setdefault(ops[a + j]["eng"], []), (tready[j], j))
        self.ops[a:b] = [ops[k] for k in order]
        print("list_schedule", a, b, "est_us", round(max(finish), 1))

    def emit(self):
        import os
        nc = self.nc
        tr = int(os.environ.get("K_TRUNC", "0"))
        if tr:
            self.ops = self.ops[:tr]
        if os.environ.get("K_NOSCHED", "0") != "1":
            i = 0
            while i < len(self.ops):
                if self.ops[i]["sched"]:
                    j = i
                    while j < len(self.ops) and self.ops[j]["sched"]:
                        j += 1
                    self.list_schedule(i, j)
                    i = j
                else:
                    i += 1
        ops = self.ops
        print("n_ops", len(ops))
        last_w = {}
        readers = {}
        for i, op in enumerate(ops):
            raw = set()
            other = set()
            for r in op["reads"]:
                if r in last_w:
                    raw.add(last_w[r])
                if isinstance(r, tuple) and r[0] == "ps":
                    for x in readers.get(r, ()):
                        if ops[x]["eng"] != op["eng"]:
                            other.add(x)
            for w in op["writes"]:
                if w in last_w:
                    other.add(last_w[w])
                for x in readers.get(w, ()):
                    other.add(x)
            deps = set()
            for d in raw | other:
                if d == i:
                    continue
                od = ops[d]
                if od["dma"] is None and od["eng"] == op["eng"]:
                    if op["eng"] == "pe":
                        continue
                    if d not in raw:
                        continue
                deps.add(d)
            op["deps"] = deps
            for r in op["reads"]:
                readers.setdefault(r, []).append(i)
            for w in op["writes"]:
                last_w[w] = i
                readers[w] = []
        need = [False] * len(ops)
        for op in ops:
            for d in op["deps"]:
                need[d] = True
        cnt = {}
        dcnt = {}
        semnames = set()
        for i, op in enumerate(ops):
            if op["dma"] is not None:
                k = op["dma"]
                dcnt[k] = dcnt.get(k, 0) + 1
                op["sig"] = (("dma", k), dcnt[k] * op["inc"])
                semnames.add(("dma", k))
            elif need[i]:
                c = cnt.get(op["eng"], 0)
                cnt[op["eng"]] = c + 1
                sn = ("eng", op["eng"], c // self.EPOCH)
                op["sig"] = (sn, c % self.EPOCH + 1)
                semnames.add(sn)
            else:
                op["sig"] = None
        semnames = sorted(semnames, key=str)
        with contextlib.ExitStack() as st:
            sems = {}
            for i, sn in enumerate(semnames):
                sems[sn] = st.enter_context(nc.semaphore("s%d" % i))
            block = st.enter_context(nc.Block())
            final_dma = {}
            for op in ops:
                if op["dma"] is not None:
                    final_dma[op["sig"][0]] = max(final_dma.get(op["sig"][0], 0), op["sig"][1])

            def run_engine(engname, e):
                waited = {}
                for op in ops:
                    if op["eng"] != engname:
                        continue
                    want = {}
                    for d in op["deps"]:
                        sn, val = ops[d]["sig"]
                        if want.get(sn, 0) < val:
                            want[sn] = val
                    for sn, val in want.items():
                        if waited.get(sn, 0) >= val:
                            continue
                        e.wait_ge(sems[sn], val)
                        waited[sn] = val
                    ins = op["fn"](e)
                    if op["sig"] is not None:
                        sn, val = op["sig"]
                        if op["dma"] is not None:
                            if op["inc"] == 16:
                                ins.then_inc(sems[sn], 16)
                            else:
                                ins.then_inc(sems[sn])
                        else:
                            ins.then_inc(sems[sn], 1)
                if engname == "sp":
                    for sn, val in final_dma.items():
                        if waited.get(sn, 0) < val:
                            e.wait_ge(sems[sn], val)

            @block.tensor
            def _(e):
                run_engine("pe", e)

            @block.scalar
            def _(e):
                run_engine("act", e)

            @block.vector
            def _(e):
                run_engine("dve", e)

            @block.gpsimd
            def _(e):
                run_engine("pool", e)

            @block.sync
            def _(e):
                run_engine("sp", e)


def build(NPC, NSC, stop_after=9):
    NCH = NPC + NSC
    NTOK = NCH * C
    assert NCH % 4 == 0
    nc = bass.Bass("TRN2", target_bir_lowering=False)
    P = Prog(nc)

    def din(name, shape, dt=F32):
        return nc.dram_tensor(name, shape, dt, kind="ExternalInput").ap()

    xin = din("xin", [NTOK, D])
    wg1 = din("wg1", [D, DFF]); wu1 = din("wu1", [D, DFF]); wd1 = din("wd1", [DFF, D])
    wg2 = din("wg2", [D, DFF]); wu2 = din("wu2", [D, DFF]); wd2 = din("wd2", [DFF, D])
    win = din("win", [D, DFF]); wout = din("wout", [D, D])
    gains = din("gains", [128, 36])
    gfin = din("gfin", [D])
    sink = din("sink", [8]); ldf_d = din("ldf", [4]); ldb_d = din("ldb", [4])
    cst = din("cst", [128, 772])
    flg = din("flg", [128, 2])
    tab = din("tab", [NCH, 128, 144])
    yout = nc.dram_tensor("yout", [NTOK, D], F32, kind="ExternalOutput").ap()
    h_s = nc.dram_tensor("h_s", [NTOK, D], F32).ap()
    pkA = nc.dram_tensor("pkA", [NCH, 128, 320], BF16).ap()
    pkB = nc.dram_tensor("pkB", [NCH, 128, 4096], BF16).ap()
    xkv_in = nc.dram_tensor("xkv_in", [128, 640], BF16)
    xkv_out = nc.dram_tensor("xkv_out", [256, 640], BF16)
    xst_in = nc.dram_tensor("xst_in", [128, 1024], F32)
    xst_out = nc.dram_tensor("xst_out", [256, 1024], F32)

    ARENA_BYTES = 207360
    arena = nc.alloc_sbuf_tensor("arena", [128, ARENA_BYTES // 2], BF16)

    def A(off, n, dt):
        assert off % 64 == 0
        bpe = 4 if dt == F32 else 2
        assert off + n * bpe <= ARENA_BYTES, (off, n)
        v = arena[:, off // 2: off // 2 + n * bpe // 2]
        if dt == F32:
            v = v.bitcast(F32)
        return v

    def v3(ap, a):
        return ap.rearrange("p (a b) -> p a b", a=a)

    banks = [nc.alloc_psum_tensor("psb%d" % i, [128, 512], F32) for i in range(8)]

    def PS(i):
        return banks[i][:]

    def PSB(i):
        return PS(i).bitcast(BF16)

    W0 = 0
    W1 = 90112
    ACT0 = 135168
    CONST0 = 199424
    Wg = v3(A(W0, 8 * DFF, BF16), 8)
    Wu = v3(A(W0 + 45056, 8 * DFF, BF16), 8)
    Wd = v3(A(W1, NFF * D, BF16), NFF)
    Win = v3(A(W1, 8 * DFF, BF16), 8)
    Wout = v3(A(W1, 8 * D, BF16), 8)
    SbS_off = W1 + 16384
    o = CONST0
    identb = A(o, 128, BF16); o += 256
    Mprev = A(o, 128, BF16); o += 256
    Mnext = A(o, 128, BF16); o += 256
    MprevH = A(o, 128, BF16); o += 256
    MnextH = A(o, 128, BF16); o += 256
    DTq = A(o, 512, F32); o += 2048
    GF_OFF = o
    WF = A(o, 512, F32); o += 2048
    WB = A(o, 512, F32); o += 2048
    gn = A(o, 36, F32); o += 192
    small = A(o, 64, F32); o += 256
    ldf = small[:, 0:4]; ldb = small[:, 4:8]; nldf = small[:, 8:12]; cdf = small[:, 12:16]; cdb = small[:, 16:20]
    wkf = small[:, 20:24]; wkb = small[:, 24:28]; esink = small[:, 28:36]; flags = small[:, 36:38]
    epsc = small[:, 38:39]; e1c = small[:, 39:43]
    assert o <= ARENA_BYTES
    CSTF = A(ACT0, 772, F32)
    SETUP_TMP = A(ACT0 + 4096, 512, F32)

    P.add("sp", lambda e: e.dma_start(out=CSTF, in_=cst), writes=["cstf"], dma="cst")
    P.add("sp", lambda e: e.dma_start(out=gn, in_=gains), writes=["gn"], dma="gn")
    P.add("sp", lambda e: e.dma_start(out=ldf, in_=ldf_d.partition_broadcast(128)), writes=["ldf"], dma="ldf")
    P.add("sp", lambda e: e.dma_start(out=ldb, in_=ldb_d.partition_broadcast(128)), writes=["ldb"], dma="ldb")
    P.add("sp", lambda e: e.dma_start(out=esink, in_=sink.partition_broadcast(128)), writes=["esink"], dma="sink")
    P.add("sp", lambda e: e.dma_start(out=flags, in_=flg), writes=["flags"], dma="flg")
    c_ident = CSTF[:, 0:128]; c_L = CSTF[:, 128:256]; c_U = CSTF[:, 256:384]; c_JI = CSTF[:, 384:512]
    c_I1 = CSTF[:, 512:640]; c_CI = CSTF[:, 640:768]; c_P1 = CSTF[:, 768:769]; c_PC = CSTF[:, 769:770]; c_P0 = CSTF[:, 770:771]
    SCALE_K = float(128 ** -0.5)
    P.add("dve", lambda e: e.tensor_copy(out=identb, in_=c_ident), reads=["cstf"], writes=["identb"])
    P.add("dve", lambda e: e.tensor_copy(out=Mprev, in_=c_U), reads=["cstf"], writes=["Mprev"])
    P.add("dve", lambda e: e.tensor_copy(out=Mnext, in_=c_L), reads=["cstf"], writes=["Mnext"])
    P.add("dve", lambda e: e.tensor_scalar(out=MprevH, in0=c_U, scalar1=flags[:, 0:1], scalar2=None, op0=ALU.mult),
          reads=["cstf", "flags"], writes=["MprevH"])
    P.add("dve", lambda e: e.tensor_scalar(out=MnextH, in0=c_L, scalar1=flags[:, 1:2], scalar2=None, op0=ALU.mult),
          reads=["cstf", "flags"], writes=["MnextH"])
    P.add("dve", lambda e: e.memset(epsc, EPS), writes=["epsc"])
    P.add("dve", lambda e: e.tensor_scalar(out=nldf, in0=ldf, scalar1=-1.0, scalar2=None, op0=ALU.mult),
          reads=["ldf"], writes=["nldf"])
    P.add("act", lambda e: e.activation(out=esink, in_=esink, func=AF.Exp), reads=["esink"], writes=["esink"])
    P.add("act", lambda e: e.activation(out=cdf, in_=ldf, func=AF.Exp, scale=float(C)), reads=["ldf"], writes=["cdf"])
    P.add("act", lambda e: e.activation(out=cdb, in_=ldb, func=AF.Exp, scale=float(C)), reads=["ldb"], writes=["cdb"])
    for h in range(4):
        hs = slice(h * 128, (h + 1) * 128)
        P.add("act", lambda e, h=h, hs=hs: e.activation(out=WF[:, hs], in_=c_I1, func=AF.Exp, scale=ldf[:, h:h + 1]),
              reads=["cstf", "ldf"], writes=[("WF", h)])
        P.add("act", lambda e, h=h, hs=hs: e.activation(out=WB[:, hs], in_=c_CI, func=AF.Exp, scale=ldb[:, h:h + 1]),
              reads=["cstf", "ldb"], writes=[("WB", h)])
        P.add("act", lambda e, h=h: e.activation(out=wkf[:, h:h + 1], in_=c_PC, func=AF.Exp, scale=ldf[:, h:h + 1]),
              reads=["cstf", "ldf"], writes=[("wkf", h)])
        P.add("act", lambda e, h=h: e.activation(out=wkb[:, h:h + 1], in_=c_P0, func=AF.Exp, scale=ldb[:, h:h + 1]),
              reads=["cstf", "ldb"], writes=[("wkb", h)])
        P.add("act", lambda e, h=h: e.activation(out=e1c[:, h:h + 1], in_=c_P1, func=AF.Exp, scale=nldf[:, h:h + 1]),
              reads=["cstf", "nldf"], writes=[("e1c", h)])
        tmp = SETUP_TMP[:, 0:128]; tmp2 = SETUP_TMP[:, 128:256]; tmp3 = SETUP_TMP[:, 256:384]
        P.add("dve", lambda e, h=h, tmp=tmp: e.tensor_scalar(out=tmp, in0=c_JI, scalar1=ldb[:, h:h + 1], scalar2=None, op0=ALU.mult),
              reads=["cstf", "ldb"], writes=["stmp"])
        P.add("dve", lambda e, h=h, tmp=tmp, tmp2=tmp2: e.scalar_tensor_tensor(out=tmp2, in0=c_I1, scalar=nldf[:, h:h + 1], in1=tmp,
                                                                        op0=ALU.mult, op1=ALU.add),
              reads=["cstf", "nldf", "stmp"], writes=["stmp2"])
        P.add("act", lambda e, tmp2=tmp2, tmp3=tmp3: e.activation(out=tmp3, in_=tmp2, func=AF.Exp), reads=["stmp2"], writes=["stmp3"])
        P.add("dve", lambda e, tmp3=tmp3: e.tensor_tensor(out=tmp3, in0=tmp3, in1=c_U, op=ALU.mult), reads=["stmp3", "cstf"], writes=["stmp3"])
        P.add("dve", lambda e, h=h, tmp3=tmp3, hs=hs: e.scalar_tensor_tensor(out=DTq[:, hs], in0=c_L, scalar=e1c[:, h:h + 1], in1=tmp3,
                                                                          op0=ALU.mult, op1=ALU.add),
              reads=["cstf", ("e1c", h), "stmp3"], writes=[("DTq", h)])
        P.add("dve", lambda e, hs=hs: e.tensor_scalar(out=DTq[:, hs], in0=DTq[:, hs], scalar1=SCALE_K, scalar2=None, op0=ALU.mult),
              reads=[("DTq", h)], writes=[("DTq", h)])
        P.add("dve", lambda e, h=h: e.tensor_scalar(out=wkf[:, h:h + 1], in0=wkf[:, h:h + 1], scalar1=SCALE_K, scalar2=None, op0=ALU.mult),
              reads=[("wkf", h)], writes=[("wkf", h)])
        P.add("dve", lambda e, h=h: e.tensor_scalar(out=wkb[:, h:h + 1], in0=wkb[:, h:h + 1], scalar1=SCALE_K, scalar2=None, op0=ALU.mult),
              reads=[("wkb", h)], writes=[("wkb", h)])
    CONST_KEYS = ["identb", "Mprev", "Mnext", "MprevH", "MnextH", "epsc", "cdf", "cdb", "esink", "gn"] + \
        [(nm, h) for nm in ("WF", "WB", "wkf", "wkb", "DTq") for h in range(4)]
    P.add("dve", lambda e: e.memset(SETUP_TMP[:, 384:385], 0.0), reads=CONST_KEYS + ["cstf", "stmp3"], writes=["setup_done"])

    def load_w_kf(tag, dst, src, key):
        sv = src.rearrange("(k p) f -> p k f", p=128)
        for qi, (a, b) in enumerate(QS):
            P.add("pool", lambda e, a=a, b=b: e.dma_start(out=dst[:, :, a * 128:b * 128], in_=sv[:, :, a * 128:b * 128]),
                  reads=["setup_done"], writes=[(key, qi)], dma=(tag, key, qi))

    def load_w_d(tag, dst, src, key, after=()):
        sv = src.rearrange("(j p) f -> p j f", p=128)
        for qi, (a, b) in enumerate(QS):
            P.add("pool", lambda e, a=a, b=b: e.dma_start(out=dst[:, a:b, :], in_=sv[:, a:b, :]),
                  reads=["setup_done"] + list(after), writes=[(key, qi)], dma=(tag, key, qi))

    def ffn_phase(tag, src, dst, srckey, dstkey, gcol, final, load_gu, wdsrc, after):
        o = ACT0
        X = [A(o + i * 4096, D, F32) for i in range(2)]; o += 8192
        XN = [A(o + i * 2048, D, BF16) for i in range(4)]; o += 8192
        XT = v3(A(o, 8 * 512, BF16), 8); o += 8192
        HT = v3(A(o, NFF * 512, BF16), NFF); o += NFF * 1024
        SG = [A(o + i * 2048, 512, F32) for i in range(2)]; o += 4096
        XR = [A(o + i * 4096, D, F32) for i in range(3)]; o += 12288
        JUNKF = A(o - 12288 - 4096, D, BF16)
        MS = A(o, 32, F32); o += 128
        GF = None
        if final:
            GF = A(GF_OFF, D, F32)
        assert o <= CONST0, o
        if load_gu:
            load_gu()
        load_w_d(tag, Wd, wdsrc, "Wd", after)
        if final:
            P.add("sp", lambda e: e.dma_start(out=GF, in_=gfin.partition_broadcast(128)), reads=["setup_done"] + list(after),
                  writes=["GF"], dma="gf")
        NB = NCH // 4

        def norm(b):
            for t in range(4):
                n = 4 * b + t
                xs = n % 2
                P.add("sp", lambda e, n=n, xs=xs: e.dma_start(out=X[xs], in_=src[n * 128:(n + 1) * 128, :]),
                      reads=[(srckey, n), "setup_done"] + list(after), writes=[("X", xs)], dma=(tag, "x", xs))
                msn = (tag, "ms", n % 4)
                P.add("act", lambda e, xs=xs, n=n, t=t: e.activation(out=XN[t], in_=X[xs], func=AF.Square, scale=1.0 / 32.0,
                                                            accum_out=MS[:, (n % 4) * 4:(n % 4) * 4 + 1]),
                      reads=[("X", xs)], writes=[msn, ("XN", t)])
                P.add("act", lambda e, n=n: e.activation(out=MS[:, (n % 4) * 4 + 1:(n % 4) * 4 + 2], in_=MS[:, (n % 4) * 4:(n % 4) * 4 + 1],
                                                    func=AF.Ln, bias=epsc, scale=1.0),
                      reads=[msn, "epsc"], writes=[(tag, "ln", n % 4)])
                P.add("act", lambda e, n=n: e.activation(out=MS[:, (n % 4) * 4 + 2:(n % 4) * 4 + 3], in_=MS[:, (n % 4) * 4 + 1:(n % 4) * 4 + 2],
                                                    func=AF.Exp, scale=-0.5),
                      reads=[(tag, "ln", n % 4)], writes=[(tag, "rstd", n % 4)])
                P.add("dve", lambda e, xs=xs, t=t, n=n: e.tensor_scalar(out=XN[t], in0=X[xs], scalar1=MS[:, (n % 4) * 4 + 2:(n % 4) * 4 + 3],
                                                                  scalar2=None, op0=ALU.mult),
                      reads=[("X", xs), (tag, "rstd", n % 4)], writes=[("XN", t)])

        def transp(b):
            for t in range(4):
                bk = 0 if t % 2 == 0 else 7

                def f(e, t=t, bk=bk):
                    for k in range(8):
                        ins = e.transpose(PSB(bk)[:, k * 128:(k + 1) * 128], XN[t][:, k * 128:(k + 1) * 128], identb)
                    return ins
                P.add("pe", f, reads=[("XN", t), "identb"], writes=[("ps", bk)])
                P.add("dve", lambda e, t=t, bk=bk: e.tensor_tensor(out=XT[:, :, t * 128:(t + 1) * 128], in0=v3(PSB(bk), 8),
                                                               in1=gn[:, gcol:gcol + 8].unsqueeze(2).to_broadcast([128, 8, 128]),
                                                               op=ALU.mult),
                      reads=[("ps", bk), "gn"], writes=[("XT", t)])

        def gateup(b):
            for j in range(NFF):
                q = _q_of(j)
                pg = 1 + j % 2
                pu = 3 + j % 2

                def fg(e, j=j, pg=pg):
                    for k in range(8):
                        ins = e.matmul(PS(pg), Wg[:, k, j * 128:(j + 1) * 128], XT[:, k, :], start=(k == 0), stop=(k == 7))
                    return ins

                def fu(e, j=j, pu=pu):
                    for k in range(8):
                        ins = e.matmul(PS(pu), Wu[:, k, j * 128:(j + 1) * 128], XT[:, k, :], start=(k == 0), stop=(k == 7))
                    return ins
                P.add("pe", fg, reads=[("Wg", q)] + [("XT", t) for t in range(4)], writes=[("ps", pg)])
                P.add("pe", fu, reads=[("Wu", q)] + [("XT", t) for t in range(4)], writes=[("ps", pu)])
                P.add("act", lambda e, j=j, pg=pg: e.activation(out=SG[j % 2], in_=PS(pg), func=AF.Silu),
                      reads=[("ps", pg)], writes=[("SG", j % 2)])
                P.add("dve", lambda e, j=j, pu=pu: e.tensor_tensor(out=HT[:, j, :], in0=SG[j % 2], in1=PS(pu), op=ALU.mult),
                      reads=[("SG", j % 2), ("ps", pu)], writes=[("HT", j)])

        def down(b):
            for t in range(4):
                n = 4 * b + t
                rs = n % 3
                P.add("sp", lambda e, n=n, rs=rs: e.dma_start(out=XR[rs], in_=src[n * 128:(n + 1) * 128, :]),
                      reads=[(srckey, n), "setup_done"] + list(after), writes=[("XR", rs, 0), ("XR", rs, 1)], dma=(tag, "xr", rs))
                for half in range(2):
                    pd = 5 + half

                    def fd(e, t=t, half=half, pd=pd):
                        for j in range(NFF):
                            ins = e.matmul(PS(pd), HT[:, j, t * 128:(t + 1) * 128], Wd[:, j, half * 512:(half + 1) * 512],
                                           start=(j == 0), stop=(j == NFF - 1))
                        return ins
                    P.add("pe", fd, reads=[("HT", j) for j in range(NFF)] + [("Wd", q) for q in range(4)], writes=[("ps", pd)])
                    P.add("dve", lambda e, rs=rs, half=half, pd=pd: e.scalar_tensor_tensor(
                        out=XR[rs][:, half * 512:(half + 1) * 512], in0=PS(pd), scalar=0.5, in1=XR[rs][:, half * 512:(half + 1) * 512],
                        op0=ALU.mult, op1=ALU.add), reads=[("ps", pd), ("XR", rs, half)], writes=[("XR", rs, half)])
                if final:
                    c0 = 16
                    P.add("act", lambda e, rs=rs: e.activation(out=JUNKF, in_=XR[rs], func=AF.Square, scale=1.0 / 32.0,
                                                            accum_out=MS[:, c0:c0 + 1]),
                          reads=[("XR", rs, 0), ("XR", rs, 1)], writes=["fms", ("SG", 0)])
                    P.add("act", lambda e: e.activation(out=MS[:, c0 + 1:c0 + 2], in_=MS[:, c0:c0 + 1], func=AF.Ln, bias=epsc, scale=1.0),
                          reads=["fms", "epsc"], writes=["fln"])
                    P.add("act", lambda e: e.activation(out=MS[:, c0 + 2:c0 + 3], in_=MS[:, c0 + 1:c0 + 2], func=AF.Exp, scale=-0.5),
                          reads=["fln"], writes=["frs"])
                    P.add("dve", lambda e, rs=rs: e.scalar_tensor_tensor(out=XR[rs], in0=XR[rs], scalar=MS[:, c0 + 2:c0 + 3], in1=GF,
                                                                      op0=ALU.mult, op1=ALU.mult),
                          reads=["frs", "GF", ("XR", rs, 0), ("XR", rs, 1)], writes=[("XR", rs, 0), ("XR", rs, 1)])
                P.add("sp", lambda e, n=n, rs=rs: e.dma_start(out=dst[n * 128:(n + 1) * 128, :], in_=XR[rs]),
                      reads=[("XR", rs, 0), ("XR", rs, 1)], writes=[(dstkey, n)], dma=(tag, "st", rs))

        norm(0)
        transp(0)
        for b in range(NB):
            if b + 1 < NB:
                norm(b + 1)
            gateup(b)
            if b + 1 < NB:
                transp(b + 1)
            down(b)

    def load_gu1():
        svg = wg1.rearrange("(k p) f -> p k f", p=128)
        svu = wu1.rearrange("(k p) f -> p k f", p=128)
        for qi, (a, b) in enumerate(QS):
            P.add("pool", lambda e, a=a, b=b: e.dma_start(out=Wg[:, :, a * 128:b * 128], in_=svg[:, :, a * 128:b * 128]),
                  reads=["setup_done"], writes=[("Wg", qi)], dma=("p1", "Wg", qi))
            P.add("pool", lambda e, a=a, b=b: e.dma_start(out=Wu[:, :, a * 128:b * 128], in_=svu[:, :, a * 128:b * 128]),
                  reads=["setup_done"], writes=[("Wu", qi)], dma=("p1", "Wu", qi))
    ffn_phase("p1", xin, h_s, "xin", "h_s", 0, False, load_gu1, wd1, ())

    def finish_copy():
        for n in range(NCH):
            P.add("sp", lambda e, n=n: e.dma_start(out=yout[n * 128:(n + 1) * 128, :], in_=h_s[n * 128:(n + 1) * 128, :]),
                  reads=[("h_s", n)], writes=[("yout", n)], dma=("fin", n % 4))
        P.emit()
        return nc
    if stop_after == 1:
        return finish_copy()
    P.sched = True
    P.extra_reads = [("h_s", NCH - 1 - i) for i in range(3)]
    load_w_kf("p2", Win, win, "Wd")
    load_w_kf("p4", Wg, wg2, "Wg")
    load_w_kf("p4", Wu, wu2, "Wu")
    P1_DONE = [("h_s", NCH - 1 - i) for i in range(3)]
    o = ACT0
    HIN = [A(o + i * 4096, D, F32) for i in range(2)]; o += 8192
    UN = [A(o + i * 2048, D, BF16) for i in range(2)]; o += 4096
    UT = [v3(A(o + i * 2048, D, BF16), 8) for i in range(2)]; o += 4096
    TAB = [A(o + i * 576, 144, F32) for i in range(2)]; o += 1152
    PKBt = [A(o + i * 8192, 4096, BF16) for i in range(2)]; o += 16384
    PKAt = [A(o + i * 640, 320, BF16) for i in range(2)]; o += 1280
    RT2 = [[A(o + (j * 8 + i) * 1024, 256, F32) for i in range(8)] for j in range(2)]; o += 16384
    RQTM2 = [A(o + j * 1024, 512, BF16) for j in range(2)]; o += 2048
    RKTM2 = [A(o + j * 1024, 512, BF16) for j in range(2)]; o += 2048
    AQTM2 = [A(o + j * 1024, 512, BF16) for j in range(2)]; o += 2048
    AKTM2 = [A(o + j * 256, 128, BF16) for j in range(2)]; o += 512
    MS2 = A(o, 16, F32); o += 64
    TOT = A(o, 1024, F32); o += 4096
    CDP = A(o, 16, F32); o += 64
    assert o <= CONST0, o
    for i in range(2):
        P.add("pool", lambda e, i=i: e.memset(PKAt[i][:, 128:320], 0.0), reads=["setup_done"] + P1_DONE,
              writes=[("PKA", i)])
        P.add("pool", lambda e, i=i: e.memset(v3(PKAt[i][:, 128:320], 2)[:, :, 64:65], 1.0), reads=[("PKA", i)], writes=[("PKA", i)])
    P.add("pool", lambda e: e.memset(TOT, 0.0), reads=["setup_done"] + P1_DONE, writes=[("TOTf", h) for h in range(4)] + [("TOTb", h) for h in range(4)])

    def rotary(eng_mul, src3, cos, sin, dst3, half, tmps, rkeys, wkey, nh):
        x1 = src3[:, :, 0:half]; x2 = src3[:, :, half:2 * half]
        cb = cos.unsqueeze(1).to_broadcast([128, nh, half]); sb = sin.unsqueeze(1).to_broadcast([128, nh, half])
        t = [v3(tm[:, 0:nh * half], nh) for tm in tmps]
        tk = [("RT", id(tm)) for tm in tmps]
        P.add("dve", lambda e: e.tensor_tensor(out=t[0], in0=x1, in1=cb, op=ALU.mult), reads=rkeys, writes=[tk[0]])
        P.add("dve", lambda e: e.tensor_tensor(out=t[1], in0=x2, in1=sb, op=ALU.mult), reads=rkeys, writes=[tk[1]])
        P.add("dve", lambda e: e.tensor_tensor(out=t[2], in0=x2, in1=cb, op=ALU.mult), reads=rkeys, writes=[tk[2]])
        P.add("dve", lambda e: e.tensor_tensor(out=t[3], in0=x1, in1=sb, op=ALU.mult), reads=rkeys, writes=[tk[3]])
        P.add("pool", lambda e: e.tensor_tensor(out=dst3[:, :, 0:half], in0=t[0], in1=t[1], op=ALU.subtract),
              reads=[tk[0], tk[1]], writes=[(wkey, 0)])
        P.add("pool", lambda e: e.tensor_tensor(out=dst3[:, :, half:2 * half], in0=t[2], in1=t[3], op=ALU.add),
              reads=[tk[2], tk[3]], writes=[(wkey, 1)])

    for n in range(NCH):
        s = n % 2
        is_prompt = n < NPC
        RT = RT2[s]; RQTM = RQTM2[s]; RKTM = RKTM2[s]; AQTM = AQTM2[s]; AKTM = AKTM2[s]
        P.add("sp", lambda e, n=n, s=s: e.dma_start(out=HIN[s], in_=h_s[n * 128:(n + 1) * 128, :]),
              reads=[("h_s", n), "setup_done"] + P1_DONE, writes=[("HIN", s)], dma=("p2", "hin", s))
        P.add("sp", lambda e, n=n, s=s: e.dma_start(out=TAB[s], in_=tab[n]), reads=["setup_done"] + P1_DONE,
              writes=[("TAB", s)], dma=("p2", "tab", s))
        P.add("act", lambda e, s=s: e.activation(out=UN[s], in_=HIN[s], func=AF.Square, scale=1.0 / 32.0, accum_out=MS2[:, s * 4:s * 4 + 1]),
              reads=[("HIN", s)], writes=[("ms2", s), ("UN", s)], cost=1.0)
        P.add("act", lambda e, s=s: e.activation(out=MS2[:, s * 4 + 1:s * 4 + 2], in_=MS2[:, s * 4:s * 4 + 1], func=AF.Ln, bias=epsc, scale=1.0),
              reads=[("ms2", s), "epsc"], writes=[("ln2", s)])
        P.add("act", lambda e, s=s: e.activation(out=MS2[:, s * 4 + 2:s * 4 + 3], in_=MS2[:, s * 4 + 1:s * 4 + 2], func=AF.Exp, scale=-0.5),
              reads=[("ln2", s)], writes=[("rs2", s)])
        P.add("act", lambda e, s=s: e.activation(out=UN[s], in_=HIN[s], func=AF.Copy, scale=MS2[:, s * 4 + 2:s * 4 + 3]),
              reads=[("HIN", s), ("rs2", s)], writes=[("UN", s)], cost=1.0)

        def ftr(e, s=s):
            for k in range(8):
                ins = e.transpose(PSB(0)[:, k * 128:(k + 1) * 128], UN[s][:, k * 128:(k + 1) * 128], identb)
            return ins
        P.add("pe", ftr, reads=[("UN", s), "identb"], writes=[("ps", 0)], cost=0.9)
        P.add("dve", lambda e, s=s: e.tensor_tensor(out=UT[s], in0=v3(PSB(0), 8), in1=gn[:, 8:16].unsqueeze(2).to_broadcast([128, 8, 128]),
                                                 op=ALU.mult), reads=[("ps", 0), "gn"], writes=[("UT", s)], cost=0.9)
        for g in range(6):
            w = 512 if g < 5 else 256

            def fp(e, g=g, w=w, s=s):
                for k in range(8):
                    ins = e.matmul(PS(1 + g)[:, 0:w], UT[s][:, k, :], Win[:, k, g * 512:g * 512 + w], start=(k == 0), stop=(k == 7))
                return ins
            P.add("pe", fp, reads=[("UT", s)] + [("Wd", q) for q in range(4)], writes=[("ps", 1 + g)], cost=(2.0 if g < 5 else 1.1))
        PB = PKBt[s]; PA = PKAt[s]
        pkb_key = ("PKB", s)
        P.add("act", lambda e, PB=PB: e.activation(out=PB[:, 3072:3584], in_=PS(4), func=AF.Copy), reads=[("ps", 4)], writes=[(pkb_key, "rv")])
        P.add("act", lambda e, PB=PB: e.activation(out=PB[:, 3584:4096], in_=PS(5), func=AF.Silu), reads=[("ps", 5)], writes=[(pkb_key, "sg")])
        P.add("act", lambda e, PA=PA: e.activation(out=v3(PA[:, 128:320], 2)[:, :, 0:64], in_=v3(PS(6)[:, 128:256], 2), func=AF.Copy),
              reads=[("ps", 6)], writes=[("PKA", s)])
        rotary("dve", v3(PS(2), 4), TAB[s][:, 0:64], TAB[s][:, 64:128], v3(RQTM, 4), 64, RT[0:4], [("ps", 2), ("TAB", s)], ("RQTM", s), 4)
        rotary("dve", v3(PS(3), 4), TAB[s][:, 0:64], TAB[s][:, 64:128], v3(RKTM, 4), 64, RT[4:8], [("ps", 3), ("TAB", s)], ("RKTM", s), 4)
        rotary("dve", v3(PS(1), 8), TAB[s][:, 128:136], TAB[s][:, 136:144], v3(AQTM, 8), 8, RT[0:4], [("ps", 1), ("TAB", s)], ("AQTM", s), 8)
        P.add("act", lambda e, AQTM=AQTM: e.activation(out=v3(AQTM, 8)[:, :, 16:64], in_=v3(PS(1), 8)[:, :, 16:64], func=AF.Copy),
              reads=[("ps", 1)], writes=[(("AQTM", s), 2)])
        rotary("dve", v3(PS(6)[:, 0:128], 2), TAB[s][:, 128:136], TAB[s][:, 136:144], v3(AKTM, 2), 8, RT[4:8], [("ps", 6), ("TAB", s)], ("AKTM", s), 2)
        P.add("act", lambda e, AKTM=AKTM: e.activation(out=v3(AKTM, 2)[:, :, 16:64], in_=v3(PS(6)[:, 0:128], 2)[:, :, 16:64], func=AF.Copy),
              reads=[("ps", 6)], writes=[(("AKTM", s), 2)])

        def ftq(e, RQTM=RQTM, RKTM=RKTM):
            for h in range(4):
                ins = e.transpose(PSB(7)[:, h * 128:(h + 1) * 128], RQTM[:, h * 128:(h + 1) * 128], identb)
            for h in range(4):
                ins = e.transpose(PSB(7)[:, 512 + h * 128:512 + (h + 1) * 128], RKTM[:, h * 128:(h + 1) * 128], identb)
            return ins
        P.add("pe", ftq, reads=[(("RQTM", s), 0), (("RQTM", s), 1), (("RKTM", s), 0), (("RKTM", s), 1), "identb"], writes=[("ps", 7)])
        P.add("dve", lambda e, PB=PB: e.tensor_tensor(out=PB[:, 512:1024], in0=PSB(7)[:, 0:512], in1=WF, op=ALU.mult),
              reads=[("ps", 7)] + [("WF", h) for h in range(4)], writes=[(pkb_key, "qf")])
        P.add("dve", lambda e, PB=PB: e.tensor_tensor(out=PB[:, 1024:1536], in0=PSB(7)[:, 0:512], in1=WB, op=ALU.mult),
              reads=[("ps", 7)] + [("WB", h) for h in range(4)], writes=[(pkb_key, "qb")])
        P.add("act", lambda e, PB=PB: e.activation(out=PB[:, 1536:2048], in_=PSB(7)[:, 512:1024], func=AF.Copy),
              reads=[("ps", 7)], writes=[(pkb_key, "kT")])

        def fta(e, AQTM=AQTM, AKTM=AKTM):
            for h in range(4):
                ins = e.transpose(PSB(1)[:, h * 128:(h + 1) * 128], AQTM[:, h * 128:(h + 1) * 128], identb)
            ins = e.transpose(PSB(1)[:, 512:640], AKTM, identb)
            return ins
        P.add("pe", fta, reads=[(("AQTM", s), 0), (("AQTM", s), 1), (("AQTM", s), 2), (("AKTM", s), 0), (("AKTM", s), 1), (("AKTM", s), 2), "identb"],
              writes=[("ps", 1)])
        P.add("act", lambda e, PB=PB: e.activation(out=PB[:, 0:512], in_=PSB(1)[:, 0:512], func=AF.Copy), reads=[("ps", 1)],
              writes=[(pkb_key, "aq")])
        P.add("act", lambda e, PA=PA: e.activation(out=PA[:, 0:128], in_=PSB(1)[:, 512:640], func=AF.Copy), reads=[("ps", 1)],
              writes=[("PKA", s)])
        P.add("pool", lambda e, PB=PB, RKTM=RKTM: e.tensor_tensor(out=v3(PB[:, 2048:2560], 4), in0=v3(RKTM, 4),
                                                     in1=wkf.unsqueeze(2).to_broadcast([128, 4, 128]), op=ALU.mult),
              reads=[(("RKTM", s), 0), (("RKTM", s), 1)] + [("wkf", h) for h in range(4)], writes=[(pkb_key, "kf")])
        P.add("pool", lambda e, PB=PB, RKTM=RKTM: e.tensor_tensor(out=v3(PB[:, 2560:3072], 4), in0=v3(RKTM, 4),
                                                     in1=wkb.unsqueeze(2).to_broadcast([128, 4, 128]), op=ALU.mult),
              reads=[(("RKTM", s), 0), (("RKTM", s), 1)] + [("wkb", h) for h in range(4)], writes=[(pkb_key, "kb")])
        if is_prompt:
            def fkv(e, PB=PB):
                for d_ in range(2):
                    for h in range(4):
                        ins = e.matmul(PS(2 + d_)[:, h * 128:(h + 1) * 128], PB[:, 2048 + d_ * 512 + h * 128:2048 + d_ * 512 + (h + 1) * 128],
                                       PB[:, 3072 + h * 128:3072 + (h + 1) * 128], start=True, stop=True)
                return ins
            P.add("pe", fkv, reads=[(pkb_key, "kf"), (pkb_key, "kb"), (pkb_key, "rv")], writes=[("ps", 2), ("ps", 3)], cost=0.9)
            P.add("act", lambda e, n=n: e.activation(out=CDP[:, 0:4], in_=ldb, func=AF.Exp, scale=float(C * n)), reads=["ldb"], writes=["cdp"])
            for h in range(4):
                hs = slice(h * 128, (h + 1) * 128)
                P.add("dve", lambda e, h=h, hs=hs: e.scalar_tensor_tensor(out=TOT[:, hs], in0=TOT[:, hs], scalar=cdf[:, h:h + 1], in1=PS(2)[:, hs],
                                                                     op0=ALU.mult, op1=ALU.add),
                      reads=[("ps", 2), "cdf", ("TOTf", h)], writes=[("TOTf", h)])
                P.add("dve", lambda e, h=h, hs=hs: e.scalar_tensor_tensor(out=TOT[:, 512 + h * 128:512 + (h + 1) * 128], in0=PS(3)[:, hs],
                                                                     scalar=CDP[:, h:h + 1], in1=TOT[:, 512 + h * 128:512 + (h + 1) * 128],
                                                                     op0=ALU.mult, op1=ALU.add),
                      reads=[("ps", 3), "cdp", ("TOTb", h)], writes=[("TOTb", h)])
        allpkb = [(pkb_key, x) for x in ("rv", "sg", "qf", "qb", "kT", "aq", "kf", "kb")]
        P.add("sp", lambda e, n=n, PB=PB: e.dma_start(out=pkB[n], in_=PB), reads=allpkb, writes=[("pkB", n)], dma=("p2", "stB", s))
        P.add("sp", lambda e, n=n, PA=PA: e.dma_start(out=pkA[n], in_=PA), reads=[("PKA", s)], writes=[("pkA", n)], dma=("p2", "stA", s))
        if n == 0:
            P.add("sp", lambda e, PA=PA: e.dma_start(out=xkv_in.ap()[:, 0:320], in_=PA), reads=[("PKA", s)], writes=["xkv_in0"], dma="xkv0")
        if n == NPC - 1:
            P.add("sp", lambda e, PA=PA: e.dma_start(out=xkv_in.ap()[:, 320:640], in_=PA), reads=[("PKA", s)], writes=["xkv_in1"], dma="xkv1")
            P.add("sp", lambda e: e.dma_start(out=xst_in.ap(), in_=TOT), reads=[("TOTf", h) for h in range(4)] + [("TOTb", h) for h in range(4)], writes=["xst_in"], dma="xst")
            PAIRS = [[0, 1], [2, 3], [4, 5], [6, 7]]
            P.add("pool", lambda e: e.collective_compute("AllGather", ALU.bypass, replica_groups=PAIRS,
                                                         ins=[xkv_in.ap().opt()], outs=[xkv_out.ap().opt()]),
                  reads=["xkv_in0", "xkv_in1"], writes=["xkv_out"], dma="cc_kv", inc=1)
            P.add("pool", lambda e: e.collective_compute("AllGather", ALU.bypass, replica_groups=PAIRS,
                                                         ins=[xst_in.ap().opt()], outs=[xst_out.ap().opt()]),
                  reads=["xst_in"], writes=["xst_out"], dma="cc_st", inc=1)

    if stop_after == 2:
        return finish_copy()
    P2_DONE = [("pkB", NCH - 1), ("pkA", NCH - 1), ("pkB", NCH - 2), ("pkA", NCH - 2), "xkv_in0", "xkv_in1", "xst_in"]
    P.extra_reads = list(P2_DONE)
    P.add("pool", lambda e: e.dma_start(out=Wout, in_=wout.rearrange("(k p) f -> p k f", p=128)), reads=P2_DONE,
          writes=[("Wd", q) for q in range(4)], dma="wout")
    o = SbS_off
    SBS_A = (CONST0 - 0)
    NSB_W1 = (ACT0 - SbS_off) // 1024
    o = ACT0
    NSB_ACT = max(0, max(NPC, NSC) - NSB_W1)
    SBS = [A(SbS_off + i * 1024, 512, BF16) for i in range(NSB_W1)] + [A(o + i * 1024, 512, BF16) for i in range(NSB_ACT)]
    o += NSB_ACT * 1024
    AQ3 = [A(o + i * 1024, 512, BF16) for i in range(3)]; o += 3072
    RET3 = [A(o + i * 7168, 3584, BF16) for i in range(2)]; o += 14336
    KVt = [A(o + i * 640, 320, BF16) for i in range(4)]; o += 2560
    HRES = [A(o + i * 4096, D, F32) for i in range(2)]; o += 8192
    PT = [[A(o + (g * 3 + b) * 1024, 512, BF16) for b in range(3)] for g in range(2)]; o += 6144
    AO = A(o, 512, F32); o += 2048
    CAT = [A(o + i * 2048, D, BF16) for i in range(3)]; o += 6144
    CATT = v3(A(o, D, BF16), 8); o += 2048
    ATM = A(o, 512, BF16); o += 1024
    SF = A(o, 512, F32); o += 2048
    SB = A(o, 512, F32); o += 2048
    SFB = [A(o + i * 1024, 512, BF16) for i in range(2)]; o += 2048
    RN = A(o, 512, F32); o += 2048
    KBV = [A(o + i * 2048, 1024, BF16) for i in range(2)]; o += 4096
    JUNK3 = A(o, 512, BF16); o += 1024
    ST3 = A(o, 64, F32); o += 256
    assert o <= CONST0, o

    def mixer_seq(c0, L, exch):
        if exch:
            P.add("sp", lambda e: e.dma_start(out=SB, in_=xst_out.ap()[128:256, 512:1024]), reads=["xst_out"] + P2_DONE, writes=[("SB", h) for h in range(4)], dma="sbinit")
            P.add("dve", lambda e: e.tensor_scalar(out=SB, in0=SB, scalar1=flags[:, 1:2], scalar2=None, op0=ALU.mult),
                  reads=[("SB", h) for h in range(4)] + ["flags"], writes=[("SB", h) for h in range(4)])
        else:
            P.add("dve", lambda e: e.memset(SB, 0.0), reads=P2_DONE, writes=[("SB", h) for h in range(4)])
        for l in range(L - 1, -1, -1):
            n = c0 + l
            s = l % 2
            P.add("act", lambda e, l=l: e.activation(out=SBS[l], in_=SB, func=AF.Copy), reads=[("SB", h) for h in range(4)], writes=[("SBS", l)])
            P.add("sp", lambda e, n=n, s=s: e.dma_start(out=KBV[s], in_=pkB[n][:, 2560:3584]), reads=[("pkB", n)] + P2_DONE,
                  writes=[("KBV", s)], dma=("p3", "kbv", s))

            def fkb(e, s=s):
                for h in range(4):
                    ins = e.matmul(PS(4)[:, h * 128:(h + 1) * 128], KBV[s][:, h * 128:(h + 1) * 128], KBV[s][:, 512 + h * 128:512 + (h + 1) * 128],
                                   start=True, stop=True)
                return ins
            P.add("pe", fkb, reads=[("KBV", s)], writes=[("ps", 4)])
            for h in range(4):
                hs = slice(h * 128, (h + 1) * 128)
                P.add("dve", lambda e, h=h, hs=hs: e.scalar_tensor_tensor(out=SB[:, hs], in0=SB[:, hs], scalar=cdb[:, h:h + 1], in1=PS(4)[:, hs],
                                                                     op0=ALU.mult, op1=ALU.add),
                      reads=[("ps", 4), "cdb", ("SB", h)], writes=[("SB", h)])
        if exch:
            P.add("sp", lambda e: e.dma_start(out=SF, in_=xst_out.ap()[0:128, 0:512]), reads=["xst_out"] + P2_DONE, writes=[("SF", h) for h in range(4)], dma="sfinit")
            P.add("dve", lambda e: e.tensor_scalar(out=SF, in0=SF, scalar1=flags[:, 0:1], scalar2=None, op0=ALU.mult),
                  reads=[("SF", h) for h in range(4)] + ["flags"], writes=[("SF", h) for h in range(4)])
        else:
            P.add("dve", lambda e: e.memset(SF, 0.0), reads=P2_DONE, writes=[("SF", h) for h in range(4)])
        P.add("act", lambda e: e.activation(out=SFB[0], in_=SF, func=AF.Copy), reads=[("SF", h) for h in range(4)], writes=[("SFB", 0)])

        def load_kv(l):
            slot = (l + 1) % 4
            if l == -1:
                src = xkv_out.ap()[0:128, 320:640]; rk = ["xkv_out"]
            elif l == L:
                src = xkv_out.ap()[128:256, 0:320]; rk = ["xkv_out"]
            else:
                src = pkA[c0 + l]; rk = [("pkA", c0 + l)]
            P.add("sp", lambda e, slot=slot, src=src: e.dma_start(out=KVt[slot], in_=src), reads=rk + P2_DONE, writes=[("KV", slot)],
                  dma=("p3", "kv", slot))

        def has_kv(l):
            return (0 <= l < L) or (exch and l in (-1, L))

        def load_attn(l):
            if not (0 <= l < L):
                return
            n = c0 + l
            if has_kv(l + 1):
                load_kv(l + 1)
            P.add("sp", lambda e, n=n, l=l: e.dma_start(out=AQ3[l % 3], in_=pkB[n][:, 0:512]), reads=[("pkB", n)] + P2_DONE,
                  writes=[("AQ3", l % 3)], dma=("p3", "aq", l % 3))

        def load_ret(l):
            if not (0 <= l < L):
                return
            n = c0 + l
            s = l % 2
            P.add("sp", lambda e, n=n, s=s: e.dma_start(out=RET3[s], in_=pkB[n][:, 512:4096]), reads=[("pkB", n)] + P2_DONE,
                  writes=[("RET3", s)], dma=("p3", "ret", s))

        def load_hres(l):
            if not (0 <= l < L):
                return
            n = c0 + l
            s = l % 2
            P.add("sp", lambda e, n=n, s=s: e.dma_start(out=HRES[s], in_=h_s[n * 128:(n + 1) * 128, :]), reads=[("h_s", n)] + P2_DONE,
                  writes=[("HRES", s, 0), ("HRES", s, 1)], dma=("p3", "hres", s))

        def blocks_of(l):
            blocks = []
            if l > 0 or exch:
                blocks.append((l - 1, MprevH if l == 0 else Mprev, "MprevH" if l == 0 else "Mprev"))
            blocks.append((l, None, None))
            if l + 1 < L or exch:
                blocks.append((l + 1, MnextH if l + 1 == L else Mnext, "MnextH" if l + 1 == L else "Mnext"))
            return blocks

        def attn_scores(l, g):
            if not (0 <= l < L):
                return
            AQ = AQ3[l % 3]
            for bi, (bl, M, Mk) in enumerate(blocks_of(l)):
                slot = (bl + 1) % 4
                scb = (g * 3 + bi) % 2
                P.add("pe", lambda e, g=g, slot=slot, scb=scb, AQ=AQ: e.matmul(
                    PS(scb), KVt[slot][g * 64:(g + 1) * 64, 0:128], AQ[g * 64:(g + 1) * 64, 0:512], start=True, stop=True,
                    tile_position=(g * 64, 0)), reads=[("KV", slot), ("AQ3", l % 3)], writes=[("ps", scb)])
                P.add("act", lambda e, g=g, bi=bi, scb=scb: e.activation(out=PT[g][bi], in_=PS(scb), func=AF.Exp, scale=0.125),
                      reads=[("ps", scb)], writes=[("PT", g, bi)])
                if M is not None:
                    P.add("pool", lambda e, g=g, bi=bi, M=M: e.tensor_tensor(out=v3(PT[g][bi], 4), in0=v3(PT[g][bi], 4),
                                                                       in1=M.unsqueeze(1).to_broadcast([128, 4, 128]), op=ALU.mult),
                          reads=[("PT", g, bi), Mk], writes=[("PT", g, bi)])

        def attn_pv(l, g):
            if not (0 <= l < L):
                return
            blocks = blocks_of(l)

            def fpv(e, g=g, blocks=blocks):
                for hh in range(4):
                    for bi, (bl, M, Mk) in enumerate(blocks):
                        slot = (bl + 1) % 4
                        ins = e.matmul(PS(2 + g)[:, hh * 72:hh * 72 + 65], PT[g][bi][:, hh * 128:(hh + 1) * 128],
                                       KVt[slot][:, 128 + g * 96:128 + g * 96 + 65], start=(bi == 0), stop=(bi == len(blocks) - 1))
                return ins
            P.add("pe", fpv, reads=[("PT", g, bi) for bi in range(len(blocks))] + [("KV", (bl + 1) % 4) for bl, _, _ in blocks],
                  writes=[("ps", 2 + g)])
            Og = v3(PS(2 + g)[:, 0:288], 4)
            P.add("dve", lambda e, g=g, Og=Og: e.tensor_tensor(out=ST3[:, g * 4:g * 4 + 4], in0=Og[:, :, 64:65].rearrange("p a b -> p (a b)"),
                                                         in1=esink[:, g * 4:g * 4 + 4], op=ALU.add),
                  reads=[("ps", 2 + g), "esink"], writes=[("den", g)])
            P.add("dve", lambda e, g=g: e.reciprocal(out=ST3[:, 8 + g * 4:8 + g * 4 + 4], in_=ST3[:, g * 4:g * 4 + 4]),
                  reads=[("den", g)], writes=[("rden", g)])
            P.add("dve", lambda e, g=g, Og=Og: e.tensor_tensor(out=v3(AO[:, g * 256:(g + 1) * 256], 4), in0=Og[:, :, 0:64],
                                                         in1=ST3[:, 8 + g * 4:8 + g * 4 + 4].unsqueeze(2).to_broadcast([128, 4, 64]),
                                                         op=ALU.mult),
                  reads=[("ps", 2 + g), ("rden", g)], writes=[("AO", g)])

        def attn_norm(l):
            if not (0 <= l < L):
                return
            c = l % 3
            P.add("act", lambda e: e.activation(out=JUNK3, in_=AO, func=AF.Square, scale=float(512 ** -0.5), accum_out=ST3[:, 16:17]),
                  reads=[("AO", 0), ("AO", 1)], writes=["msa"] + [("junk3", h) for h in range(4)])
            P.add("act", lambda e: e.activation(out=ST3[:, 17:18], in_=ST3[:, 16:17], func=AF.Ln, bias=epsc, scale=1.0), reads=["msa", "epsc"], writes=["lna"])
            P.add("act", lambda e: e.activation(out=ST3[:, 18:19], in_=ST3[:, 17:18], func=AF.Exp, scale=-0.5), reads=["lna"], writes=["rsa"])
            P.add("act", lambda e, c=c: e.activation(out=CAT[c][:, 0:512], in_=AO, func=AF.Copy, scale=ST3[:, 18:19]),
                  reads=["rsa", ("AO", 0), ("AO", 1)], writes=[("CAT", c, 0)])

        def ret_mm(l):
            if not (0 <= l < L):
                return
            s = l % 2
            R3 = RET3[s]

            def fat(e, R3=R3):
                for h in range(4):
                    ins = e.matmul(PS(4)[:, h * 128:(h + 1) * 128], R3[:, 1024 + h * 128:1024 + (h + 1) * 128],
                                   R3[:, h * 128:(h + 1) * 128], start=True, stop=True)
                return ins
            P.add("pe", fat, reads=[("RET3", s)], writes=[("ps", 4)])
            P.add("dve", lambda e: e.tensor_tensor(out=ATM, in0=PS(4), in1=DTq, op=ALU.mult),
                  reads=[("ps", 4)] + [("DTq", h) for h in range(4)], writes=["ATM"])

            def frr(e, R3=R3, l=l, s=s):
                for h in range(4):
                    hs = slice(h * 128, (h + 1) * 128)
                    e.matmul(PS(5)[:, hs], ATM[:, hs], R3[:, 2560 + h * 128:2560 + (h + 1) * 128], start=True, stop=False)
                    e.matmul(PS(5)[:, hs], R3[:, h * 128:(h + 1) * 128], SFB[s][:, hs], start=False, stop=False)
                    ins = e.matmul(PS(5)[:, hs], R3[:, 512 + h * 128:512 + (h + 1) * 128], SBS[l][:, hs], start=False, stop=True)
                return ins
            P.add("pe", frr, reads=["ATM", ("RET3", s), ("SFB", s), ("SBS", l)], writes=[("ps", 5)], cost=1.3)

        def ret_stats(l):
            if not (0 <= l < L):
                return
            P.add("dve", lambda e: e.tensor_reduce(out=ST3[:, 20:24], in_=v3(PS(5), 4), axis=AX.X, op=ALU.add), reads=[("ps", 5)], writes=["s1"])
            P.add("dve", lambda e: e.tensor_scalar(out=ST3[:, 28:32], in0=ST3[:, 20:24], scalar1=1.0 / 128.0, scalar2=None, op0=ALU.mult),
                  reads=["s1"], writes=["mean"])
            for h in range(4):
                hs = slice(h * 128, (h + 1) * 128)
                P.add("dve", lambda e, h=h, hs=hs: e.tensor_scalar(out=RN[:, hs], in0=PS(5)[:, hs], scalar1=ST3[:, 28 + h:29 + h],
                                                              scalar2=None, op0=ALU.subtract),
                      reads=[("ps", 5), "mean"], writes=[("RN", h)])
                P.add("act", lambda e, h=h, hs=hs: e.activation(out=JUNK3[:, hs], in_=RN[:, hs], func=AF.Square,
                                                           scale=float(128 ** -0.5), accum_out=ST3[:, 36 + h:37 + h]),
                      reads=[("RN", h)], writes=[("var", h), ("junk3", h)])
            P.add("act", lambda e: e.activation(out=ST3[:, 40:44], in_=ST3[:, 36:40], func=AF.Ln, bias=epsc, scale=1.0),
                  reads=[("var", h) for h in range(4)] + ["epsc"], writes=["lnr"])
            P.add("act", lambda e: e.activation(out=ST3[:, 44:48], in_=ST3[:, 40:44], func=AF.Exp, scale=-0.5), reads=["lnr"], writes=["rsr"])

        def ret_fin(l):
            if not (0 <= l < L):
                return
            s = l % 2
            c = l % 3
            R3 = RET3[s]
            for h in range(4):
                hs = slice(h * 128, (h + 1) * 128)
                P.add("dve", lambda e, h=h, hs=hs, c=c, R3=R3: e.scalar_tensor_tensor(
                    out=CAT[c][:, 512 + h * 128:512 + (h + 1) * 128], in0=RN[:, hs], scalar=ST3[:, 44 + h:45 + h],
                    in1=R3[:, 3072 + h * 128:3072 + (h + 1) * 128], op0=ALU.mult, op1=ALU.mult),
                      reads=[("RN", h), "rsr", ("RET3", s)], writes=[("CAT", c, 1)])

        def state_upd(l):
            if not (0 <= l < L) or l + 1 >= L:
                return
            s = l % 2
            R3 = RET3[s]

            def fkf(e, R3=R3):
                for h in range(4):
                    ins = e.matmul(PS(4)[:, h * 128:(h + 1) * 128], R3[:, 1536 + h * 128:1536 + (h + 1) * 128],
                                   R3[:, 2560 + h * 128:2560 + (h + 1) * 128], start=True, stop=True)
                return ins
            P.add("pe", fkf, reads=[("RET3", s)], writes=[("ps", 4)])
            for h in range(4):
                hs = slice(h * 128, (h + 1) * 128)
                P.add("dve", lambda e, h=h, hs=hs: e.scalar_tensor_tensor(out=SF[:, hs], in0=SF[:, hs], scalar=cdf[:, h:h + 1], in1=PS(4)[:, hs],
                                                                     op0=ALU.mult, op1=ALU.add),
                      reads=[("ps", 4), "cdf", ("SF", h)], writes=[("SF", h)])
            P.add("act", lambda e, s=s: e.activation(out=SFB[1 - s], in_=SF, func=AF.Copy), reads=[("SF", h) for h in range(4)], writes=[("SFB", 1 - s)])

        def outp(l):
            if not (0 <= l < L):
                return
            n = c0 + l
            s = l % 2
            c = l % 3

            def ftc(e, c=c):
                for f in range(8):
                    ins = e.transpose(PSB(6)[:, f * 128:(f + 1) * 128], CAT[c][:, f * 128:(f + 1) * 128], identb)
                return ins
            P.add("pe", ftc, reads=[("CAT", c, 0), ("CAT", c, 1), "identb"], writes=[("ps", 6)], cost=0.9)
            P.add("dve", lambda e: e.tensor_tensor(out=CATT[:, 0:4, :], in0=v3(PSB(6), 8)[:, 0:4, :],
                                                in1=gn[:, 24:28].unsqueeze(2).to_broadcast([128, 4, 128]), op=ALU.mult),
                  reads=[("ps", 6), "gn"], writes=[("CATT", 0)])
            P.add("act", lambda e: e.activation(out=CATT[:, 4:8, :], in_=v3(PSB(6), 8)[:, 4:8, :], func=AF.Copy), reads=[("ps", 6)], writes=[("CATT", 1)])
            for half in range(2):
                def fo(e, half=half):
                    for f in range(8):
                        ins = e.matmul(PS(7), CATT[:, f, :], Wout[:, f, half * 512:(half + 1) * 512], start=(f == 0), stop=(f == 7))
                    return ins
                P.add("pe", fo, reads=[("CATT", 0), ("CATT", 1)] + [("Wd", q) for q in range(4)], writes=[("ps", 7)], cost=2.0)
                P.add("dve", lambda e, s=s, half=half: e.tensor_tensor(out=HRES[s][:, half * 512:(half + 1) * 512], in0=PS(7),
                                                                  in1=HRES[s][:, half * 512:(half + 1) * 512], op=ALU.add),
                      reads=[("ps", 7), ("HRES", s, half)], writes=[("HRES", s, half)])
            P.add("pool", lambda e, n=n, s=s: e.dma_start(out=h_s[n * 128:(n + 1) * 128, :], in_=HRES[s]),
                  reads=[("HRES", s, 0), ("HRES", s, 1)], writes=[("h_s", n)], dma=("p3", "hst", s))

        if exch:
            load_kv(-1)
        load_kv(0)
        load_attn(0)
        load_ret(0)
        load_hres(0)
        load_attn(1)
        attn_scores(0, 0); attn_pv(0, 0); attn_scores(0, 1); attn_pv(0, 1); attn_norm(0)
        load_attn(2)
        for i in range(0, L + 1):
            load_ret(i + 1)
            attn_scores(i + 1, 0)
            ret_mm(i)
            attn_pv(i + 1, 0)
            attn_scores(i + 1, 1)
            ret_stats(i)
            attn_pv(i + 1, 1)
            state_upd(i)
            ret_fin(i)
            outp(i - 1)
            attn_norm(i + 1)
            load_attn(i + 3)
            load_hres(i + 1)

    mixer_seq(NPC, NSC, False)
    mixer_seq(0, NPC, True)

    if stop_after == 3:
        return finish_copy()
    P.sched = False
    P.extra_reads = []
    ffn_phase("p4", h_s, yout, "h_s", "yout", 16, True, None, wd2, [("h_s", NPC - 1), ("h_s", NPC - 2), ("h_s", NCH - 1), ("h_s", NCH - 2)])

    P.emit()
    return nc


def _w_in_perm():
    aq = []
    for hp in range(4):
        for h in (hp, hp + 4):
            aq += list(range(h * 64, (h + 1) * 64))
    o1 = 512; o2 = 640; o3 = 768; o4 = 1280; o5 = 1792; o6 = 2304
    return np.array(aq + list(range(o3, o4)) + list(range(o4, o5)) + list(range(o5, o6)) + list(range(o6, 2816))
                    + list(range(o1, o2)) + list(range(o2, o3)), dtype=np.int64)


def _consts():
    p = np.arange(128, dtype=np.float32)
    j = p[:, None]; i = p[None, :]
    cst = np.zeros((128, 772), np.float32)
    cst[:, 0:128] = np.eye(128, dtype=np.float32)
    cst[:, 128:256] = (j <= i)
    cst[:, 256:384] = (j >= i)
    cst[:, 384:512] = j - i
    cst[:, 512:640] = np.broadcast_to(i + 1.0, (128, 128))
    cst[:, 640:768] = np.broadcast_to(128.0 - i, (128, 128))
    cst[:, 768] = p + 1.0
    cst[:, 769] = 127.0 - p
    cst[:, 770] = p
    return cst


def _tables(pos):
    pos = pos.astype(np.float32)
    inv_a = (np.float32(500000.0) ** (-np.arange(0, 16, 2, dtype=np.float32) / np.float32(16))).astype(np.float32)
    inv_r = (np.float32(10000.0) ** (-np.linspace(0.0, 1.0, 64, dtype=np.float32))).astype(np.float32)
    ang_r = (pos[:, None] * inv_r[None, :]).astype(np.float32)
    ang_a = (pos[:, None] * inv_a[None, :]).astype(np.float32)
    return np.concatenate([np.cos(ang_r), np.sin(ang_r), np.cos(ang_a), np.sin(ang_a)], axis=1).astype(np.float32)


_NC_CACHE = {}


def run_cores(x_prompt, x_sample, W, NPC, NSC, stop_after=9):
    key = (NPC, NSC, stop_after)
    if key not in _NC_CACHE:
        _NC_CACHE[key] = build(NPC, NSC, stop_after)
    nc = _NC_CACHE[key]
    perm = _w_in_perm()
    f = lambda a: np.ascontiguousarray(np.asarray(a, dtype=np.float32))
    gains = np.zeros((128, 36), np.float32)
    gains[:, 0:8] = f(W["ffn1_norm"])[0].reshape(8, 128).T
    gains[:, 8:16] = f(W["mix_norm"])[0].reshape(8, 128).T
    gains[:, 16:24] = f(W["ffn2_norm"])[0].reshape(8, 128).T
    gains[:, 24:28] = f(W["attn_out_norm"])[0].reshape(4, 128).T
    common = dict(
        wg1=f(W["ffn1_w_gate"])[0], wu1=f(W["ffn1_w_up"])[0], wd1=f(W["ffn1_w_down"])[0],
        wg2=f(W["ffn2_w_gate"])[0], wu2=f(W["ffn2_w_up"])[0], wd2=f(W["ffn2_w_down"])[0],
        win=np.ascontiguousarray(f(W["w_in"])[0][:, perm]), wout=f(W["w_out"])[0],
        gains=gains, gfin=f(W["final_norm"]), sink=f(W["attn_sink"])[0],
        ldf=f(W["ret_log_decay_fwd"])[0], ldb=f(W["ret_log_decay_bwd"])[0], cst=_consts(),
    )
    HL = NPC * 128
    in_maps = []
    for i in range(8):
        p, half = i // 2, i % 2
        xin = np.concatenate([x_prompt[p, half * HL:(half + 1) * HL], x_sample[i]], axis=0)
        pos = np.concatenate([np.arange(half * HL, (half + 1) * HL), np.arange(NSC * 128)])
        tab = _tables(pos).reshape(NPC + NSC, 128, 144)
        flg = np.zeros((128, 2), np.float32)
        flg[:, 0] = 1.0 if half == 1 else 0.0
        flg[:, 1] = 1.0 if half == 0 else 0.0
        m = dict(common)
        m.update(xin=np.ascontiguousarray(xin, dtype=np.float32), tab=np.ascontiguousarray(tab), flg=flg)
        in_maps.append(m)
    res = run_bass_kernel_spmd(nc, in_maps, core_ids=list(range(8)))
    yp = np.zeros_like(np.asarray(x_prompt, dtype=np.float32))
    ys = np.zeros_like(np.asarray(x_sample, dtype=np.float32))
    for i in range(8):
        y = np.asarray(res.results[i]["yout"])
        p, half = i // 2, i % 2
        yp[p, half * HL:(half + 1) * HL] = y[:HL]
        ys[i] = y[HL:]
    return yp, ys


def kernel(x_prompt, x_sample, **W):
    x_prompt = np.asarray(x_prompt, dtype=np.float32)
    x_sample = np.asarray(x_sample, dtype=np.float32)
    NPC = x_prompt.shape[1] // 256
    NSC = x_sample.shape[1] // 128
    yp, ys = run_cores(x_prompt, x_sample, W, NPC, NSC)
    return (yp, ys)
```

```python
import contextlib
import numpy as np
import ml_dtypes
import concourse.bass as bass
import concourse.mybir as mybir
from concourse.bass_utils import run_bass_kernel_spmd

F32 = mybir.dt.float32
BF16 = mybir.dt.bfloat16
AF = mybir.ActivationFunctionType
ALU = mybir.AluOpType
AX = mybir.AxisListType

D = 1024
DFF = 2816
NFF = 22
C = 128
EPS = 1e-6
QS = [(0, 6), (6, 12), (12, 17), (17, 22)]


def _q_of(j):
    for qi, (a, b) in enumerate(QS):
        if a <= j < b:
            return qi


class Prog:
    EPOCH = 3000

    DEFCOST = {"pe": 0.6, "act": 0.6, "dve": 0.5, "pool": 1.15, "sp": 0.15}

    def __init__(self, nc):
        self.nc = nc
        self.ops = []
        self.sched = False
        self.extra_reads = []

    def add(self, eng, fn, reads=(), writes=(), dma=None, inc=16, cost=None):
        if cost is None:
            cost = 0.15 if dma is not None else self.DEFCOST[eng]
        tset = None
        if eng == "act" and self.sched and dma is None:
            import inspect
            try:
                src = inspect.getsource(fn)
            except Exception:
                src = ""
            if "AF.Silu" in src:
                tset = "silu"
            elif "AF.Ln" in src or "AF.Exp" in src:
                tset = "exp"
        self.ops.append(dict(eng=eng, fn=fn, reads=tuple(reads) + tuple(self.extra_reads), writes=tuple(writes), dma=dma, inc=inc,
                             cost=cost, sched=self.sched, tset=tset))

    def list_schedule(self, a, b):
        import heapq
        ops = self.ops
        n = b - a
        last_w = {}
        readers = {}
        preds = [set() for _ in range(n)]
        for i in range(a, b):
            op = ops[i]
            ps = preds[i - a]
            for r in op["reads"]:
                if r in last_w:
                    ps.add(last_w[r])
                if isinstance(r, tuple) and r[0] == "ps":
                    for x in readers.get(r, ()):
                        if ops[x]["eng"] != op["eng"]:
                            ps.add(x)
            for w in op["writes"]:
                if w in last_w:
                    ps.add(last_w[w])
                for x in readers.get(w, ()):
                    ps.add(x)
            ps.discard(i)
            for r in op["reads"]:
                readers.setdefault(r, []).append(i)
            for w in op["writes"]:
                last_w[w] = i
                readers[w] = []
        succs = [[] for _ in range(n)]
        indeg = [0] * n
        for i in range(n):
            for p in preds[i]:
                succs[p - a].append(i)
            indeg[i] = len(preds[i])
        LAT = 0.35
        DMA_LAT = 2.5
        finish = [0.0] * n
        tready = [0.0] * n
        heaps = {}
        eng_free = {}
        for i in range(n):
            if indeg[i] == 0:
                heapq.heappush(heaps.setdefault(ops[a + i]["eng"], []), (0.0, i))
        order = []
        act_set = [None]
        while len(order) < n:
            best = None
            for eng, hp in heaps.items():
                if not hp:
                    continue
                tr, i = hp[0]
                st = max(tr, eng_free.get(eng, 0.0))
                if best is None or (st, i) < best[:2]:
                    best = (st, i, eng)
            st, i, eng = best
            heapq.heappop(heaps[eng])
            op = ops[a + i]
            extra = 0.0
            if eng == "act":
                ts = op.get("tset")
                if ts is not None and ts != act_set[0]:
                    hp = heaps[eng]
                    popped = []
                    alt = None
                    while hp and len(popped) < 8:
                        tr2, i2 = heapq.heappop(hp)
                        if max(tr2, eng_free.get(eng, 0.0)) <= st + 0.3 and ops[a + i2].get("tset") in (None, act_set[0]):
                            alt = (tr2, i2)
                            break
                        popped.append((tr2, i2))
                    for it in popped:
                        heapq.heappush(hp, it)
                    if alt is not None:
                        heapq.heappush(hp, (tready[i], i))
                        i = alt[1]
                        op = ops[a + i]
                        st = max(alt[0], eng_free.get(eng, 0.0))
                        ts = op.get("tset")
                if ts is not None and ts != act_set[0]:
                    extra = 1.3
                    act_set[0] = ts
            eng_free[eng] = st + op["cost"] + extra
            finish[i] = st + op["cost"] + extra + (DMA_LAT if op["dma"] is not None else 0.0)
            order.append(a + i)
            for j in succs[i]:
                tready[j] = max(tready[j], finish[i] + LAT)
                indeg[j] -= 1
                if indeg[j] == 0:
                    heapq.heappush(heaps.setdefault(ops[a + j]["eng"], []), (tready[j], j))
        self.ops[a:b] = [ops[k] for k in order]
        print("list_schedule", a, b, "est_us", round(max(finish), 1))

    def emit(self):
        import os
        nc = self.nc
        tr = int(os.environ.get("K_TRUNC", "0"))
        if tr:
            self.ops = self.ops[:tr]
        if os.environ.get("K_NOSCHED", "0") != "1":
            i = 0
            while i < len(self.ops):
                if self.ops[i]["sched"]:
                    j = i
                    while j < len(self.ops) and self.ops[j]["sched"]:
                        j += 1
                    self.list_schedule(i, j)
                    i = j
                else:
                    i += 1
        ops = self.ops
        print("n_ops", len(ops))
        last_w = {}
        readers = {}
        for i, op in enumerate(ops):
            raw = set()
            other = set()
            for r in op["reads"]:
                if r in last_w:
                    raw.add(last_w[r])
                if isinstance(r, tuple) and r[0] == "ps":
                    for x in readers.get(r, ()):
                        if ops[x]["eng"] != op["eng"]:
                            other.add(x)
            for w in op["writes"]:
                if w in last_w:
                    other.add(last_w[w])
                for x in readers.get(w, ()):
                    other.add(x)
            deps = set()
            for d in raw | other:
                if d == i:
                    continue
                od = ops[d]
                if od["dma"] is None and od["eng"] == op["eng"]:
                    if op["eng"] == "pe":
                        continue
                    if d not in raw:
                        continue
                deps.add(d)
            op["deps"] = deps
            for r in op["reads"]:
                readers.setdefault(r, []).append(i)
            for w in op["writes"]:
                last_w[w] = i
                readers[w] = []
        need = [False] * len(ops)
        for op in ops:
            for d in op["deps"]:
                need[d] = True
        cnt = {}
        dcnt = {}
        semnames = set()
        for i, op in enumerate(ops):
            if op["dma"] is not None:
                k = op["dma"]
                dcnt[k] = dcnt.get(k, 0) + 1
                op["sig"] = (("dma", k), dcnt[k] * op["inc"])
                semnames.add(("dma", k))
            elif need[i]:
                c = cnt.get(op["eng"], 0)
                cnt[op["eng"]] = c + 1
                sn = ("eng", op["eng"], c // self.EPOCH)
                op["sig"] = (sn, c % self.EPOCH + 1)
                semnames.add(sn)
            else:
                op["sig"] = None
        semnames = sorted(semnames, key=str)
        with contextlib.ExitStack() as st:
            sems = {}
            for i, sn in enumerate(semnames):
                sems[sn] = st.enter_context(nc.semaphore("s%d" % i))
            block = st.enter_context(nc.Block())
            final_dma = {}
            for op in ops:
                if op["dma"] is not None:
                    final_dma[op["sig"][0]] = max(final_dma.get(op["sig"][0], 0), op["sig"][1])

            def run_engine(engname, e):
                waited = {}
                for op in ops:
                    if op["eng"] != engname:
                        continue
                    want = {}
                    for d in op["deps"]:
                        sn, val = ops[d]["sig"]
                        if want.get(sn, 0) < val:
                            want[sn] = val
                    for sn, val in want.items():
                        if waited.get(sn, 0) >= val:
                            continue
                        e.wait_ge(sems[sn], val)
                        waited[sn] = val
                    ins = op["fn"](e)
                    if op["sig"] is not None:
                        sn, val = op["sig"]
                        if op["dma"] is not None:
                            if op["inc"] == 16:
                                ins.then_inc(sems[sn], 16)
                            else:
                                ins.then_inc(sems[sn])
                        else:
                            ins.then_inc(sems[sn], 1)
                if engname == "sp":
                    for sn, val in final_dma.items():
                        if waited.get(sn, 0) < val:
                            e.wait_ge(sems[sn], val)

            @block.tensor
            def _(e):
                run_engine("pe", e)

            @block.scalar
            def _(e):
                run_engine("act", e)

            @block.vector
            def _(e):
                run_engine("dve", e)

            @block.gpsimd
            def _(e):
                run_engine("pool", e)

            @block.sync
            def _(e):
                run_engine("sp", e)


def build(NPC, NSC, stop_after=9):
    NCH = NPC + NSC
    NTOK = NCH * C
    assert NCH % 4 == 0
    nc = bass.Bass("TRN2", target_bir_lowering=False)
    P = Prog(nc)

    def din(name, shape, dt=F32):
        return nc.dram_tensor(name, shape, dt, kind="ExternalInput").ap()

    xin = din("xin", [NTOK, D])
    wg1 = din("wg1", [D, DFF]); wu1 = din("wu1", [D, DFF]); wd1 = din("wd1", [DFF, D])
    wg2 = din("wg2", [D, DFF]); wu2 = din("wu2", [D, DFF]); wd2 = din("wd2", [DFF, D])
    win = din("win", [D, DFF]); wout = din("wout", [D, D])
    gains = din("gains", [128, 36])
    gfin = din("gfin", [D])
    sink = din("sink", [8]); ldf_d = din("ldf", [4]); ldb_d = din("ldb", [4])
    cst = din("cst", [128, 772])
    flg = din("flg", [128, 2])
    tab = din("tab", [NCH, 128, 144])
    yout = nc.dram_tensor("yout", [NTOK, D], F32, kind="ExternalOutput").ap()
    h_s = nc.dram_tensor("h_s", [NTOK, D], F32).ap()
    pkA = nc.dram_tensor("pkA", [NCH, 128, 320], BF16).ap()
    pkB = nc.dram_tensor("pkB", [NCH, 128, 4096], BF16).ap()
    xkv_in = nc.dram_tensor("xkv_in", [128, 640], BF16)
    xkv_out = nc.dram_tensor("xkv_out", [256, 640], BF16)
    xst_in = nc.dram_tensor("xst_in", [128, 1024], F32)
    xst_out = nc.dram_tensor("xst_out", [256, 1024], F32)

    ARENA_BYTES = 207360
    arena = nc.alloc_sbuf_tensor("arena", [128, ARENA_BYTES // 2], BF16)

    def A(off, n, dt):
        assert off % 64 == 0
        bpe = 4 if dt == F32 else 2
        assert off + n * bpe <= ARENA_BYTES, (off, n)
        v = arena[:, off // 2: off // 2 + n * bpe // 2]
        if dt == F32:
            v = v.bitcast(F32)
        return v

    def v3(ap, a):
        return ap.rearrange("p (a b) -> p a b", a=a)

    banks = [nc.alloc_psum_tensor("psb%d" % i, [128, 512], F32) for i in range(8)]

    def PS(i):
        return banks[i][:]

    def PSB(i):
        return PS(i).bitcast(BF16)

    W0 = 0
    W1 = 90112
    ACT0 = 135168
    CONST0 = 199424
    Wg = v3(A(W0, 8 * DFF, BF16), 8)
    Wu = v3(A(W0 + 45056, 8 * DFF, BF16), 8)
    Wd = v3(A(W1, NFF * D, BF16), NFF)
    Win = v3(A(W1, 8 * DFF, BF16), 8)
    Wout = v3(A(W1, 8 * D, BF16), 8)
    SbS_off = W1 + 16384
    o = CONST0
    identb = A(o, 128, BF16); o += 256
    Mprev = A(o, 128, BF16); o += 256
    Mnext = A(o, 128, BF16); o += 256
    MprevH = A(o, 128, BF16); o += 256
    MnextH = A(o, 128, BF16); o += 256
    DTq = A(o, 512, F32); o += 2048
    GF_OFF = o
    WF = A(o, 512, F32); o += 2048
    WB = A(o, 512, F32); o += 2048
    gn = A(o, 36, F32); o += 192
    small = A(o, 64, F32); o += 256
    ldf = small[:, 0:4]; ldb = small[:, 4:8]; nldf = small[:, 8:12]; cdf = small[:, 12:16]; cdb = small[:, 16:20]
    wkf = small[:, 20:24]; wkb = small[:, 24:28]; esink = small[:, 28:36]; flags = small[:, 36:38]
    epsc = small[:, 38:39]; e1c = small[:, 39:43]
    assert o <= ARENA_BYTES
    CSTF = A(ACT0, 772, F32)
    SETUP_TMP = A(ACT0 + 4096, 512, F32)

    P.add("sp", lambda e: e.dma_start(out=CSTF, in_=cst), writes=["cstf"], dma="cst")
    P.add("sp", lambda e: e.dma_start(out=gn, in_=gains), writes=["gn"], dma="gn")
    P.add("sp", lambda e: e.dma_start(out=ldf, in_=ldf_d.partition_broadcast(128)), writes=["ldf"], dma="ldf")
    P.add("sp", lambda e: e.dma_start(out=ldb, in_=ldb_d.partition_broadcast(128)), writes=["ldb"], dma="ldb")
    P.add("sp", lambda e: e.dma_start(out=esink, in_=sink.partition_broadcast(128)), writes=["esink"], dma="sink")
    P.add("sp", lambda e: e.dma_start(out=flags, in_=flg), writes=["flags"], dma="flg")
    c_ident = CSTF[:, 0:128]; c_L = CSTF[:, 128:256]; c_U = CSTF[:, 256:384]; c_JI = CSTF[:, 384:512]
    c_I1 = CSTF[:, 512:640]; c_CI = CSTF[:, 640:768]; c_P1 = CSTF[:, 768:769]; c_PC = CSTF[:, 769:770]; c_P0 = CSTF[:, 770:771]
    SCALE_K = float(128 ** -0.5)
    P.add("dve", lambda e: e.tensor_copy(out=identb, in_=c_ident), reads=["cstf"], writes=["identb"])
    P.add("dve", lambda e: e.tensor_copy(out=Mprev, in_=c_U), reads=["cstf"], writes=["Mprev"])
    P.add("dve", lambda e: e.tensor_copy(out=Mnext, in_=c_L), reads=["cstf"], writes=["Mnext"])
    P.add("dve", lambda e: e.tensor_scalar(out=MprevH, in0=c_U, scalar1=flags[:, 0:1], scalar2=None, op0=ALU.mult),
          reads=["cstf", "flags"], writes=["MprevH"])
    P.add("dve", lambda e: e.tensor_scalar(out=MnextH, in0=c_L, scalar1=flags[:, 1:2], scalar2=None, op0=ALU.mult),
          reads=["cstf", "flags"], writes=["MnextH"])
    P.add("dve", lambda e: e.memset(epsc, EPS), writes=["epsc"])
    P.add("dve", lambda e: e.tensor_scalar(out=nldf, in0=ldf, scalar1=-1.0, scalar2=None, op0=ALU.mult),
          reads=["ldf"], writes=["nldf"])
    P.add("act", lambda e: e.activation(out=esink, in_=esink, func=AF.Exp), reads=["esink"], writes=["esink"])
    P.add("act", lambda e: e.activation(out=cdf, in_=ldf, func=AF.Exp, scale=float(C)), reads=["ldf"], writes=["cdf"])
    P.add("act", lambda e: e.activation(out=cdb, in_=ldb, func=AF.Exp, scale=float(C)), reads=["ldb"], writes=["cdb"])
    for h in range(4):
        hs = slice(h * 128, (h + 1) * 128)
        P.add("act", lambda e, h=h, hs=hs: e.activation(out=WF[:, hs], in_=c_I1, func=AF.Exp, scale=ldf[:, h:h + 1]),
              reads=["cstf", "ldf"], writes=[("WF", h)])
        P.add("act", lambda e, h=h, hs=hs: e.activation(out=WB[:, hs], in_=c_CI, func=AF.Exp, scale=ldb[:, h:h + 1]),
              reads=["cstf", "ldb"], writes=[("WB", h)])
        P.add("act", lambda e, h=h: e.activation(out=wkf[:, h:h + 1], in_=c_PC, func=AF.Exp, scale=ldf[:, h:h + 1]),
              reads=["cstf", "ldf"], writes=[("wkf", h)])
        P.add("act", lambda e, h=h: e.activation(out=wkb[:, h:h + 1], in_=c_P0, func=AF.Exp, scale=ldb[:, h:h + 1]),
              reads=["cstf", "ldb"], writes=[("wkb", h)])
        P.add("act", lambda e, h=h: e.activation(out=e1c[:, h:h + 1], in_=c_P1, func=AF.Exp, scale=nldf[:, h:h + 1]),
              reads=["cstf", "nldf"], writes=[("e1c", h)])
        tmp = SETUP_TMP[:, 0:128]; tmp2 = SETUP_TMP[:, 128:256]; tmp3 = SETUP_TMP[:, 256:384]
        P.add("dve", lambda e, h=h, tmp=tmp: e.tensor_scalar(out=tmp, in0=c_JI, scalar1=ldb[:, h:h + 1], scalar2=None, op0=ALU.mult),
              reads=["cstf", "ldb"], writes=["stmp"])
        P.add("dve", lambda e, h=h, tmp=tmp, tmp2=tmp2: e.scalar_tensor_tensor(out=tmp2, in0=c_I1, scalar=nldf[:, h:h + 1], in1=tmp,
                                                                        op0=ALU.mult, op1=ALU.add),
              reads=["cstf", "nldf", "stmp"], writes=["stmp2"])
        P.add("act", lambda e, tmp2=tmp2, tmp3=tmp3: e.activation(out=tmp3, in_=tmp2, func=AF.Exp), reads=["stmp2"], writes=["stmp3"])
        P.add("dve", lambda e, tmp3=tmp3: e.tensor_tensor(out=tmp3, in0=tmp3, in1=c_U, op=ALU.mult), reads=["stmp3", "cstf"], writes=["stmp3"])
        P.add("dve", lambda e, h=h, tmp3=tmp3, hs=hs: e.scalar_tensor_tensor(out=DTq[:, hs], in0=c_L, scalar=e1c[:, h:h + 1], in1=tmp3,
                                                                          op0=ALU.mult, op1=ALU.add),
              reads=["cstf", ("e1c", h), "stmp3"], writes=[("DTq", h)])
        P.add("dve", lambda e, hs=hs: e.tensor_scalar(out=DTq[:, hs], in0=DTq[:, hs], scalar1=SCALE_K, scalar2=None, op0=ALU.mult),
              reads=[("DTq", h)], writes=[("DTq", h)])
        P.add("dve", lambda e, h=h: e.tensor_scalar(out=wkf[:, h:h + 1], in0=wkf[:, h:h + 1], scalar1=SCALE_K, scalar2=None, op0=ALU.mult),
              reads=[("wkf", h)], writes=[("wkf", h)])
        P.add("dve", lambda e, h=h: e.tensor_scalar(out=wkb[:, h:h + 1], in0=wkb[:, h:h + 1], scalar1=SCALE_K, scalar2=None, op0=ALU.mult),
              reads=[("wkb", h)], writes=[("wkb", h)])
    CONST_KEYS = ["identb", "Mprev", "Mnext", "MprevH", "MnextH", "epsc", "cdf", "cdb", "esink", "gn"] + \
        [(nm, h) for nm in ("WF", "WB", "wkf", "wkb", "DTq") for h in range(4)]
    P.add("dve", lambda e: e.memset(SETUP_TMP[:, 384:385], 0.0), reads=CONST_KEYS + ["cstf", "stmp3"], writes=["setup_done"])

    def load_w_kf(tag, dst, src, key):
        sv = src.rearrange("(k p) f -> p k f", p=128)
        for qi, (a, b) in enumerate(QS):
            P.add("pool", lambda e, a=a, b=b: e.dma_start(out=dst[:, :, a * 128:b * 128], in_=sv[:, :, a * 128:b * 128]),
                  reads=["setup_done"], writes=[(key, qi)], dma=(tag, key, qi))

    def load_w_d(tag, dst, src, key, after=()):
        sv = src.rearrange("(j p) f -> p j f", p=128)
        for qi, (a, b) in enumerate(QS):
            P.add("pool", lambda e, a=a, b=b: e.dma_start(out=dst[:, a:b, :], in_=sv[:, a:b, :]),
                  reads=["setup_done"] + list(after), writes=[(key, qi)], dma=(tag, key, qi))

    def ffn_phase(tag, src, dst, srckey, dstkey, gcol, final, load_gu, wdsrc, after):
        o = ACT0
        X = [A(o + i * 4096, D, F32) for i in range(2)]; o += 8192
        XN = [A(o + i * 2048, D, BF16) for i in range(4)]; o += 8192
        XT = v3(A(o, 8 * 512, BF16), 8); o += 8192
        HT = v3(A(o, NFF * 512, BF16), NFF); o += NFF * 1024
        SG = [A(o + i * 2048, 512, F32) for i in range(2)]; o += 4096
        XR = [A(o + i * 4096, D, F32) for i in range(3)]; o += 12288
        JUNKF = A(o - 12288 - 4096, D, BF16)
        MS = A(o, 32, F32); o += 128
        GF = None
        if final:
            GF = A(GF_OFF, D, F32)
        assert o <= CONST0, o
        if load_gu:
            load_gu()
        load_w_d(tag, Wd, wdsrc, "Wd", after)
        if final:
            P.add("sp", lambda e: e.dma_start(out=GF, in_=gfin.partition_broadcast(128)), reads=["setup_done"] + list(after),
                  writes=["GF"], dma="gf")
        NB = NCH // 4

        def norm(b):
            for t in range(4):
                n = 4 * b + t
                xs = n % 2
                P.add("sp", lambda e, n=n, xs=xs: e.dma_start(out=X[xs], in_=src[n * 128:(n + 1) * 128, :]),
                      reads=[(srckey, n), "setup_done"] + list(after), writes=[("X", xs)], dma=(tag, "x", xs))
                msn = (tag, "ms", n % 4)
                P.add("act", lambda e, xs=xs, n=n, t=t: e.activation(out=XN[t], in_=X[xs], func=AF.Square, scale=1.0 / 32.0,
                                                            accum_out=MS[:, (n % 4) * 4:(n % 4) * 4 + 1]),
                      reads=[("X", xs)], writes=[msn, ("XN", t)])
                P.add("act", lambda e, n=n: e.activation(out=MS[:, (n % 4) * 4 + 1:(n % 4) * 4 + 2], in_=MS[:, (n % 4) * 4:(n % 4) * 4 + 1],
                                                    func=AF.Ln, bias=epsc, scale=1.0),
                      reads=[msn, "epsc"], writes=[(tag, "ln", n % 4)])
                P.add("act", lambda e, n=n: e.activation(out=MS[:, (n % 4) * 4 + 2:(n % 4) * 4 + 3], in_=MS[:, (n % 4) * 4 + 1:(n % 4) * 4 + 2],
                                                    func=AF.Exp, scale=-0.5),
                      reads=[(tag, "ln", n % 4)], writes=[(tag, "rstd", n % 4)])
                P.add("dve", lambda e, xs=xs, t=t, n=n: e.tensor_scalar(out=XN[t], in0=X[xs], scalar1=MS[:, (n % 4) * 4 + 2:(n % 4) * 4 + 3],
                                                                  scalar2=None, op0=ALU.mult),
                      reads=[("X", xs), (tag, "rstd", n % 4)], writes=[("XN", t)])

        def transp(b):
            for t in range(4):
                bk = 0 if t % 2 == 0 else 7

                def f(e, t=t, bk=bk):
                    for k in range(8):
                        ins = e.transpose(PSB(bk)[:, k * 128:(k + 1) * 128], XN[t][:, k * 128:(k + 1) * 128], identb)
                    return ins
                P.add("pe", f, reads=[("XN", t), "identb"], writes=[("ps", bk)])
                P.add("dve", lambda e, t=t, bk=bk: e.tensor_tensor(out=XT[:, :, t * 128:(t + 1) * 128], in0=v3(PSB(bk), 8),
                                                               in1=gn[:, gcol:gcol + 8].unsqueeze(2).to_broadcast([128, 8, 128]),
                                                               op=ALU.mult),
                      reads=[("ps", bk), "gn"], writes=[("XT", t)])

        def gateup(b):
            for j in range(NFF):
                q = _q_of(j)
                pg = 1 + j % 2
                pu = 3 + j % 2

                def fg(e, j=j, pg=pg):
                    for k in range(8):
                        ins = e.matmul(PS(pg), Wg[:, k, j * 128:(j + 1) * 128], XT[:, k, :], start=(k == 0), stop=(k == 7))
                    return ins

                def fu(e, j=j, pu=pu):
                    for k in range(8):
                        ins = e.matmul(PS(pu), Wu[:, k, j * 128:(j + 1) * 128], XT[:, k, :], start=(k == 0), stop=(k == 7))
                    return ins
                P.add("pe", fg, reads=[("Wg", q)] + [("XT", t) for t in range(4)], writes=[("ps", pg)])
                P.add("pe", fu, reads=[("Wu", q)] + [("XT", t) for t in range(4)], writes=[("ps", pu)])
                P.add("act", lambda e, j=j, pg=pg: e.activation(out=SG[j % 2], in_=PS(pg), func=AF.Silu),
                      reads=[("ps", pg)], writes=[("SG", j % 2)])
                P.add("dve", lambda e, j=j, pu=pu: e.tensor_tensor(out=HT[:, j, :], in0=SG[j % 2], in1=PS(pu), op=ALU.mult),
                      reads=[("SG", j % 2), ("ps", pu)], writes=[("HT", j)])

        def down(b):
            for t in range(4):
                n = 4 * b + t
                rs = n % 3
                P.add("sp", lambda e, n=n, rs=rs: e.dma_start(out=XR[rs], in_=src[n * 128:(n + 1) * 128, :]),
                      reads=[(srckey, n), "setup_done"] + list(after), writes=[("XR", rs, 0), ("XR", rs, 1)], dma=(tag, "xr", rs))
                for half in range(2):
                    pd = 5 + half

                    def fd(e, t=t, half=half, pd=pd):
                        for j in range(NFF):
                            ins = e.matmul(PS(pd), HT[:, j, t * 128:(t + 1) * 128], Wd[:, j, half * 512:(half + 1) * 512],
                                           start=(j == 0), stop=(j == NFF - 1))
                        return ins
                    P.add("pe", fd, reads=[("HT", j) for j in range(NFF)] + [("Wd", q) for q in range(4)], writes=[("ps", pd)])
                    P.add("dve", lambda e, rs=rs, half=half, pd=pd: e.scalar_tensor_tensor(
                        out=XR[rs][:, half * 512:(half + 1) * 512], in0=PS(pd), scalar=0.5, in1=XR[rs][:, half * 512:(half + 1) * 512],
                        op0=ALU.mult, op1=ALU.add), reads=[("ps", pd), ("XR", rs, half)], writes=[("XR", rs, half)])
                if final:
                    c0 = 16
                    P.add("act", lambda e, rs=rs: e.activation(out=JUNKF, in_=XR[rs], func=AF.Square, scale=1.0 / 32.0,
                                                            accum_out=MS[:, c0:c0 + 1]),
                          reads=[("XR", rs, 0), ("XR", rs, 1)], writes=["fms", ("SG", 0)])
                    P.add("act", lambda e: e.activation(out=MS[:, c0 + 1:c0 + 2], in_=MS[:, c0:c0 + 1], func=AF.Ln, bias=epsc, scale=1.0),
                          reads=["fms", "epsc"], writes=["fln"])
                    P.add("act", lambda e: e.activation(out=MS[:, c0 + 2:c0 + 3], in_=MS[:, c0 + 1:c0 + 2], func=AF.Exp, scale=-0.5),
                          reads=["fln"], writes=["frs"])
                    P.add("dve", lambda e, rs=rs: e.scalar_tensor_tensor(out=XR[rs], in0=XR[rs], scalar=MS[:, c0 + 2:c0 + 3], in1=GF,
                                                                      op0=ALU.mult, op1=ALU.mult),
                          reads=["frs", "GF", ("XR", rs, 0), ("XR", rs, 1)], writes=[("XR", rs, 0), ("XR", rs, 1)])
                P.add("sp", lambda e, n=n, rs=rs: e.dma_start(out=dst[n * 128:(n + 1) * 128, :], in_=XR[rs]),
                      reads=[("XR", rs, 0), ("XR", rs, 1)], writes=[(dstkey, n)], dma=(tag, "st", rs))

        norm(0)
        transp(0)
        for b in range(NB):
            if b + 1 < NB:
                norm(b + 1)
            gateup(b)
            if b + 1 < NB:
                transp(b + 1)
            down(b)

    def load_gu1():
        svg = wg1.rearrange("(k p) f -> p k f", p=128)
        svu = wu1.rearrange("(k p) f -> p k f", p=128)
        for qi, (a, b) in enumerate(QS):
            P.add("pool", lambda e, a=a, b=b: e.dma_start(out=Wg[:, :, a * 128:b * 128], in_=svg[:, :, a * 128:b * 128]),
                  reads=["setup_done"], writes=[("Wg", qi)], dma=("p1", "Wg", qi))
            P.add("pool", lambda e, a=a, b=b: e.dma_start(out=Wu[:, :, a * 128:b * 128], in_=svu[:, :, a * 128:b * 128]),
                  reads=["setup_done"], writes=[("Wu", qi)], dma=("p1", "Wu", qi))
    ffn_phase("p1", xin, h_s, "xin", "h_s", 0, False, load_gu1, wd1, ())

    def finish_copy():
        for n in range(NCH):
            P.add("sp", lambda e, n=n: e.dma_start(out=yout[n * 128:(n + 1) * 128, :], in_=h_s[n * 128:(n + 1) * 128, :]),
                  reads=[("h_s", n)], writes=[("yout", n)], dma=("fin", n % 4))
        P.emit()
        return nc
    if stop_after == 1:
        return finish_copy()
    P.sched = True
    P.extra_reads = [("h_s", NCH - 1 - i) for i in range(3)]
    load_w_kf("p2", Win, win, "Wd")
    load_w_kf("p4", Wg, wg2, "Wg")
    load_w_kf("p4", Wu, wu2, "Wu")
    P1_DONE = [("h_s", NCH - 1 - i) for i in range(3)]
    o = ACT0
    HIN = [A(o + i * 4096, D, F32) for i in range(2)]; o += 8192
    UN = [A(o + i * 2048, D, BF16) for i in range(2)]; o += 4096
    UT = [v3(A(o + i * 2048, D, BF16), 8) for i in range(2)]; o += 4096
    TAB = [A(o + i * 576, 144, F32) for i in range(2)]; o += 1152
    PKBt = [A(o + i * 8192, 4096, BF16) for i in range(2)]; o += 16384
    PKAt = [A(o + i * 640, 320, BF16) for i in range(2)]; o += 1280
    RT2 = [[A(o + (j * 8 + i) * 1024, 256, F32) for i in range(8)] for j in range(2)]; o += 16384
    RQTM2 = [A(o + j * 1024, 512, BF16) for j in range(2)]; o += 2048
    RKTM2 = [A(o + j * 1024, 512, BF16) for j in range(2)]; o += 2048
    AQTM2 = [A(o + j * 1024, 512, BF16) for j in range(2)]; o += 2048
    AKTM2 = [A(o + j * 256, 128, BF16) for j in range(2)]; o += 512
    MS2 = A(o, 16, F32); o += 64
    TOT = A(o, 1024, F32); o += 4096
    CDP = A(o, 16, F32); o += 64
    assert o <= CONST0, o
    for i in range(2):
        P.add("pool", lambda e, i=i: e.memset(PKAt[i][:, 128:320], 0.0), reads=["setup_done"] + P1_DONE,
              writes=[("PKA", i)])
        P.add("pool", lambda e, i=i: e.memset(v3(PKAt[i][:, 128:320], 2)[:, :, 64:65], 1.0), reads=[("PKA", i)], writes=[("PKA", i)])
    P.add("pool", lambda e: e.memset(TOT, 0.0), reads=["setup_done"] + P1_DONE, writes=["TOTf", "TOTb"])

    def rotary(eng_mul, src3, cos, sin, dst3, half, tmps, rkeys, wkey, nh):
        x1 = src3[:, :, 0:half]; x2 = src3[:, :, half:2 * half]
        cb = cos.unsqueeze(1).to_broadcast([128, nh, half]); sb = sin.unsqueeze(1).to_broadcast([128, nh, half])
        t = [v3(tm[:, 0:nh * half], nh) for tm in tmps]
        tk = [("RT", id(tm)) for tm in tmps]
        P.add("dve", lambda e: e.tensor_tensor(out=t[0], in0=x1, in1=cb, op=ALU.mult), reads=rkeys, writes=[tk[0]])
        P.add("dve", lambda e: e.tensor_tensor(out=t[1], in0=x2, in1=sb, op=ALU.mult), reads=rkeys, writes=[tk[1]])
        P.add("dve", lambda e: e.tensor_tensor(out=t[2], in0=x2, in1=cb, op=ALU.mult), reads=rkeys, writes=[tk[2]])
        P.add("dve", lambda e: e.tensor_tensor(out=t[3], in0=x1, in1=sb, op=ALU.mult), reads=rkeys, writes=[tk[3]])
        P.add("pool", lambda e: e.tensor_tensor(out=dst3[:, :, 0:half], in0=t[0], in1=t[1], op=ALU.subtract),
              reads=[tk[0], tk[1]], writes=[(wkey, 0)])
        P.add("pool", lambda e: e.tensor_tensor(out=dst3[:, :, half:2 * half], in0=t[2], in1=t[3], op=ALU.add),
              reads=[tk[2], tk[3]], writes=[(wkey, 1)])

    for n in range(NCH):
        s = n % 2
        is_prompt = n < NPC
        RT = RT2[s]; RQTM = RQTM2[s]; RKTM = RKTM2[s]; AQTM = AQTM2[s]; AKTM = AKTM2[s]
        P.add("sp", lambda e, n=n, s=s: e.dma_start(out=HIN[s], in_=h_s[n * 128:(n + 1) * 128, :]),
              reads=[("h_s", n), "setup_done"] + P1_DONE, writes=[("HIN", s)], dma=("p2", "hin", s))
        P.add("sp", lambda e, n=n, s=s: e.dma_start(out=TAB[s], in_=tab[n]), reads=["setup_done"] + P1_DONE,
              writes=[("TAB", s)], dma=("p2", "tab", s))
        P.add("act", lambda e, s=s: e.activation(out=UN[s], in_=HIN[s], func=AF.Square, scale=1.0 / 32.0, accum_out=MS2[:, s * 4:s * 4 + 1]),
              reads=[("HIN", s)], writes=[("ms2", s), ("UN", s)], cost=1.0)
        P.add("act", lambda e, s=s: e.activation(out=MS2[:, s * 4 + 1:s * 4 + 2], in_=MS2[:, s * 4:s * 4 + 1], func=AF.Ln, bias=epsc, scale=1.0),
              reads=[("ms2", s), "epsc"], writes=[("ln2", s)])
        P.add("act", lambda e, s=s: e.activation(out=MS2[:, s * 4 + 2:s * 4 + 3], in_=MS2[:, s * 4 + 1:s * 4 + 2], func=AF.Exp, scale=-0.5),
              reads=[("ln2", s)], writes=[("rs2", s)])
        P.add("act", lambda e, s=s: e.activation(out=UN[s], in_=HIN[s], func=AF.Copy, scale=MS2[:, s * 4 + 2:s * 4 + 3]),
              reads=[("HIN", s), ("rs2", s)], writes=[("UN", s)], cost=1.0)

        def ftr(e, s=s):
            for k in range(8):
                ins = e.transpose(PSB(0)[:, k * 128:(k + 1) * 128], UN[s][:, k * 128:(k + 1) * 128], identb)
            return ins
        P.add("pe", ftr, reads=[("UN", s), "identb"], writes=[("ps", 0)], cost=0.9)
        P.add("dve", lambda e, s=s: e.tensor_tensor(out=UT[s], in0=v3(PSB(0), 8), in1=gn[:, 8:16].unsqueeze(2).to_broadcast([128, 8, 128]),
                                                 op=ALU.mult), reads=[("ps", 0), "gn"], writes=[("UT", s)], cost=0.9)
        for g in range(6):
            w = 512 if g < 5 else 256

            def fp(e, g=g, w=w, s=s):
                for k in range(8):
                    ins = e.matmul(PS(1 + g)[:, 0:w], UT[s][:, k, :], Win[:, k, g * 512:g * 512 + w], start=(k == 0), stop=(k == 7))
                return ins
            P.add("pe", fp, reads=[("UT", s)] + [("Wd", q) for q in range(4)], writes=[("ps", 1 + g)], cost=(2.0 if g < 5 else 1.1))
        PB = PKBt[s]; PA = PKAt[s]
        pkb_key = ("PKB", s)
        P.add("act", lambda e, PB=PB: e.activation(out=PB[:, 3072:3584], in_=PS(4), func=AF.Copy), reads=[("ps", 4)], writes=[(pkb_key, "rv")])
        P.add("act", lambda e, PB=PB: e.activation(out=PB[:, 3584:4096], in_=PS(5), func=AF.Silu), reads=[("ps", 5)], writes=[(pkb_key, "sg")])
        P.add("act", lambda e, PA=PA: e.activation(out=v3(PA[:, 128:320], 2)[:, :, 0:64], in_=v3(PS(6)[:, 128:256], 2), func=AF.Copy),
              reads=[("ps", 6)], writes=[("PKA", s)])
        rotary("dve", v3(PS(2), 4), TAB[s][:, 0:64], TAB[s][:, 64:128], v3(RQTM, 4), 64, RT[0:4], [("ps", 2), ("TAB", s)], ("RQTM", s), 4)
        rotary("dve", v3(PS(3), 4), TAB[s][:, 0:64], TAB[s][:, 64:128], v3(RKTM, 4), 64, RT[4:8], [("ps", 3), ("TAB", s)], ("RKTM", s), 4)
        rotary("dve", v3(PS(1), 8), TAB[s][:, 128:136], TAB[s][:, 136:144], v3(AQTM, 8), 8, RT[0:4], [("ps", 1), ("TAB", s)], ("AQTM", s), 8)
        P.add("act", lambda e, AQTM=AQTM: e.activation(out=v3(AQTM, 8)[:, :, 16:64], in_=v3(PS(1), 8)[:, :, 16:64], func=AF.Copy),
              reads=[("ps", 1)], writes=[(("AQTM", s), 2)])
        rotary("dve", v3(PS(6)[:, 0:128], 2), TAB[s][:, 128:136], TAB[s][:, 136:144], v3(AKTM, 2), 8, RT[4:8], [("ps", 6), ("TAB", s)], ("AKTM", s), 2)
        P.add("act", lambda e, AKTM=AKTM: e.activation(out=v3(AKTM, 2)[:, :, 16:64], in_=v3(PS(6)[:, 0:128], 2)[:, :, 16:64], func=AF.Copy),
              reads=[("ps", 6)], writes=[(("AKTM", s), 2)])

        def ftq(e, RQTM=RQTM, RKTM=RKTM):
            for h in range(4):
                ins = e.transpose(PSB(7)[:, h * 128:(h + 1) * 128], RQTM[:, h * 128:(h + 1) * 128], identb)
            for h in range(4):
                ins = e.transpose(PSB(7)[:, 512 + h * 128:512 + (h + 1) * 128], RKTM[:, h * 128:(h + 1) * 128], identb)
            return ins
        P.add("pe", ftq, reads=[(("RQTM", s), 0), (("RQTM", s), 1), (("RKTM", s), 0), (("RKTM", s), 1), "identb"], writes=[("ps", 7)])
        P.add("dve", lambda e, PB=PB: e.tensor_tensor(out=PB[:, 512:1024], in0=PSB(7)[:, 0:512], in1=WF, op=ALU.mult),
              reads=[("ps", 7)] + [("WF", h) for h in range(4)], writes=[(pkb_key, "qf")])
        P.add("dve", lambda e, PB=PB: e.tensor_tensor(out=PB[:, 1024:1536], in0=PSB(7)[:, 0:512], in1=WB, op=ALU.mult),
              reads=[("ps", 7)] + [("WB", h) for h in range(4)], writes=[(pkb_key, "qb")])
        P.add("act", lambda e, PB=PB: e.activation(out=PB[:, 1536:2048], in_=PSB(7)[:, 512:1024], func=AF.Copy),
              reads=[("ps", 7)], writes=[(pkb_key, "kT")])

        def fta(e, AQTM=AQTM, AKTM=AKTM):
            for h in range(4):
                ins = e.transpose(PSB(1)[:, h * 128:(h + 1) * 128], AQTM[:, h * 128:(h + 1) * 128], identb)
            ins = e.transpose(PSB(1)[:, 512:640], AKTM, identb)
            return ins
        P.add("pe", fta, reads=[(("AQTM", s), 0), (("AQTM", s), 1), (("AQTM", s), 2), (("AKTM", s), 0), (("AKTM", s), 1), (("AKTM", s), 2), "identb"],
              writes=[("ps", 1)])
        P.add("act", lambda e, PB=PB: e.activation(out=PB[:, 0:512], in_=PSB(1)[:, 0:512], func=AF.Copy), reads=[("ps", 1)],
              writes=[(pkb_key, "aq")])
        P.add("act", lambda e, PA=PA: e.activation(out=PA[:, 0:128], in_=PSB(1)[:, 512:640], func=AF.Copy), reads=[("ps", 1)],
              writes=[("PKA", s)])
        P.add("pool", lambda e, PB=PB, RKTM=RKTM: e.tensor_tensor(out=v3(PB[:, 2048:2560], 4), in0=v3(RKTM, 4),
                                                     in1=wkf.unsqueeze(2).to_broadcast([128, 4, 128]), op=ALU.mult),
              reads=[(("RKTM", s), 0), (("RKTM", s), 1)] + [("wkf", h) for h in range(4)], writes=[(pkb_key, "kf")])
        P.add("pool", lambda e, PB=PB, RKTM=RKTM: e.tensor_tensor(out=v3(PB[:, 2560:3072], 4), in0=v3(RKTM, 4),
                                                     in1=wkb.unsqueeze(2).to_broadcast([128, 4, 128]), op=ALU.mult),
              reads=[(("RKTM", s), 0), (("RKTM", s), 1)] + [("wkb", h) for h in range(4)], writes=[(pkb_key, "kb")])
        if is_prompt:
            def fkv(e, PB=PB):
                for d_ in range(2):
                    for h in range(4):
                        ins = e.matmul(PS(2 + d_)[:, h * 128:(h + 1) * 128], PB[:, 2048 + d_ * 512 + h * 128:2048 + d_ * 512 + (h + 1) * 128],
                                       PB[:, 3072 + h * 128:3072 + (h + 1) * 128], start=True, stop=True)
                return ins
            P.add("pe", fkv, reads=[(pkb_key, "kf"), (pkb_key, "kb"), (pkb_key, "rv")], writes=[("ps", 2), ("ps", 3)], cost=0.9)
            P.add("act", lambda e, n=n: e.activation(out=CDP[:, 0:4], in_=ldb, func=AF.Exp, scale=float(C * n)), reads=["ldb"], writes=["cdp"])
            for h in range(4):
                hs = slice(h * 128, (h + 1) * 128)
                P.add("dve", lambda e, h=h, hs=hs: e.scalar_tensor_tensor(out=TOT[:, hs], in0=TOT[:, hs], scalar=cdf[:, h:h + 1], in1=PS(2)[:, hs],
                                                                     op0=ALU.mult, op1=ALU.add),
                      reads=[("ps", 2), "cdf", "TOTf"], writes=["TOTf"])
                P.add("dve", lambda e, h=h, hs=hs: e.scalar_tensor_tensor(out=TOT[:, 512 + h * 128:512 + (h + 1) * 128], in0=PS(3)[:, hs],
                                                                     scalar=CDP[:, h:h + 1], in1=TOT[:, 512 + h * 128:512 + (h + 1) * 128],
                                                                     op0=ALU.mult, op1=ALU.add),
                      reads=[("ps", 3), "cdp", "TOTb"], writes=["TOTb"])
        allpkb = [(pkb_key, x) for x in ("rv", "sg", "qf", "qb", "kT", "aq", "kf", "kb")]
        P.add("sp", lambda e, n=n, PB=PB: e.dma_start(out=pkB[n], in_=PB), reads=allpkb, writes=[("pkB", n)], dma=("p2", "stB", s))
        P.add("sp", lambda e, n=n, PA=PA: e.dma_start(out=pkA[n], in_=PA), reads=[("PKA", s)], writes=[("pkA", n)], dma=("p2", "stA", s))
        if n == 0:
            P.add("sp", lambda e, PA=PA: e.dma_start(out=xkv_in.ap()[:, 0:320], in_=PA), reads=[("PKA", s)], writes=["xkv_in0"], dma="xkv0")
        if n == NPC - 1:
            P.add("sp", lambda e, PA=PA: e.dma_start(out=xkv_in.ap()[:, 320:640], in_=PA), reads=[("PKA", s)], writes=["xkv_in1"], dma="xkv1")
            P.add("sp", lambda e: e.dma_start(out=xst_in.ap(), in_=TOT), reads=["TOTf", "TOTb"], writes=["xst_in"], dma="xst")
            PAIRS = [[0, 1], [2, 3], [4, 5], [6, 7]]
            P.add("pool", lambda e: e.collective_compute("AllGather", ALU.bypass, replica_groups=PAIRS,
                                                         ins=[xkv_in.ap().opt()], outs=[xkv_out.ap().opt()]),
                  reads=["xkv_in0", "xkv_in1"], writes=["xkv_out"], dma="cc_kv", inc=1)
            P.add("pool", lambda e: e.collective_compute("AllGather", ALU.bypass, replica_groups=PAIRS,
                                                         ins=[xst_in.ap().opt()], outs=[xst_out.ap().opt()]),
                  reads=["xst_in"], writes=["xst_out"], dma="cc_st", inc=1)

    if stop_after == 2:
        return finish_copy()
    P2_DONE = [("pkB", NCH - 1), ("pkA", NCH - 1), ("pkB", NCH - 2), ("pkA", NCH - 2), "xkv_in0", "xkv_in1", "xst_in"]
    P.extra_reads = list(P2_DONE)
    P.add("pool", lambda e: e.dma_start(out=Wout, in_=wout.rearrange("(k p) f -> p k f", p=128)), reads=P2_DONE,
          writes=[("Wd", q) for q in range(4)], dma="wout")
    o = SbS_off
    SBS_A = (CONST0 - 0)
    NSB_W1 = (ACT0 - SbS_off) // 1024
    o = ACT0
    NSB_ACT = max(0, max(NPC, NSC) - NSB_W1)
    SBS = [A(SbS_off + i * 1024, 512, BF16) for i in range(NSB_W1)] + [A(o + i * 1024, 512, BF16) for i in range(NSB_ACT)]
    o += NSB_ACT * 1024
    AQ3 = [A(o + i * 1024, 512, BF16) for i in range(3)]; o += 3072
    RET3 = [A(o + i * 7168, 3584, BF16) for i in range(2)]; o += 14336
    KVt = [A(o + i * 640, 320, BF16) for i in range(4)]; o += 2560
    HRES = [A(o + i * 4096, D, F32) for i in range(2)]; o += 8192
    PT = [[A(o + (g * 3 + b) * 1024, 512, BF16) for b in range(3)] for g in range(2)]; o += 6144
    AO = A(o, 512, F32); o += 2048
    CAT = [A(o + i * 2048, D, BF16) for i in range(3)]; o += 6144
    CATT = v3(A(o, D, BF16), 8); o += 2048
    ATM = A(o, 512, BF16); o += 1024
    SF = A(o, 512, F32); o += 2048
    SB = A(o, 512, F32); o += 2048
    SFB = [A(o + i * 1024, 512, BF16) for i in range(2)]; o += 2048
    RN = A(o, 512, F32); o += 2048
    KBV = [A(o + i * 2048, 1024, BF16) for i in range(2)]; o += 4096
    JUNK3 = A(o, 512, BF16); o += 1024
    ST3 = A(o, 64, F32); o += 256
    assert o <= CONST0, o

    def mixer_seq(c0, L, exch):
        if exch:
            P.add("sp", lambda e: e.dma_start(out=SB, in_=xst_out.ap()[128:256, 512:1024]), reads=["xst_out"] + P2_DONE, writes=["SB"], dma="sbinit")
            P.add("dve", lambda e: e.tensor_scalar(out=SB, in0=SB, scalar1=flags[:, 1:2], scalar2=None, op0=ALU.mult),
                  reads=["SB", "flags"], writes=["SB"])
        else:
            P.add("dve", lambda e: e.memset(SB, 0.0), reads=P2_DONE, writes=["SB"])
        for l in range(L - 1, -1, -1):
            n = c0 + l
            s = l % 2
            P.add("act", lambda e, l=l: e.activation(out=SBS[l], in_=SB, func=AF.Copy), reads=["SB"], writes=[("SBS", l)])
            P.add("sp", lambda e, n=n, s=s: e.dma_start(out=KBV[s], in_=pkB[n][:, 2560:3584]), reads=[("pkB", n)] + P2_DONE,
                  writes=[("KBV", s)], dma=("p3", "kbv", s))

            def fkb(e, s=s):
                for h in range(4):
                    ins = e.matmul(PS(4)[:, h * 128:(h + 1) * 128], KBV[s][:, h * 128:(h + 1) * 128], KBV[s][:, 512 + h * 128:512 + (h + 1) * 128],
                                   start=True, stop=True)
                return ins
            P.add("pe", fkb, reads=[("KBV", s)], writes=[("ps", 4)])
            for h in range(4):
                hs = slice(h * 128, (h + 1) * 128)
                P.add("dve", lambda e, h=h, hs=hs: e.scalar_tensor_tensor(out=SB[:, hs], in0=SB[:, hs], scalar=cdb[:, h:h + 1], in1=PS(4)[:, hs],
                                                                     op0=ALU.mult, op1=ALU.add),
                      reads=[("ps", 4), "cdb", "SB"], writes=["SB"])
        if exch:
            P.add("sp", lambda e: e.dma_start(out=SF, in_=xst_out.ap()[0:128, 0:512]), reads=["xst_out"] + P2_DONE, writes=["SF"], dma="sfinit")
            P.add("dve", lambda e: e.tensor_scalar(out=SF, in0=SF, scalar1=flags[:, 0:1], scalar2=None, op0=ALU.mult),
                  reads=["SF", "flags"], writes=["SF"])
        else:
            P.add("dve", lambda e: e.memset(SF, 0.0), reads=P2_DONE, writes=["SF"])
        P.add("act", lambda e: e.activation(out=SFB[0], in_=SF, func=AF.Copy), reads=["SF"], writes=[("SFB", 0)])

        def load_kv(l):
            slot = (l + 1) % 4
            if l == -1:
                src = xkv_out.ap()[0:128, 320:640]; rk = ["xkv_out"]
            elif l == L:
                src = xkv_out.ap()[128:256, 0:320]; rk = ["xkv_out"]
            else:
                src = pkA[c0 + l]; rk = [("pkA", c0 + l)]
            P.add("sp", lambda e, slot=slot, src=src: e.dma_start(out=KVt[slot], in_=src), reads=rk + P2_DONE, writes=[("KV", slot)],
                  dma=("p3", "kv", slot))

        def has_kv(l):
            return (0 <= l < L) or (exch and l in (-1, L))

        def load_attn(l):
            if not (0 <= l < L):
                return
            n = c0 + l
            if has_kv(l + 1):
                load_kv(l + 1)
            P.add("sp", lambda e, n=n, l=l: e.dma_start(out=AQ3[l % 3], in_=pkB[n][:, 0:512]), reads=[("pkB", n)] + P2_DONE,
                  writes=[("AQ3", l % 3)], dma=("p3", "aq", l % 3))

        def load_ret(l):
            if not (0 <= l < L):
                return
            n = c0 + l
            s = l % 2
            P.add("sp", lambda e, n=n, s=s: e.dma_start(out=RET3[s], in_=pkB[n][:, 512:4096]), reads=[("pkB", n)] + P2_DONE,
                  writes=[("RET3", s)], dma=("p3", "ret", s))

        def load_hres(l):
            if not (0 <= l < L):
                return
            n = c0 + l
            s = l % 2
            P.add("sp", lambda e, n=n, s=s: e.dma_start(out=HRES[s], in_=h_s[n * 128:(n + 1) * 128, :]), reads=[("h_s", n)] + P2_DONE,
                  writes=[("HRES", s, 0), ("HRES", s, 1)], dma=("p3", "hres", s))

        def blocks_of(l):
            blocks = []
            if l > 0 or exch:
                blocks.append((l - 1, MprevH if l == 0 else Mprev, "MprevH" if l == 0 else "Mprev"))
            blocks.append((l, None, None))
            if l + 1 < L or exch:
                blocks.append((l + 1, MnextH if l + 1 == L else Mnext, "MnextH" if l + 1 == L else "Mnext"))
            return blocks

        def attn_scores(l, g):
            if not (0 <= l < L):
                return
            AQ = AQ3[l % 3]
            for bi, (bl, M, Mk) in enumerate(blocks_of(l)):
                slot = (bl + 1) % 4
                scb = (g * 3 + bi) % 2
                P.add("pe", lambda e, g=g, slot=slot, scb=scb, AQ=AQ: e.matmul(
                    PS(scb), KVt[slot][g * 64:(g + 1) * 64, 0:128], AQ[g * 64:(g + 1) * 64, 0:512], start=True, stop=True,
                    tile_position=(g * 64, 0)), reads=[("KV", slot), ("AQ3", l % 3)], writes=[("ps", scb)])
                P.add("act", lambda e, g=g, bi=bi, scb=scb: e.activation(out=PT[g][bi], in_=PS(scb), func=AF.Exp, scale=0.125),
                      reads=[("ps", scb)], writes=[("PT", g, bi)])
                if M is not None:
                    P.add("pool", lambda e, g=g, bi=bi, M=M: e.tensor_tensor(out=v3(PT[g][bi], 4), in0=v3(PT[g][bi], 4),
                                                                       in1=M.unsqueeze(1).to_broadcast([128, 4, 128]), op=ALU.mult),
                          reads=[("PT", g, bi), Mk], writes=[("PT", g, bi)])

        def attn_pv(l, g):
            if not (0 <= l < L):
                return
            blocks = blocks_of(l)

            def fpv(e, g=g, blocks=blocks):
                for hh in range(4):
                    for bi, (bl, M, Mk) in enumerate(blocks):
                        slot = (bl + 1) % 4
                        ins = e.matmul(PS(2 + g)[:, hh * 72:hh * 72 + 65], PT[g][bi][:, hh * 128:(hh + 1) * 128],
                                       KVt[slot][:, 128 + g * 96:128 + g * 96 + 65], start=(bi == 0), stop=(bi == len(blocks) - 1))
                return ins
            P.add("pe", fpv, reads=[("PT", g, bi) for bi in range(len(blocks))] + [("KV", (bl + 1) % 4) for bl, _, _ in blocks],
                  writes=[("ps", 2 + g)])
            Og = v3(PS(2 + g)[:, 0:288], 4)
            P.add("dve", lambda e, g=g, Og=Og: e.tensor_tensor(out=ST3[:, g * 4:g * 4 + 4], in0=Og[:, :, 64:65].rearrange("p a b -> p (a b)"),
                                                         in1=esink[:, g * 4:g * 4 + 4], op=ALU.add),
                  reads=[("ps", 2 + g), "esink"], writes=[("den", g)])
            P.add("dve", lambda e, g=g: e.reciprocal(out=ST3[:, 8 + g * 4:8 + g * 4 + 4], in_=ST3[:, g * 4:g * 4 + 4]),
                  reads=[("den", g)], writes=[("rden", g)])
            P.add("dve", lambda e, g=g, Og=Og: e.tensor_tensor(out=v3(AO[:, g * 256:(g + 1) * 256], 4), in0=Og[:, :, 0:64],
                                                         in1=ST3[:, 8 + g * 4:8 + g * 4 + 4].unsqueeze(2).to_broadcast([128, 4, 64]),
                                                         op=ALU.mult),
                  reads=[("ps", 2 + g), ("rden", g)], writes=[("AO", g)])

        def attn_norm(l):
            if not (0 <= l < L):
                return
            c = l % 3
            P.add("act", lambda e: e.activation(out=JUNK3, in_=AO, func=AF.Square, scale=float(512 ** -0.5), accum_out=ST3[:, 16:17]),
                  reads=[("AO", 0), ("AO", 1)], writes=["msa"] + [("junk3", h) for h in range(4)])
            P.add("act", lambda e: e.activation(out=ST3[:, 17:18], in_=ST3[:, 16:17], func=AF.Ln, bias=epsc, scale=1.0), reads=["msa", "epsc"], writes=["lna"])
            P.add("act", lambda e: e.activation(out=ST3[:, 18:19], in_=ST3[:, 17:18], func=AF.Exp, scale=-0.5), reads=["lna"], writes=["rsa"])
            P.add("act", lambda e, c=c: e.activation(out=CAT[c][:, 0:512], in_=AO, func=AF.Copy, scale=ST3[:, 18:19]),
                  reads=["rsa", ("AO", 0), ("AO", 1)], writes=[("CAT", c, 0)])

        def ret_mm(l):
            if not (0 <= l < L):
                return
            s = l % 2
            R3 = RET3[s]

            def fat(e, R3=R3):
                for h in range(4):
                    ins = e.matmul(PS(4)[:, h * 128:(h + 1) * 128], R3[:, 1024 + h * 128:1024 + (h + 1) * 128],
                                   R3[:, h * 128:(h + 1) * 128], start=True, stop=True)
                return ins
            P.add("pe", fat, reads=[("RET3", s)], writes=[("ps", 4)])
            P.add("dve", lambda e: e.tensor_tensor(out=ATM, in0=PS(4), in1=DTq, op=ALU.mult),
                  reads=[("ps", 4)] + [("DTq", h) for h in range(4)], writes=["ATM"])

            def frr(e, R3=R3, l=l, s=s):
                for h in range(4):
                    hs = slice(h * 128, (h + 1) * 128)
                    e.matmul(PS(5)[:, hs], ATM[:, hs], R3[:, 2560 + h * 128:2560 + (h + 1) * 128], start=True, stop=False)
                    e.matmul(PS(5)[:, hs], R3[:, h * 128:(h + 1) * 128], SFB[s][:, hs], start=False, stop=False)
                    ins = e.matmul(PS(5)[:, hs], R3[:, 512 + h * 128:512 + (h + 1) * 128], SBS[l][:, hs], start=False, stop=True)
                return ins
            P.add("pe", frr, reads=["ATM", ("RET3", s), ("SFB", s), ("SBS", l)], writes=[("ps", 5)], cost=1.3)

        def ret_stats(l):
            if not (0 <= l < L):
                return
            P.add("dve", lambda e: e.tensor_reduce(out=ST3[:, 20:24], in_=v3(PS(5), 4), axis=AX.X, op=ALU.add), reads=[("ps", 5)], writes=["s1"])
            P.add("dve", lambda e: e.tensor_scalar(out=ST3[:, 28:32], in0=ST3[:, 20:24], scalar1=1.0 / 128.0, scalar2=None, op0=ALU.mult),
                  reads=["s1"], writes=["mean"])
            for h in range(4):
                hs = slice(h * 128, (h + 1) * 128)
                P.add("dve", lambda e, h=h, hs=hs: e.tensor_scalar(out=RN[:, hs], in0=PS(5)[:, hs], scalar1=ST3[:, 28 + h:29 + h],
                                                              scalar2=None, op0=ALU.subtract),
                      reads=[("ps", 5), "mean"], writes=[("RN", h)])
                P.add("act", lambda e, h=h, hs=hs: e.activation(out=JUNK3[:, hs], in_=RN[:, hs], func=AF.Square,
                                                           scale=float(128 ** -0.5), accum_out=ST3[:, 36 + h:37 + h]),
                      reads=[("RN", h)], writes=[("var", h), ("junk3", h)])
            P.add("act", lambda e: e.activation(out=ST3[:, 40:44], in_=ST3[:, 36:40], func=AF.Ln, bias=epsc, scale=1.0),
                  reads=[("var", h) for h in range(4)] + ["epsc"], writes=["lnr"])
            P.add("act", lambda e: e.activation(out=ST3[:, 44:48], in_=ST3[:, 40:44], func=AF.Exp, scale=-0.5), reads=["lnr"], writes=["rsr"])

        def ret_fin(l):
            if not (0 <= l < L):
                return
            s = l % 2
            c = l % 3
            R3 = RET3[s]
            for h in range(4):
                hs = slice(h * 128, (h + 1) * 128)
                P.add("dve", lambda e, h=h, hs=hs, c=c, R3=R3: e.scalar_tensor_tensor(
                    out=CAT[c][:, 512 + h * 128:512 + (h + 1) * 128], in0=RN[:, hs], scalar=ST3[:, 44 + h:45 + h],
                    in1=R3[:, 3072 + h * 128:3072 + (h + 1) * 128], op0=ALU.mult, op1=ALU.mult),
                      reads=[("RN", h), "rsr", ("RET3", s)], writes=[("CAT", c, 1)])

        def state_upd(l):
            if not (0 <= l < L) or l + 1 >= L:
                return
            s = l % 2
            R3 = RET3[s]

            def fkf(e, R3=R3):
                for h in range(4):
                    ins = e.matmul(PS(4)[:, h * 128:(h + 1) * 128], R3[:, 1536 + h * 128:1536 + (h + 1) * 128],
                                   R3[:, 2560 + h * 128:2560 + (h + 1) * 128], start=True, stop=True)
                return ins
            P.add("pe", fkf, reads=[("RET3", s)], writes=[("ps", 4)])
            for h in range(4):
                hs = slice(h * 128, (h + 1) * 128)
                P.add("dve", lambda e, h=h, hs=hs: e.scalar_tensor_tensor(out=SF[:, hs], in0=SF[:, hs], scalar=cdf[:, h:h + 1], in1=PS(4)[:, hs],
                                                                     op0=ALU.mult, op1=ALU.add),
                      reads=[("ps", 4), "cdf", "SF"], writes=["SF"])
            P.add("act", lambda e, s=s: e.activation(out=SFB[1 - s], in_=SF, func=AF.Copy), reads=["SF"], writes=[("SFB", 1 - s)])

        def outp(l):
            if not (0 <= l < L):
                return
            n = c0 + l
            s = l % 2
            c = l % 3

            def ftc(e, c=c):
                for f in range(8):
                    ins = e.transpose(PSB(6)[:, f * 128:(f + 1) * 128], CAT[c][:, f * 128:(f + 1) * 128], identb)
                return ins
            P.add("pe", ftc, reads=[("CAT", c, 0), ("CAT", c, 1), "identb"], writes=[("ps", 6)], cost=0.9)
            P.add("dve", lambda e: e.tensor_tensor(out=CATT[:, 0:4, :], in0=v3(PSB(6), 8)[:, 0:4, :],
                                                in1=gn[:, 24:28].unsqueeze(2).to_broadcast([128, 4, 128]), op=ALU.mult),
                  reads=[("ps", 6), "gn"], writes=[("CATT", 0)])
            P.add("act", lambda e: e.activation(out=CATT[:, 4:8, :], in_=v3(PSB(6), 8)[:, 4:8, :], func=AF.Copy), reads=[("ps", 6)], writes=[("CATT", 1)])
            for half in range(2):
                def fo(e, half=half):
                    for f in range(8):
                        ins = e.matmul(PS(7), CATT[:, f, :], Wout[:, f, half * 512:(half + 1) * 512], start=(f == 0), stop=(f == 7))
                    return ins
                P.add("pe", fo, reads=[("CATT", 0), ("CATT", 1)] + [("Wd", q) for q in range(4)], writes=[("ps", 7)], cost=2.0)
                P.add("dve", lambda e, s=s, half=half: e.tensor_tensor(out=HRES[s][:, half * 512:(half + 1) * 512], in0=PS(7),
                                                                  in1=HRES[s][:, half * 512:(half + 1) * 512], op=ALU.add),
                      reads=[("ps", 7), ("HRES", s, half)], writes=[("HRES", s, half)])
            P.add("pool", lambda e, n=n, s=s: e.dma_start(out=h_s[n * 128:(n + 1) * 128, :], in_=HRES[s]),
                  reads=[("HRES", s, 0), ("HRES", s, 1)], writes=[("h_s", n)], dma=("p3", "hst", s))

        if exch:
            load_kv(-1)
        load_kv(0)
        load_attn(0)
        load_ret(0)
        load_hres(0)
        load_attn(1)
        attn_scores(0, 0); attn_pv(0, 0); attn_scores(0, 1); attn_pv(0, 1); attn_norm(0)
        load_attn(2)
        for i in range(0, L + 1):
            load_ret(i + 1)
            attn_scores(i + 1, 0)
            ret_mm(i)
            attn_pv(i + 1, 0)
            attn_scores(i + 1, 1)
            ret_stats(i)
            attn_pv(i + 1, 1)
            state_upd(i)
            ret_fin(i)
            outp(i - 1)
            attn_norm(i + 1)
            load_attn(i + 3)
            load_hres(i + 1)

    mixer_seq(NPC, NSC, False)
    mixer_seq(0, NPC, True)

    if stop_after == 3:
        return finish_copy()
    P.sched = False
    P.extra_reads = []
    ffn_phase("p4", h_s, yout, "h_s", "yout", 16, True, None, wd2, [("h_s", NPC - 1), ("h_s", NPC - 2), ("h_s", NCH - 1), ("h_s", NCH - 2)])

    P.emit()
    return nc


def _w_in_perm():
    aq = []
    for hp in range(4):
        for h in (hp, hp + 4):
            aq += list(range(h * 64, (h + 1) * 64))
    o1 = 512; o2 = 640; o3 = 768; o4 = 1280; o5 = 1792; o6 = 2304
    return np.array(aq + list(range(o3, o4)) + list(range(o4, o5)) + list(range(o5, o6)) + list(range(o6, 2816))
                    + list(range(o1, o2)) + list(range(o2, o3)), dtype=np.int64)


def _consts():
    p = np.arange(128, dtype=np.float32)
    j = p[:, None]; i = p[None, :]
    cst = np.zeros((128, 772), np.float32)
    cst[:, 0:128] = np.eye(128, dtype=np.float32)
    cst[:, 128:256] = (j <= i)
    cst[:, 256:384] = (j >= i)
    cst[:, 384:512] = j - i
    cst[:, 512:640] = np.broadcast_to(i + 1.0, (128, 128))
    cst[:, 640:768] = np.broadcast_to(128.0 - i, (128, 128))
    cst[:, 768] = p + 1.0
    cst[:, 769] = 127.0 - p
    cst[:, 770] = p
    return cst


def _tables(pos):
    pos = pos.astype(np.float32)
    inv_a = (np.float32(500000.0) ** (-np.arange(0, 16, 2, dtype=np.float32) / np.float32(16))).astype(np.float32)
    inv_r = (np.float32(10000.0) ** (-np.linspace(0.0, 1.0, 64, dtype=np.float32))).astype(np.float32)
    ang_r = (pos[:, None] * inv_r[None, :]).astype(np.float32)
    ang_a = (pos[:, None] * inv_a[None, :]).astype(np.float32)
    return np.concatenate([np.cos(ang_r), np.sin(ang_r), np.cos(ang_a), np.sin(ang_a)], axis=1).astype(np.float32)


_NC_CACHE = {}


def run_cores(x_prompt, x_sample, W, NPC, NSC, stop_after=9):
    key = (NPC, NSC, stop_after)
    if key not in _NC_CACHE:
        _NC_CACHE[key] = build(NPC, NSC, stop_after)
    nc = _NC_CACHE[key]
    perm = _w_in_perm()
    f = lambda a: np.ascontiguousarray(np.asarray(a, dtype=np.float32))
    gains = np.zeros((128, 36), np.float32)
    gains[:, 0:8] = f(W["ffn1_norm"])[0].reshape(8, 128).T
    gains[:, 8:16] = f(W["mix_norm"])[0].reshape(8, 128).T
    gains[:, 16:24] = f(W["ffn2_norm"])[0].reshape(8, 128).T
    gains[:, 24:28] = f(W["attn_out_norm"])[0].reshape(4, 128).T
    common = dict(
        wg1=f(W["ffn1_w_gate"])[0], wu1=f(W["ffn1_w_up"])[0], wd1=f(W["ffn1_w_down"])[0],
        wg2=f(W["ffn2_w_gate"])[0], wu2=f(W["ffn2_w_up"])[0], wd2=f(W["ffn2_w_down"])[0],
        win=np.ascontiguousarray(f(W["w_in"])[0][:, perm]), wout=f(W["w_out"])[0],
        gains=gains, gfin=f(W["final_norm"]), sink=f(W["attn_sink"])[0],
        ldf=f(W["ret_log_decay_fwd"])[0], ldb=f(W["ret_log_decay_bwd"])[0], cst=_consts(),
    )
    HL = NPC * 128
    in_maps = []
    for i in range(8):
        p, half = i // 2, i % 2
        xin = np.concatenate([x_prompt[p, half * HL:(half + 1) * HL], x_sample[i]], axis=0)
        pos = np.concatenate([np.arange(half * HL, (half + 1) * HL), np.arange(NSC * 128)])
        tab = _tables(pos).reshape(NPC + NSC, 128, 144)
        flg = np.zeros((128, 2), np.float32)
        flg[:, 0] = 1.0 if half == 1 else 0.0
        flg[:, 1] = 1.0 if half == 0 else 0.0
        m = dict(common)
        m.update(xin=np.ascontiguousarray(xin, dtype=np.float32), tab=np.ascontiguousarray(tab), flg=flg)
        in_maps.append(m)
    res = run_bass_kernel_spmd(nc, in_maps, core_ids=list(range(8)))
    yp = np.zeros_like(np.asarray(x_prompt, dtype=np.float32))
    ys = np.zeros_like(np.asarray(x_sample, dtype=np.float32))
    for i in range(8):
        y = np.asarray(res.results[i]["yout"])
        p, half = i // 2, i % 2
        yp[p, half * HL:(half + 1) * HL] = y[:HL]
        ys[i] = y[HL:]
    return yp, ys


def kernel(x_prompt, x_sample, **W):
    x_prompt = np.asarray(x_prompt, dtype=np.float32)
    x_sample = np.asarray(x_sample, dtype=np.float32)
    NPC = x_prompt.shape[1] // 256
    NSC = x_sample.shape[1] // 128
    yp, ys = run_cores(x_prompt, x_sample, W, NPC, NSC)
    return (yp, ys)
```

```python
import contextlib
import numpy as np
import ml_dtypes
import concourse.bass as bass
import concourse.mybir as mybir
from concourse.bass_utils import run_bass_kernel_spmd

F32 = mybir.dt.float32
BF16 = mybir.dt.bfloat16
AF = mybir.ActivationFunctionType
ALU = mybir.AluOpType
AX = mybir.AxisListType

D = 1024
DFF = 2816
NFF = 22
C = 128
EPS = 1e-6
QS = [(0, 6), (6, 12), (12, 17), (17, 22)]


def _q_of(j):
    for qi, (a, b) in enumerate(QS):
        if a <= j < b:
            return qi


class Prog:
    EPOCH = 3000

    DEFCOST = {"pe": 0.6, "act": 0.6, "dve": 0.5, "pool": 1.15, "sp": 0.15}

    def __init__(self, nc):
        self.nc = nc
        self.ops = []
        self.sched = False
        self.extra_reads = []

    def add(self, eng, fn, reads=(), writes=(), dma=None, inc=16, cost=None):
        if cost is None:
            cost = 0.15 if dma is not None else self.DEFCOST[eng]
        tset = None
        if eng == "act" and self.sched and dma is None:
            import inspect
            try:
                src = inspect.getsource(fn)
            except Exception:
                src = ""
            if "AF.Silu" in src:
                tset = "silu"
            elif "AF.Ln" in src or "AF.Exp" in src:
                tset = "exp"
        self.ops.append(dict(eng=eng, fn=fn, reads=tuple(reads) + tuple(self.extra_reads), writes=tuple(writes), dma=dma, inc=inc,
                             cost=cost, sched=self.sched, tset=tset))

    def list_schedule(self, a, b):
        import heapq
        ops = self.ops
        n = b - a
        last_w = {}
        readers = {}
        preds = [set() for _ in range(n)]
        for i in range(a, b):
            op = ops[i]
            ps = preds[i - a]
            for r in op["reads"]:
                if r in last_w:
                    ps.add(last_w[r])
                if isinstance(r, tuple) and r[0] == "ps":
                    for x in readers.get(r, ()):
                        if ops[x]["eng"] != op["eng"]:
                            ps.add(x)
            for w in op["writes"]:
                if w in last_w:
                    ps.add(last_w[w])
                for x in readers.get(w, ()):
                    ps.add(x)
            ps.discard(i)
            for r in op["reads"]:
                readers.setdefault(r, []).append(i)
            for w in op["writes"]:
                last_w[w] = i
                readers[w] = []
        succs = [[] for _ in range(n)]
        indeg = [0] * n
        for i in range(n):
            for p in preds[i]:
                succs[p - a].append(i)
            indeg[i] = len(preds[i])
        LAT = 0.45
        DMA_LAT = 2.5
        finish = [0.0] * n
        tready = [0.0] * n
        heaps = {}
        eng_free = {}
        for i in range(n):
            if indeg[i] == 0:
                heapq.heappush(heaps.setdefault(ops[a + i]["eng"], []), (0.0, i))
        order = []
        act_set = [None]
        while len(order) < n:
            best = None
            for eng, hp in heaps.items():
                if not hp:
                    continue
                tr, i = hp[0]
                st = max(tr, eng_free.get(eng, 0.0))
                if best is None or (st, i) < best[:2]:
                    best = (st, i, eng)
            st, i, eng = best
            heapq.heappop(heaps[eng])
            op = ops[a + i]
            extra = 0.0
            if eng == "act":
                ts = op.get("tset")
                if ts is not None and ts != act_set[0]:
                    hp = heaps[eng]
                    popped = []
                    alt = None
                    while hp and len(popped) < 8:
                        tr2, i2 = heapq.heappop(hp)
                        if max(tr2, eng_free.get(eng, 0.0)) <= st + 0.3 and ops[a + i2].get("tset") in (None, act_set[0]):
                            alt = (tr2, i2)
                            break
                        popped.append((tr2, i2))
                    for it in popped:
                        heapq.heappush(hp, it)
                    if alt is not None:
                        heapq.heappush(hp, (tready[i], i))
                        i = alt[1]
                        op = ops[a + i]
                        st = max(alt[0], eng_free.get(eng, 0.0))
                        ts = op.get("tset")
                if ts is not None and ts != act_set[0]:
                    extra = 1.3
                    act_set[0] = ts
            eng_free[eng] = st + op["cost"] + extra
            finish[i] = st + op["cost"] + extra + (DMA_LAT if op["dma"] is not None else 0.0)
            order.append(a + i)
            for j in succs[i]:
                tready[j] = max(tready[j], finish[i] + LAT)
                indeg[j] -= 1
                if indeg[j] == 0:
                    heapq.heappush(heaps.setdefault(ops[a + j]["eng"], []), (tready[j], j))
        self.ops[a:b] = [ops[k] for k in order]
        print("list_schedule", a, b, "est_us", round(max(finish), 1))

    def emit(self):
        import os
        nc = self.nc
        tr = int(os.environ.get("K_TRUNC", "0"))
        if tr:
            self.ops = self.ops[:tr]
        if os.environ.get("K_NOSCHED", "0") != "1":
            i = 0
            while i < len(self.ops):
                if self.ops[i]["sched"]:
                    j = i
                    while j < len(self.ops) and self.ops[j]["sched"]:
                        j += 1
                    self.list_schedule(i, j)
                    i = j
                else:
                    i += 1
        ops = self.ops
        print("n_ops", len(ops))
        last_w = {}
        readers = {}
        for i, op in enumerate(ops):
            raw = set()
            other = set()
            for r in op["reads"]:
                if r in last_w:
                    raw.add(last_w[r])
                if isinstance(r, tuple) and r[0] == "ps":
                    for x in readers.get(r, ()):
                        if ops[x]["eng"] != op["eng"]:
                            other.add(x)
            for w in op["writes"]:
                if w in last_w:
                    other.add(last_w[w])
                for x in readers.get(w, ()):
                    other.add(x)
            deps = set()
            for d in raw | other:
                if d == i:
                    continue
                od = ops[d]
                if od["dma"] is None and od["eng"] == op["eng"]:
                    if op["eng"] == "pe":
                        continue
                    if d not in raw:
                        continue
                deps.add(d)
            op["deps"] = deps
            for r in op["reads"]:
                readers.setdefault(r, []).append(i)
            for w in op["writes"]:
                last_w[w] = i
                readers[w] = []
        need = [False] * len(ops)
        for op in ops:
            for d in op["deps"]:
                need[d] = True
        cnt = {}
        dcnt = {}
        semnames = set()
        for i, op in enumerate(ops):
            if op["dma"] is not None:
                k = op["dma"]
                dcnt[k] = dcnt.get(k, 0) + 1
                op["sig"] = (("dma", k), dcnt[k] * op["inc"])
                semnames.add(("dma", k))
            elif need[i]:
                c = cnt.get(op["eng"], 0)
                cnt[op["eng"]] = c + 1
                sn = ("eng", op["eng"], c // self.EPOCH)
                op["sig"] = (sn, c % self.EPOCH + 1)
                semnames.add(sn)
            else:
                op["sig"] = None
        semnames = sorted(semnames, key=str)
        with contextlib.ExitStack() as st:
            sems = {}
            for i, sn in enumerate(semnames):
                sems[sn] = st.enter_context(nc.semaphore("s%d" % i))
            block = st.enter_context(nc.Block())
            final_dma = {}
            for op in ops:
                if op["dma"] is not None:
                    final_dma[op["sig"][0]] = max(final_dma.get(op["sig"][0], 0), op["sig"][1])

            def run_engine(engname, e):
                waited = {}
                for op in ops:
                    if op["eng"] != engname:
                        continue
                    want = {}
                    for d in op["deps"]:
                        sn, val = ops[d]["sig"]
                        if want.get(sn, 0) < val:
                            want[sn] = val
                    for sn, val in want.items():
                        if waited.get(sn, 0) >= val:
                            continue
                        e.wait_ge(sems[sn], val)
                        waited[sn] = val
                    ins = op["fn"](e)
                    if op["sig"] is not None:
                        sn, val = op["sig"]
                        if op["dma"] is not None:
                            if op["inc"] == 16:
                                ins.then_inc(sems[sn], 16)
                            else:
                                ins.then_inc(sems[sn])
                        else:
                            ins.then_inc(sems[sn], 1)
                if engname == "sp":
                    for sn, val in final_dma.items():
                        if waited.get(sn, 0) < val:
                            e.wait_ge(sems[sn], val)

            @block.tensor
            def _(e):
                run_engine("pe", e)

            @block.scalar
            def _(e):
                run_engine("act", e)

            @block.vector
            def _(e):
                run_engine("dve", e)

            @block.gpsimd
            def _(e):
                run_engine("pool", e)

            @block.sync
            def _(e):
                run_engine("sp", e)


def build(NPC, NSC, stop_after=9):
    NCH = NPC + NSC
    NTOK = NCH * C
    assert NCH % 4 == 0
    nc = bass.Bass("TRN2", target_bir_lowering=False)
    P = Prog(nc)

    def din(name, shape, dt=F32):
        return nc.dram_tensor(name, shape, dt, kind="ExternalInput").ap()

    xin = din("xin", [NTOK, D])
    wg1 = din("wg1", [D, DFF]); wu1 = din("wu1", [D, DFF]); wd1 = din("wd1", [DFF, D])
    wg2 = din("wg2", [D, DFF]); wu2 = din("wu2", [D, DFF]); wd2 = din("wd2", [DFF, D])
    win = din("win", [D, DFF]); wout = din("wout", [D, D])
    gains = din("gains", [128, 36])
    gfin = din("gfin", [D])
    sink = din("sink", [8]); ldf_d = din("ldf", [4]); ldb_d = din("ldb", [4])
    cst = din("cst", [128, 772])
    flg = din("flg", [128, 2])
    tab = din("tab", [NCH, 128, 144])
    yout = nc.dram_tensor("yout", [NTOK, D], F32, kind="ExternalOutput").ap()
    h_s = nc.dram_tensor("h_s", [NTOK, D], F32).ap()
    pkA = nc.dram_tensor("pkA", [NCH, 128, 320], BF16).ap()
    pkB = nc.dram_tensor("pkB", [NCH, 128, 4096], BF16).ap()
    xkv_in = nc.dram_tensor("xkv_in", [128, 640], BF16)
    xkv_out = nc.dram_tensor("xkv_out", [256, 640], BF16)
    xst_in = nc.dram_tensor("xst_in", [128, 1024], F32)
    xst_out = nc.dram_tensor("xst_out", [256, 1024], F32)

    ARENA_BYTES = 207360
    arena = nc.alloc_sbuf_tensor("arena", [128, ARENA_BYTES // 2], BF16)

    def A(off, n, dt):
        assert off % 64 == 0
        bpe = 4 if dt == F32 else 2
        assert off + n * bpe <= ARENA_BYTES, (off, n)
        v = arena[:, off // 2: off // 2 + n * bpe // 2]
        if dt == F32:
            v = v.bitcast(F32)
        return v

    def v3(ap, a):
        return ap.rearrange("p (a b) -> p a b", a=a)

    banks = [nc.alloc_psum_tensor("psb%d" % i, [128, 512], F32) for i in range(8)]

    def PS(i):
        return banks[i][:]

    def PSB(i):
        return PS(i).bitcast(BF16)

    W0 = 0
    W1 = 90112
    ACT0 = 135168
    CONST0 = 199424
    Wg = v3(A(W0, 8 * DFF, BF16), 8)
    Wu = v3(A(W0 + 45056, 8 * DFF, BF16), 8)
    Wd = v3(A(W1, NFF * D, BF16), NFF)
    Win = v3(A(W1, 8 * DFF, BF16), 8)
    Wout = v3(A(W1, 8 * D, BF16), 8)
    SbS_off = W1 + 16384
    o = CONST0
    identb = A(o, 128, BF16); o += 256
    Mprev = A(o, 128, BF16); o += 256
    Mnext = A(o, 128, BF16); o += 256
    MprevH = A(o, 128, BF16); o += 256
    MnextH = A(o, 128, BF16); o += 256
    DTq = A(o, 512, F32); o += 2048
    GF_OFF = o
    WF = A(o, 512, F32); o += 2048
    WB = A(o, 512, F32); o += 2048
    gn = A(o, 36, F32); o += 192
    small = A(o, 64, F32); o += 256
    ldf = small[:, 0:4]; ldb = small[:, 4:8]; nldf = small[:, 8:12]; cdf = small[:, 12:16]; cdb = small[:, 16:20]
    wkf = small[:, 20:24]; wkb = small[:, 24:28]; esink = small[:, 28:36]; flags = small[:, 36:38]
    epsc = small[:, 38:39]; e1c = small[:, 39:43]
    assert o <= ARENA_BYTES
    CSTF = A(ACT0, 772, F32)
    SETUP_TMP = A(ACT0 + 4096, 512, F32)

    P.add("sp", lambda e: e.dma_start(out=CSTF, in_=cst), writes=["cstf"], dma="cst")
    P.add("sp", lambda e: e.dma_start(out=gn, in_=gains), writes=["gn"], dma="gn")
    P.add("sp", lambda e: e.dma_start(out=ldf, in_=ldf_d.partition_broadcast(128)), writes=["ldf"], dma="ldf")
    P.add("sp", lambda e: e.dma_start(out=ldb, in_=ldb_d.partition_broadcast(128)), writes=["ldb"], dma="ldb")
    P.add("sp", lambda e: e.dma_start(out=esink, in_=sink.partition_broadcast(128)), writes=["esink"], dma="sink")
    P.add("sp", lambda e: e.dma_start(out=flags, in_=flg), writes=["flags"], dma="flg")
    c_ident = CSTF[:, 0:128]; c_L = CSTF[:, 128:256]; c_U = CSTF[:, 256:384]; c_JI = CSTF[:, 384:512]
    c_I1 = CSTF[:, 512:640]; c_CI = CSTF[:, 640:768]; c_P1 = CSTF[:, 768:769]; c_PC = CSTF[:, 769:770]; c_P0 = CSTF[:, 770:771]
    SCALE_K = float(128 ** -0.5)
    P.add("dve", lambda e: e.tensor_copy(out=identb, in_=c_ident), reads=["cstf"], writes=["identb"])
    P.add("dve", lambda e: e.tensor_copy(out=Mprev, in_=c_U), reads=["cstf"], writes=["Mprev"])
    P.add("dve", lambda e: e.tensor_copy(out=Mnext, in_=c_L), reads=["cstf"], writes=["Mnext"])
    P.add("dve", lambda e: e.tensor_scalar(out=MprevH, in0=c_U, scalar1=flags[:, 0:1], scalar2=None, op0=ALU.mult),
          reads=["cstf", "flags"], writes=["MprevH"])
    P.add("dve", lambda e: e.tensor_scalar(out=MnextH, in0=c_L, scalar1=flags[:, 1:2], scalar2=None, op0=ALU.mult),
          reads=["cstf", "flags"], writes=["MnextH"])
    P.add("dve", lambda e: e.memset(epsc, EPS), writes=["epsc"])
    P.add("dve", lambda e: e.tensor_scalar(out=nldf, in0=ldf, scalar1=-1.0, scalar2=None, op0=ALU.mult),
          reads=["ldf"], writes=["nldf"])
    P.add("act", lambda e: e.activation(out=esink, in_=esink, func=AF.Exp), reads=["esink"], writes=["esink"])
    P.add("act", lambda e: e.activation(out=cdf, in_=ldf, func=AF.Exp, scale=float(C)), reads=["ldf"], writes=["cdf"])
    P.add("act", lambda e: e.activation(out=cdb, in_=ldb, func=AF.Exp, scale=float(C)), reads=["ldb"], writes=["cdb"])
    for h in range(4):
        hs = slice(h * 128, (h + 1) * 128)
        P.add("act", lambda e, h=h, hs=hs: e.activation(out=WF[:, hs], in_=c_I1, func=AF.Exp, scale=ldf[:, h:h + 1]),
              reads=["cstf", "ldf"], writes=[("WF", h)])
        P.add("act", lambda e, h=h, hs=hs: e.activation(out=WB[:, hs], in_=c_CI, func=AF.Exp, scale=ldb[:, h:h + 1]),
              reads=["cstf", "ldb"], writes=[("WB", h)])
        P.add("act", lambda e, h=h: e.activation(out=wkf[:, h:h + 1], in_=c_PC, func=AF.Exp, scale=ldf[:, h:h + 1]),
              reads=["cstf", "ldf"], writes=[("wkf", h)])
        P.add("act", lambda e, h=h: e.activation(out=wkb[:, h:h + 1], in_=c_P0, func=AF.Exp, scale=ldb[:, h:h + 1]),
              reads=["cstf", "ldb"], writes=[("wkb", h)])
        P.add("act", lambda e, h=h: e.activation(out=e1c[:, h:h + 1], in_=c_P1, func=AF.Exp, scale=nldf[:, h:h + 1]),
              reads=["cstf", "nldf"], writes=[("e1c", h)])
        tmp = SETUP_TMP[:, 0:128]; tmp2 = SETUP_TMP[:, 128:256]; tmp3 = SETUP_TMP[:, 256:384]
        P.add("dve", lambda e, h=h, tmp=tmp: e.tensor_scalar(out=tmp, in0=c_JI, scalar1=ldb[:, h:h + 1], scalar2=None, op0=ALU.mult),
              reads=["cstf", "ldb"], writes=["stmp"])
        P.add("dve", lambda e, h=h, tmp=tmp, tmp2=tmp2: e.scalar_tensor_tensor(out=tmp2, in0=c_I1, scalar=nldf[:, h:h + 1], in1=tmp,
                                                                        op0=ALU.mult, op1=ALU.add),
              reads=["cstf", "nldf", "stmp"], writes=["stmp2"])
        P.add("act", lambda e, tmp2=tmp2, tmp3=tmp3: e.activation(out=tmp3, in_=tmp2, func=AF.Exp), reads=["stmp2"], writes=["stmp3"])
        P.add("dve", lambda e, tmp3=tmp3: e.tensor_tensor(out=tmp3, in0=tmp3, in1=c_U, op=ALU.mult), reads=["stmp3", "cstf"], writes=["stmp3"])
        P.add("dve", lambda e, h=h, tmp3=tmp3, hs=hs: e.scalar_tensor_tensor(out=DTq[:, hs], in0=c_L, scalar=e1c[:, h:h + 1], in1=tmp3,
                                                                          op0=ALU.mult, op1=ALU.add),
              reads=["cstf", ("e1c", h), "stmp3"], writes=[("DTq", h)])
        P.add("dve", lambda e, hs=hs: e.tensor_scalar(out=DTq[:, hs], in0=DTq[:, hs], scalar1=SCALE_K, scalar2=None, op0=ALU.mult),
              reads=[("DTq", h)], writes=[("DTq", h)])
        P.add("dve", lambda e, h=h: e.tensor_scalar(out=wkf[:, h:h + 1], in0=wkf[:, h:h + 1], scalar1=SCALE_K, scalar2=None, op0=ALU.mult),
              reads=[("wkf", h)], writes=[("wkf", h)])
        P.add("dve", lambda e, h=h: e.tensor_scalar(out=wkb[:, h:h + 1], in0=wkb[:, h:h + 1], scalar1=SCALE_K, scalar2=None, op0=ALU.mult),
              reads=[("wkb", h)], writes=[("wkb", h)])
    CONST_KEYS = ["identb", "Mprev", "Mnext", "MprevH", "MnextH", "epsc", "cdf", "cdb", "esink", "gn"] + \
        [(nm, h) for nm in ("WF", "WB", "wkf", "wkb", "DTq") for h in range(4)]
    P.add("dve", lambda e: e.memset(SETUP_TMP[:, 384:385], 0.0), reads=CONST_KEYS + ["cstf", "stmp3"], writes=["setup_done"])

    def load_w_kf(tag, dst, src, key):
        sv = src.rearrange("(k p) f -> p k f", p=128)
        for qi, (a, b) in enumerate(QS):
            P.add("pool", lambda e, a=a, b=b: e.dma_start(out=dst[:, :, a * 128:b * 128], in_=sv[:, :, a * 128:b * 128]),
                  reads=["setup_done"], writes=[(key, qi)], dma=(tag, key, qi))

    def load_w_d(tag, dst, src, key, after=()):
        sv = src.rearrange("(j p) f -> p j f", p=128)
        for qi, (a, b) in enumerate(QS):
            P.add("pool", lambda e, a=a, b=b: e.dma_start(out=dst[:, a:b, :], in_=sv[:, a:b, :]),
                  reads=["setup_done"] + list(after), writes=[(key, qi)], dma=(tag, key, qi))

    def ffn_phase(tag, src, dst, srckey, dstkey, gcol, final, load_gu, wdsrc, after):
        o = ACT0
        X = [A(o + i * 4096, D, F32) for i in range(2)]; o += 8192
        XN = [A(o + i * 2048, D, BF16) for i in range(4)]; o += 8192
        XT = v3(A(o, 8 * 512, BF16), 8); o += 8192
        HT = v3(A(o, NFF * 512, BF16), NFF); o += NFF * 1024
        SG = [A(o + i * 2048, 512, F32) for i in range(2)]; o += 4096
        XR = [A(o + i * 4096, D, F32) for i in range(3)]; o += 12288
        JUNKF = A(o - 12288 - 4096, D, BF16)
        MS = A(o, 32, F32); o += 128
        GF = None
        if final:
            GF = A(GF_OFF, D, F32)
        assert o <= CONST0, o
        if load_gu:
            load_gu()
        load_w_d(tag, Wd, wdsrc, "Wd", after)
        if final:
            P.add("sp", lambda e: e.dma_start(out=GF, in_=gfin.partition_broadcast(128)), reads=["setup_done"] + list(after),
                  writes=["GF"], dma="gf")
        NB = NCH // 4

        def norm(b):
            for t in range(4):
                n = 4 * b + t
                xs = n % 2
                P.add("sp", lambda e, n=n, xs=xs: e.dma_start(out=X[xs], in_=src[n * 128:(n + 1) * 128, :]),
                      reads=[(srckey, n), "setup_done"] + list(after), writes=[("X", xs)], dma=(tag, "x", xs))
                msn = (tag, "ms", n % 4)
                P.add("act", lambda e, xs=xs, n=n, t=t: e.activation(out=XN[t], in_=X[xs], func=AF.Square, scale=1.0 / 32.0,
                                                            accum_out=MS[:, (n % 4) * 4:(n % 4) * 4 + 1]),
                      reads=[("X", xs)], writes=[msn, ("XN", t)])
                P.add("act", lambda e, n=n: e.activation(out=MS[:, (n % 4) * 4 + 1:(n % 4) * 4 + 2], in_=MS[:, (n % 4) * 4:(n % 4) * 4 + 1],
                                                    func=AF.Ln, bias=epsc, scale=1.0),
                      reads=[msn, "epsc"], writes=[(tag, "ln", n % 4)])
                P.add("act", lambda e, n=n: e.activation(out=MS[:, (n % 4) * 4 + 2:(n % 4) * 4 + 3], in_=MS[:, (n % 4) * 4 + 1:(n % 4) * 4 + 2],
                                                    func=AF.Exp, scale=-0.5),
                      reads=[(tag, "ln", n % 4)], writes=[(tag, "rstd", n % 4)])
                P.add("dve", lambda e, xs=xs, t=t, n=n: e.tensor_scalar(out=XN[t], in0=X[xs], scalar1=MS[:, (n % 4) * 4 + 2:(n % 4) * 4 + 3],
                                                                  scalar2=None, op0=ALU.mult),
                      reads=[("X", xs), (tag, "rstd", n % 4)], writes=[("XN", t)])

        def transp(b):
            for t in range(4):
                bk = 0 if t % 2 == 0 else 7

                def f(e, t=t, bk=bk):
                    for k in range(8):
                        ins = e.transpose(PSB(bk)[:, k * 128:(k + 1) * 128], XN[t][:, k * 128:(k + 1) * 128], identb)
                    return ins
                P.add("pe", f, reads=[("XN", t), "identb"], writes=[("ps", bk)])
                P.add("dve", lambda e, t=t, bk=bk: e.tensor_tensor(out=XT[:, :, t * 128:(t + 1) * 128], in0=v3(PSB(bk), 8),
                                                               in1=gn[:, gcol:gcol + 8].unsqueeze(2).to_broadcast([128, 8, 128]),
                                                               op=ALU.mult),
                      reads=[("ps", bk), "gn"], writes=[("XT", t)])

        def gateup(b):
            for j in range(NFF):
                q = _q_of(j)
                pg = 1 + j % 2
                pu = 3 + j % 2

                def fg(e, j=j, pg=pg):
                    for k in range(8):
                        ins = e.matmul(PS(pg), Wg[:, k, j * 128:(j + 1) * 128], XT[:, k, :], start=(k == 0), stop=(k == 7))
                    return ins

                def fu(e, j=j, pu=pu):
                    for k in range(8):
                        ins = e.matmul(PS(pu), Wu[:, k, j * 128:(j + 1) * 128], XT[:, k, :], start=(k == 0), stop=(k == 7))
                    return ins
                P.add("pe", fg, reads=[("Wg", q)] + [("XT", t) for t in range(4)], writes=[("ps", pg)])
                P.add("pe", fu, reads=[("Wu", q)] + [("XT", t) for t in range(4)], writes=[("ps", pu)])
                P.add("act", lambda e, j=j, pg=pg: e.activation(out=SG[j % 2], in_=PS(pg), func=AF.Silu),
                      reads=[("ps", pg)], writes=[("SG", j % 2)])
                P.add("dve", lambda e, j=j, pu=pu: e.tensor_tensor(out=HT[:, j, :], in0=SG[j % 2], in1=PS(pu), op=ALU.mult),
                      reads=[("SG", j % 2), ("ps", pu)], writes=[("HT", j)])

        def down(b):
            for t in range(4):
                n = 4 * b + t
                rs = n % 3
                P.add("sp", lambda e, n=n, rs=rs: e.dma_start(out=XR[rs], in_=src[n * 128:(n + 1) * 128, :]),
                      reads=[(srckey, n), "setup_done"] + list(after), writes=[("XR", rs, 0), ("XR", rs, 1)], dma=(tag, "xr", rs))
                for half in range(2):
                    pd = 5 + half

                    def fd(e, t=t, half=half, pd=pd):
                        for j in range(NFF):
                            ins = e.matmul(PS(pd), HT[:, j, t * 128:(t + 1) * 128], Wd[:, j, half * 512:(half + 1) * 512],
                                           start=(j == 0), stop=(j == NFF - 1))
                        return ins
                    P.add("pe", fd, reads=[("HT", j) for j in range(NFF)] + [("Wd", q) for q in range(4)], writes=[("ps", pd)])
                    P.add("dve", lambda e, rs=rs, half=half, pd=pd: e.scalar_tensor_tensor(
                        out=XR[rs][:, half * 512:(half + 1) * 512], in0=PS(pd), scalar=0.5, in1=XR[rs][:, half * 512:(half + 1) * 512],
                        op0=ALU.mult, op1=ALU.add), reads=[("ps", pd), ("XR", rs, half)], writes=[("XR", rs, half)])
                if final:
                    c0 = 16
                    P.add("act", lambda e, rs=rs: e.activation(out=JUNKF, in_=XR[rs], func=AF.Square, scale=1.0 / 32.0,
                                                            accum_out=MS[:, c0:c0 + 1]),
                          reads=[("XR", rs, 0), ("XR", rs, 1)], writes=["fms", ("SG", 0)])
                    P.add("act", lambda e: e.activation(out=MS[:, c0 + 1:c0 + 2], in_=MS[:, c0:c0 + 1], func=AF.Ln, bias=epsc, scale=1.0),
                          reads=["fms", "epsc"], writes=["fln"])
                    P.add("act", lambda e: e.activation(out=MS[:, c0 + 2:c0 + 3], in_=MS[:, c0 + 1:c0 + 2], func=AF.Exp, scale=-0.5),
                          reads=["fln"], writes=["frs"])
                    P.add("dve", lambda e, rs=rs: e.scalar_tensor_tensor(out=XR[rs], in0=XR[rs], scalar=MS[:, c0 + 2:c0 + 3], in1=GF,
                                                                      op0=ALU.mult, op1=ALU.mult),
                          reads=["frs", "GF", ("XR", rs, 0), ("XR", rs, 1)], writes=[("XR", rs, 0), ("XR", rs, 1)])
                P.add("sp", lambda e, n=n, rs=rs: e.dma_start(out=dst[n * 128:(n + 1) * 128, :], in_=XR[rs]),
                      reads=[("XR", rs, 0), ("XR", rs, 1)], writes=[(dstkey, n)], dma=(tag, "st", rs))

        norm(0)
        transp(0)
        for b in range(NB):
            if b + 1 < NB:
                norm(b + 1)
            gateup(b)
            if b + 1 < NB:
                transp(b + 1)
            down(b)

    def load_gu1():
        svg = wg1.rearrange("(k p) f -> p k f", p=128)
        svu = wu1.rearrange("(k p) f -> p k f", p=128)
        for qi, (a, b) in enumerate(QS):
            P.add("pool", lambda e, a=a, b=b: e.dma_start(out=Wg[:, :, a * 128:b * 128], in_=svg[:, :, a * 128:b * 128]),
                  reads=["setup_done"], writes=[("Wg", qi)], dma=("p1", "Wg", qi))
            P.add("pool", lambda e, a=a, b=b: e.dma_start(out=Wu[:, :, a * 128:b * 128], in_=svu[:, :, a * 128:b * 128]),
                  reads=["setup_done"], writes=[("Wu", qi)], dma=("p1", "Wu", qi))
    ffn_phase("p1", xin, h_s, "xin", "h_s", 0, False, load_gu1, wd1, ())

    def finish_copy():
        for n in range(NCH):
            P.add("sp", lambda e, n=n: e.dma_start(out=yout[n * 128:(n + 1) * 128, :], in_=h_s[n * 128:(n + 1) * 128, :]),
                  reads=[("h_s", n)], writes=[("yout", n)], dma=("fin", n % 4))
        P.emit()
        return nc
    if stop_after == 1:
        return finish_copy()
    P.sched = True
    P.extra_reads = [("h_s", NCH - 1 - i) for i in range(3)]
    load_w_kf("p2", Win, win, "Wd")
    load_w_kf("p4", Wg, wg2, "Wg")
    load_w_kf("p4", Wu, wu2, "Wu")
    P1_DONE = [("h_s", NCH - 1 - i) for i in range(3)]
    o = ACT0
    HIN = [A(o + i * 4096, D, F32) for i in range(2)]; o += 8192
    UN = [A(o + i * 2048, D, BF16) for i in range(2)]; o += 4096
    UT = [v3(A(o + i * 2048, D, BF16), 8) for i in range(2)]; o += 4096
    TAB = [A(o + i * 576, 144, F32) for i in range(2)]; o += 1152
    PKBt = [A(o + i * 8192, 4096, BF16) for i in range(2)]; o += 16384
    PKAt = [A(o + i * 640, 320, BF16) for i in range(2)]; o += 1280
    RT2 = [[A(o + (j * 8 + i) * 1024, 256, F32) for i in range(8)] for j in range(2)]; o += 16384
    RQTM2 = [A(o + j * 1024, 512, BF16) for j in range(2)]; o += 2048
    RKTM2 = [A(o + j * 1024, 512, BF16) for j in range(2)]; o += 2048
    AQTM2 = [A(o + j * 1024, 512, BF16) for j in range(2)]; o += 2048
    AKTM2 = [A(o + j * 256, 128, BF16) for j in range(2)]; o += 512
    MS2 = A(o, 16, F32); o += 64
    TOT = A(o, 1024, F32); o += 4096
    CDP = A(o, 16, F32); o += 64
    assert o <= CONST0, o
    for i in range(2):
        P.add("pool", lambda e, i=i: e.memset(PKAt[i][:, 128:320], 0.0), reads=["setup_done"] + P1_DONE,
              writes=[("PKA", i)])
        P.add("pool", lambda e, i=i: e.memset(v3(PKAt[i][:, 128:320], 2)[:, :, 64:65], 1.0), reads=[("PKA", i)], writes=[("PKA", i)])
    P.add("pool", lambda e: e.memset(TOT, 0.0), reads=["setup_done"] + P1_DONE, writes=["TOTf", "TOTb"])

    def rotary(eng_mul, src3, cos, sin, dst3, half, tmps, rkeys, wkey, nh):
        x1 = src3[:, :, 0:half]; x2 = src3[:, :, half:2 * half]
        cb = cos.unsqueeze(1).to_broadcast([128, nh, half]); sb = sin.unsqueeze(1).to_broadcast([128, nh, half])
        t = [v3(tm[:, 0:nh * half], nh) for tm in tmps]
        tk = [("RT", id(tm)) for tm in tmps]
        P.add("dve", lambda e: e.tensor_tensor(out=t[0], in0=x1, in1=cb, op=ALU.mult), reads=rkeys, writes=[tk[0]])
        P.add("dve", lambda e: e.tensor_tensor(out=t[1], in0=x2, in1=sb, op=ALU.mult), reads=rkeys, writes=[tk[1]])
        P.add("dve", lambda e: e.tensor_tensor(out=t[2], in0=x2, in1=cb, op=ALU.mult), reads=rkeys, writes=[tk[2]])
        P.add("dve", lambda e: e.tensor_tensor(out=t[3], in0=x1, in1=sb, op=ALU.mult), reads=rkeys, writes=[tk[3]])
        P.add("pool", lambda e: e.tensor_tensor(out=dst3[:, :, 0:half], in0=t[0], in1=t[1], op=ALU.subtract),
              reads=[tk[0], tk[1]], writes=[(wkey, 0)])
        P.add("pool", lambda e: e.tensor_tensor(out=dst3[:, :, half:2 * half], in0=t[2], in1=t[3], op=ALU.add),
              reads=[tk[2], tk[3]], writes=[(wkey, 1)])

    for n in range(NCH):
        s = n % 2
        is_prompt = n < NPC
        RT = RT2[s]; RQTM = RQTM2[s]; RKTM = RKTM2[s]; AQTM = AQTM2[s]; AKTM = AKTM2[s]
        P.add("sp", lambda e, n=n, s=s: e.dma_start(out=HIN[s], in_=h_s[n * 128:(n + 1) * 128, :]),
              reads=[("h_s", n), "setup_done"] + P1_DONE, writes=[("HIN", s)], dma=("p2", "hin", s))
        P.add("sp", lambda e, n=n, s=s: e.dma_start(out=TAB[s], in_=tab[n]), reads=["setup_done"] + P1_DONE,
              writes=[("TAB", s)], dma=("p2", "tab", s))
        P.add("act", lambda e, s=s: e.activation(out=UN[s], in_=HIN[s], func=AF.Square, scale=1.0 / 32.0, accum_out=MS2[:, s * 4:s * 4 + 1]),
              reads=[("HIN", s)], writes=[("ms2", s), ("UN", s)], cost=1.0)
        P.add("act", lambda e, s=s: e.activation(out=MS2[:, s * 4 + 1:s * 4 + 2], in_=MS2[:, s * 4:s * 4 + 1], func=AF.Ln, bias=epsc, scale=1.0),
              reads=[("ms2", s), "epsc"], writes=[("ln2", s)])
        P.add("act", lambda e, s=s: e.activation(out=MS2[:, s * 4 + 2:s * 4 + 3], in_=MS2[:, s * 4 + 1:s * 4 + 2], func=AF.Exp, scale=-0.5),
              reads=[("ln2", s)], writes=[("rs2", s)])
        P.add("act", lambda e, s=s: e.activation(out=UN[s], in_=HIN[s], func=AF.Copy, scale=MS2[:, s * 4 + 2:s * 4 + 3]),
              reads=[("HIN", s), ("rs2", s)], writes=[("UN", s)], cost=1.0)

        def ftr(e, s=s):
            for k in range(8):
                ins = e.transpose(PSB(0)[:, k * 128:(k + 1) * 128], UN[s][:, k * 128:(k + 1) * 128], identb)
            return ins
        P.add("pe", ftr, reads=[("UN", s), "identb"], writes=[("ps", 0)], cost=0.9)
        P.add("dve", lambda e, s=s: e.tensor_tensor(out=UT[s], in0=v3(PSB(0), 8), in1=gn[:, 8:16].unsqueeze(2).to_broadcast([128, 8, 128]),
                                                 op=ALU.mult), reads=[("ps", 0), "gn"], writes=[("UT", s)], cost=0.9)
        for g in range(6):
            w = 512 if g < 5 else 256

            def fp(e, g=g, w=w, s=s):
                for k in range(8):
                    ins = e.matmul(PS(1 + g)[:, 0:w], UT[s][:, k, :], Win[:, k, g * 512:g * 512 + w], start=(k == 0), stop=(k == 7))
                return ins
            P.add("pe", fp, reads=[("UT", s)] + [("Wd", q) for q in range(4)], writes=[("ps", 1 + g)], cost=(2.0 if g < 5 else 1.1))
        PB = PKBt[s]; PA = PKAt[s]
        pkb_key = ("PKB", s)
        P.add("act", lambda e, PB=PB: e.activation(out=PB[:, 3072:3584], in_=PS(4), func=AF.Copy), reads=[("ps", 4)], writes=[(pkb_key, "rv")])
        P.add("act", lambda e, PB=PB: e.activation(out=PB[:, 3584:4096], in_=PS(5), func=AF.Silu), reads=[("ps", 5)], writes=[(pkb_key, "sg")])
        P.add("act", lambda e, PA=PA: e.activation(out=v3(PA[:, 128:320], 2)[:, :, 0:64], in_=v3(PS(6)[:, 128:256], 2), func=AF.Copy),
              reads=[("ps", 6)], writes=[("PKA", s)])
        rotary("dve", v3(PS(2), 4), TAB[s][:, 0:64], TAB[s][:, 64:128], v3(RQTM, 4), 64, RT[0:4], [("ps", 2), ("TAB", s)], ("RQTM", s), 4)
        rotary("dve", v3(PS(3), 4), TAB[s][:, 0:64], TAB[s][:, 64:128], v3(RKTM, 4), 64, RT[4:8], [("ps", 3), ("TAB", s)], ("RKTM", s), 4)
        rotary("dve", v3(PS(1), 8), TAB[s][:, 128:136], TAB[s][:, 136:144], v3(AQTM, 8), 8, RT[0:4], [("ps", 1), ("TAB", s)], ("AQTM", s), 8)
        P.add("act", lambda e, AQTM=AQTM: e.activation(out=v3(AQTM, 8)[:, :, 16:64], in_=v3(PS(1), 8)[:, :, 16:64], func=AF.Copy),
              reads=[("ps", 1)], writes=[(("AQTM", s), 2)])
        rotary("dve", v3(PS(6)[:, 0:128], 2), TAB[s][:, 128:136], TAB[s][:, 136:144], v3(AKTM, 2), 8, RT[4:8], [("ps", 6), ("TAB", s)], ("AKTM", s), 2)
        P.add("act", lambda e, AKTM=AKTM: e.activation(out=v3(AKTM, 2)[:, :, 16:64], in_=v3(PS(6)[:, 0:128], 2)[:, :, 16:64], func=AF.Copy),
              reads=[("ps", 6)], writes=[(("AKTM", s), 2)])

        def ftq(e, RQTM=RQTM, RKTM=RKTM):
            for h in range(4):
                ins = e.transpose(PSB(7)[:, h * 128:(h + 1) * 128], RQTM[:, h * 128:(h + 1) * 128], identb)
            for h in range(4):
                ins = e.transpose(PSB(7)[:, 512 + h * 128:512 + (h + 1) * 128], RKTM[:, h * 128:(h + 1) * 128], identb)
            return ins
        P.add("pe", ftq, reads=[(("RQTM", s), 0), (("RQTM", s), 1), (("RKTM", s), 0), (("RKTM", s), 1), "identb"], writes=[("ps", 7)])
        P.add("dve", lambda e, PB=PB: e.tensor_tensor(out=PB[:, 512:1024], in0=PSB(7)[:, 0:512], in1=WF, op=ALU.mult),
              reads=[("ps", 7)] + [("WF", h) for h in range(4)], writes=[(pkb_key, "qf")])
        P.add("dve", lambda e, PB=PB: e.tensor_tensor(out=PB[:, 1024:1536], in0=PSB(7)[:, 0:512], in1=WB, op=ALU.mult),
              reads=[("ps", 7)] + [("WB", h) for h in range(4)], writes=[(pkb_key, "qb")])
        P.add("act", lambda e, PB=PB: e.activation(out=PB[:, 1536:2048], in_=PSB(7)[:, 512:1024], func=AF.Copy),
              reads=[("ps", 7)], writes=[(pkb_key, "kT")])

        def fta(e, AQTM=AQTM, AKTM=AKTM):
            for h in range(4):
                ins = e.transpose(PSB(1)[:, h * 128:(h + 1) * 128], AQTM[:, h * 128:(h + 1) * 128], identb)
            ins = e.transpose(PSB(1)[:, 512:640], AKTM, identb)
            return ins
        P.add("pe", fta, reads=[(("AQTM", s), 0), (("AQTM", s), 1), (("AQTM", s), 2), (("AKTM", s), 0), (("AKTM", s), 1), (("AKTM", s), 2), "identb"],
              writes=[("ps", 1)])
        P.add("act", lambda e, PB=PB: e.activation(out=PB[:, 0:512], in_=PSB(1)[:, 0:512], func=AF.Copy), reads=[("ps", 1)],
              writes=[(pkb_key, "aq")])
        P.add("act", lambda e, PA=PA: e.activation(out=PA[:, 0:128], in_=PSB(1)[:, 512:640], func=AF.Copy), reads=[("ps", 1)],
              writes=[("PKA", s)])
        P.add("pool", lambda e, PB=PB, RKTM=RKTM: e.tensor_tensor(out=v3(PB[:, 2048:2560], 4), in0=v3(RKTM, 4),
                                                     in1=wkf.unsqueeze(2).to_broadcast([128, 4, 128]), op=ALU.mult),
              reads=[(("RKTM", s), 0), (("RKTM", s), 1)] + [("wkf", h) for h in range(4)], writes=[(pkb_key, "kf")])
        P.add("pool", lambda e, PB=PB, RKTM=RKTM: e.tensor_tensor(out=v3(PB[:, 2560:3072], 4), in0=v3(RKTM, 4),
                                                     in1=wkb.unsqueeze(2).to_broadcast([128, 4, 128]), op=ALU.mult),
              reads=[(("RKTM", s), 0), (("RKTM", s), 1)] + [("wkb", h) for h in range(4)], writes=[(pkb_key, "kb")])
        if is_prompt:
            def fkv(e, PB=PB):
                for d_ in range(2):
                    for h in range(4):
                        ins = e.matmul(PS(2 + d_)[:, h * 128:(h + 1) * 128], PB[:, 2048 + d_ * 512 + h * 128:2048 + d_ * 512 + (h + 1) * 128],
                                       PB[:, 3072 + h * 128:3072 + (h + 1) * 128], start=True, stop=True)
                return ins
            P.add("pe", fkv, reads=[(pkb_key, "kf"), (pkb_key, "kb"), (pkb_key, "rv")], writes=[("ps", 2), ("ps", 3)], cost=0.9)
            P.add("act", lambda e, n=n: e.activation(out=CDP[:, 0:4], in_=ldb, func=AF.Exp, scale=float(C * n)), reads=["ldb"], writes=["cdp"])
            for h in range(4):
                hs = slice(h * 128, (h + 1) * 128)
                P.add("dve", lambda e, h=h, hs=hs: e.scalar_tensor_tensor(out=TOT[:, hs], in0=TOT[:, hs], scalar=cdf[:, h:h + 1], in1=PS(2)[:, hs],
                                                                     op0=ALU.mult, op1=ALU.add),
                      reads=[("ps", 2), "cdf", "TOTf"], writes=["TOTf"])
                P.add("dve", lambda e, h=h, hs=hs: e.scalar_tensor_tensor(out=TOT[:, 512 + h * 128:512 + (h + 1) * 128], in0=PS(3)[:, hs],
                                                                     scalar=CDP[:, h:h + 1], in1=TOT[:, 512 + h * 128:512 + (h + 1) * 128],
                                                                     op0=ALU.mult, op1=ALU.add),
                      reads=[("ps", 3), "cdp", "TOTb"], writes=["TOTb"])
        allpkb = [(pkb_key, x) for x in ("rv", "sg", "qf", "qb", "kT", "aq", "kf", "kb")]
        P.add("sp", lambda e, n=n, PB=PB: e.dma_start(out=pkB[n], in_=PB), reads=allpkb, writes=[("pkB", n)], dma=("p2", "stB", s))
        P.add("sp", lambda e, n=n, PA=PA: e.dma_start(out=pkA[n], in_=PA), reads=[("PKA", s)], writes=[("pkA", n)], dma=("p2", "stA", s))
        if n == 0:
            P.add("sp", lambda e, PA=PA: e.dma_start(out=xkv_in.ap()[:, 0:320], in_=PA), reads=[("PKA", s)], writes=["xkv_in0"], dma="xkv0")
        if n == NPC - 1:
            P.add("sp", lambda e, PA=PA: e.dma_start(out=xkv_in.ap()[:, 320:640], in_=PA), reads=[("PKA", s)], writes=["xkv_in1"], dma="xkv1")
            P.add("sp", lambda e: e.dma_start(out=xst_in.ap(), in_=TOT), reads=["TOTf", "TOTb"], writes=["xst_in"], dma="xst")
            PAIRS = [[0, 1], [2, 3], [4, 5], [6, 7]]
            P.add("pool", lambda e: e.collective_compute("AllGather", ALU.bypass, replica_groups=PAIRS,
                                                         ins=[xkv_in.ap().opt()], outs=[xkv_out.ap().opt()]),
                  reads=["xkv_in0", "xkv_in1"], writes=["xkv_out"], dma="cc_kv", inc=1)
            P.add("pool", lambda e: e.collective_compute("AllGather", ALU.bypass, replica_groups=PAIRS,
                                                         ins=[xst_in.ap().opt()], outs=[xst_out.ap().opt()]),
                  reads=["xst_in"], writes=["xst_out"], dma="cc_st", inc=1)

    if stop_after == 2:
        return finish_copy()
    P2_DONE = [("pkB", NCH - 1), ("pkA", NCH - 1), ("pkB", NCH - 2), ("pkA", NCH - 2), "xkv_in0", "xkv_in1", "xst_in"]
    P.extra_reads = list(P2_DONE)
    P.add("pool", lambda e: e.dma_start(out=Wout, in_=wout.rearrange("(k p) f -> p k f", p=128)), reads=P2_DONE,
          writes=[("Wd", q) for q in range(4)], dma="wout")
    o = SbS_off
    SBS_A = (CONST0 - 0)
    NSB_W1 = (ACT0 - SbS_off) // 1024
    o = ACT0
    NSB_ACT = max(0, max(NPC, NSC) - NSB_W1)
    SBS = [A(SbS_off + i * 1024, 512, BF16) for i in range(NSB_W1)] + [A(o + i * 1024, 512, BF16) for i in range(NSB_ACT)]
    o += NSB_ACT * 1024
    AQ3 = [A(o + i * 1024, 512, BF16) for i in range(3)]; o += 3072
    RET3 = [A(o + i * 7168, 3584, BF16) for i in range(2)]; o += 14336
    KVt = [A(o + i * 640, 320, BF16) for i in range(4)]; o += 2560
    HRES = [A(o + i * 4096, D, F32) for i in range(2)]; o += 8192
    PT = [[A(o + (g * 3 + b) * 1024, 512, BF16) for b in range(3)] for g in range(2)]; o += 6144
    AO = A(o, 512, F32); o += 2048
    CAT = [A(o + i * 2048, D, BF16) for i in range(3)]; o += 6144
    CATT = v3(A(o, D, BF16), 8); o += 2048
    ATM = A(o, 512, BF16); o += 1024
    SF = A(o, 512, F32); o += 2048
    SB = A(o, 512, F32); o += 2048
    SFB = [A(o + i * 1024, 512, BF16) for i in range(2)]; o += 2048
    RN = A(o, 512, F32); o += 2048
    KBV = [A(o + i * 2048, 1024, BF16) for i in range(2)]; o += 4096
    JUNK3 = A(o, 512, BF16); o += 1024
    ST3 = A(o, 64, F32); o += 256
    assert o <= CONST0, o

    def mixer_seq(c0, L, exch):
        if exch:
            P.add("sp", lambda e: e.dma_start(out=SB, in_=xst_out.ap()[128:256, 512:1024]), reads=["xst_out"] + P2_DONE, writes=["SB"], dma="sbinit")
            P.add("dve", lambda e: e.tensor_scalar(out=SB, in0=SB, scalar1=flags[:, 1:2], scalar2=None, op0=ALU.mult),
                  reads=["SB", "flags"], writes=["SB"])
        else:
            P.add("dve", lambda e: e.memset(SB, 0.0), reads=P2_DONE, writes=["SB"])
        for l in range(L - 1, -1, -1):
            n = c0 + l
            s = l % 2
            P.add("act", lambda e, l=l: e.activation(out=SBS[l], in_=SB, func=AF.Copy), reads=["SB"], writes=[("SBS", l)])
            P.add("sp", lambda e, n=n, s=s: e.dma_start(out=KBV[s], in_=pkB[n][:, 2560:3584]), reads=[("pkB", n)] + P2_DONE,
                  writes=[("KBV", s)], dma=("p3", "kbv", s))

            def fkb(e, s=s):
                for h in range(4):
                    ins = e.matmul(PS(4)[:, h * 128:(h + 1) * 128], KBV[s][:, h * 128:(h + 1) * 128], KBV[s][:, 512 + h * 128:512 + (h + 1) * 128],
                                   start=True, stop=True)
                return ins
            P.add("pe", fkb, reads=[("KBV", s)], writes=[("ps", 4)])
            for h in range(4):
                hs = slice(h * 128, (h + 1) * 128)
                P.add("dve", lambda e, h=h, hs=hs: e.scalar_tensor_tensor(out=SB[:, hs], in0=SB[:, hs], scalar=cdb[:, h:h + 1], in1=PS(4)[:, hs],
                                                                     op0=ALU.mult, op1=ALU.add),
                      reads=[("ps", 4), "cdb", "SB"], writes=["SB"])
        if exch:
            P.add("sp", lambda e: e.dma_start(out=SF, in_=xst_out.ap()[0:128, 0:512]), reads=["xst_out"] + P2_DONE, writes=["SF"], dma="sfinit")
            P.add("dve", lambda e: e.tensor_scalar(out=SF, in0=SF, scalar1=flags[:, 0:1], scalar2=None, op0=ALU.mult),
                  reads=["SF", "flags"], writes=["SF"])
        else:
            P.add("dve", lambda e: e.memset(SF, 0.0), reads=P2_DONE, writes=["SF"])
        P.add("act", lambda e: e.activation(out=SFB[0], in_=SF, func=AF.Copy), reads=["SF"], writes=[("SFB", 0)])

        def load_kv(l):
            slot = (l + 1) % 4
            if l == -1:
                src = xkv_out.ap()[0:128, 320:640]; rk = ["xkv_out"]
            elif l == L:
                src = xkv_out.ap()[128:256, 0:320]; rk = ["xkv_out"]
            else:
                src = pkA[c0 + l]; rk = [("pkA", c0 + l)]
            P.add("sp", lambda e, slot=slot, src=src: e.dma_start(out=KVt[slot], in_=src), reads=rk + P2_DONE, writes=[("KV", slot)],
                  dma=("p3", "kv", slot))

        def has_kv(l):
            return (0 <= l < L) or (exch and l in (-1, L))

        def load_attn(l):
            if not (0 <= l < L):
                return
            n = c0 + l
            if has_kv(l + 1):
                load_kv(l + 1)
            P.add("sp", lambda e, n=n, l=l: e.dma_start(out=AQ3[l % 3], in_=pkB[n][:, 0:512]), reads=[("pkB", n)] + P2_DONE,
                  writes=[("AQ3", l % 3)], dma=("p3", "aq", l % 3))

        def load_ret(l):
            if not (0 <= l < L):
                return
            n = c0 + l
            s = l % 2
            P.add("sp", lambda e, n=n, s=s: e.dma_start(out=RET3[s], in_=pkB[n][:, 512:4096]), reads=[("pkB", n)] + P2_DONE,
                  writes=[("RET3", s)], dma=("p3", "ret", s))

        def load_hres(l):
            if not (0 <= l < L):
                return
            n = c0 + l
            s = l % 2
            P.add("sp", lambda e, n=n, s=s: e.dma_start(out=HRES[s], in_=h_s[n * 128:(n + 1) * 128, :]), reads=[("h_s", n)] + P2_DONE,
                  writes=[("HRES", s, 0), ("HRES", s, 1)], dma=("p3", "hres", s))

        def blocks_of(l):
            blocks = []
            if l > 0 or exch:
                blocks.append((l - 1, MprevH if l == 0 else Mprev, "MprevH" if l == 0 else "Mprev"))
            blocks.append((l, None, None))
            if l + 1 < L or exch:
                blocks.append((l + 1, MnextH if l + 1 == L else Mnext, "MnextH" if l + 1 == L else "Mnext"))
            return blocks

        def attn_scores(l, g):
            if not (0 <= l < L):
                return
            AQ = AQ3[l % 3]
            for bi, (bl, M, Mk) in enumerate(blocks_of(l)):
                slot = (bl + 1) % 4
                scb = (g * 3 + bi) % 2
                P.add("pe", lambda e, g=g, slot=slot, scb=scb, AQ=AQ: e.matmul(
                    PS(scb), KVt[slot][g * 64:(g + 1) * 64, 0:128], AQ[g * 64:(g + 1) * 64, 0:512], start=True, stop=True,
                    tile_position=(g * 64, 0)), reads=[("KV", slot), ("AQ3", l % 3)], writes=[("ps", scb)])
                P.add("act", lambda e, g=g, bi=bi, scb=scb: e.activation(out=PT[g][bi], in_=PS(scb), func=AF.Exp, scale=0.125),
                      reads=[("ps", scb)], writes=[("PT", g, bi)])
                if M is not None:
                    P.add("pool", lambda e, g=g, bi=bi, M=M: e.tensor_tensor(out=v3(PT[g][bi], 4), in0=v3(PT[g][bi], 4),
                                                                       in1=M.unsqueeze(1).to_broadcast([128, 4, 128]), op=ALU.mult),
                          reads=[("PT", g, bi), Mk], writes=[("PT", g, bi)])

        def attn_pv(l, g):
            if not (0 <= l < L):
                return
            blocks = blocks_of(l)

            def fpv(e, g=g, blocks=blocks):
                for hh in range(4):
                    for bi, (bl, M, Mk) in enumerate(blocks):
                        slot = (bl + 1) % 4
                        ins = e.matmul(PS(2 + g)[:, hh * 72:hh * 72 + 65], PT[g][bi][:, hh * 128:(hh + 1) * 128],
                                       KVt[slot][:, 128 + g * 96:128 + g * 96 + 65], start=(bi == 0), stop=(bi == len(blocks) - 1))
                return ins
            P.add("pe", fpv, reads=[("PT", g, bi) for bi in range(len(blocks))] + [("KV", (bl + 1) % 4) for bl, _, _ in blocks],
                  writes=[("ps", 2 + g)])
            Og = v3(PS(2 + g)[:, 0:288], 4)
            P.add("dve", lambda e, g=g, Og=Og: e.tensor_tensor(out=ST3[:, g * 4:g * 4 + 4], in0=Og[:, :, 64:65].rearrange("p a b -> p (a b)"),
                                                         in1=esink[:, g * 4:g * 4 + 4], op=ALU.add),
                  reads=[("ps", 2 + g), "esink"], writes=[("den", g)])
            P.add("dve", lambda e, g=g: e.reciprocal(out=ST3[:, 8 + g * 4:8 + g * 4 + 4], in_=ST3[:, g * 4:g * 4 + 4]),
                  reads=[("den", g)], writes=[("rden", g)])
            P.add("dve", lambda e, g=g, Og=Og: e.tensor_tensor(out=v3(AO[:, g * 256:(g + 1) * 256], 4), in0=Og[:, :, 0:64],
                                                         in1=ST3[:, 8 + g * 4:8 + g * 4 + 4].unsqueeze(2).to_broadcast([128, 4, 64]),
                                                         op=ALU.mult),
                  reads=[("ps", 2 + g), ("rden", g)], writes=[("AO", g)])

        def attn_norm(l):
            if not (0 <= l < L):
                return
            c = l % 3
            P.add("act", lambda e: e.activation(out=JUNK3, in_=AO, func=AF.Square, scale=float(512 ** -0.5), accum_out=ST3[:, 16:17]),
                  reads=[("AO", 0), ("AO", 1)], writes=["msa"] + [("junk3", h) for h in range(4)])
            P.add("act", lambda e: e.activation(out=ST3[:, 17:18], in_=ST3[:, 16:17], func=AF.Ln, bias=epsc, scale=1.0), reads=["msa", "epsc"], writes=["lna"])
            P.add("act", lambda e: e.activation(out=ST3[:, 18:19], in_=ST3[:, 17:18], func=AF.Exp, scale=-0.5), reads=["lna"], writes=["rsa"])
            P.add("act", lambda e, c=c: e.activation(out=CAT[c][:, 0:512], in_=AO, func=AF.Copy, scale=ST3[:, 18:19]),
                  reads=["rsa", ("AO", 0), ("AO", 1)], writes=[("CAT", c, 0)])

        def ret_mm(l):
            if not (0 <= l < L):
                return
            s = l % 2
            R3 = RET3[s]

            def fat(e, R3=R3):
                for h in range(4):
                    ins = e.matmul(PS(4)[:, h * 128:(h + 1) * 128], R3[:, 1024 + h * 128:1024 + (h + 1) * 128],
                                   R3[:, h * 128:(h + 1) * 128], start=True, stop=True)
                return ins
            P.add("pe", fat, reads=[("RET3", s)], writes=[("ps", 4)])
            P.add("dve", lambda e: e.tensor_tensor(out=ATM, in0=PS(4), in1=DTq, op=ALU.mult),
                  reads=[("ps", 4)] + [("DTq", h) for h in range(4)], writes=["ATM"])

            def frr(e, R3=R3, l=l, s=s):
                for h in range(4):
                    hs = slice(h * 128, (h + 1) * 128)
                    e.matmul(PS(5)[:, hs], ATM[:, hs], R3[:, 2560 + h * 128:2560 + (h + 1) * 128], start=True, stop=False)
                    e.matmul(PS(5)[:, hs], R3[:, h * 128:(h + 1) * 128], SFB[s][:, hs], start=False, stop=False)
                    ins = e.matmul(PS(5)[:, hs], R3[:, 512 + h * 128:512 + (h + 1) * 128], SBS[l][:, hs], start=False, stop=True)
                return ins
            P.add("pe", frr, reads=["ATM", ("RET3", s), ("SFB", s), ("SBS", l)], writes=[("ps", 5)], cost=1.3)

        def ret_stats(l):
            if not (0 <= l < L):
                return
            P.add("dve", lambda e: e.tensor_reduce(out=ST3[:, 20:24], in_=v3(PS(5), 4), axis=AX.X, op=ALU.add), reads=[("ps", 5)], writes=["s1"])
            P.add("dve", lambda e: e.tensor_scalar(out=ST3[:, 28:32], in0=ST3[:, 20:24], scalar1=1.0 / 128.0, scalar2=None, op0=ALU.mult),
                  reads=["s1"], writes=["mean"])
            for h in range(4):
                hs = slice(h * 128, (h + 1) * 128)
                P.add("dve", lambda e, h=h, hs=hs: e.tensor_scalar(out=RN[:, hs], in0=PS(5)[:, hs], scalar1=ST3[:, 28 + h:29 + h],
                                                              scalar2=None, op0=ALU.subtract),
                      reads=[("ps", 5), "mean"], writes=[("RN", h)])
                P.add("act", lambda e, h=h, hs=hs: e.activation(out=JUNK3[:, hs], in_=RN[:, hs], func=AF.Square,
                                                           scale=float(128 ** -0.5), accum_out=ST3[:, 36 + h:37 + h]),
                      reads=[("RN", h)], writes=[("var", h), ("junk3", h)])
            P.add("act", lambda e: e.activation(out=ST3[:, 40:44], in_=ST3[:, 36:40], func=AF.Ln, bias=epsc, scale=1.0),
                  reads=[("var", h) for h in range(4)] + ["epsc"], writes=["lnr"])
            P.add("act", lambda e: e.activation(out=ST3[:, 44:48], in_=ST3[:, 40:44], func=AF.Exp, scale=-0.5), reads=["lnr"], writes=["rsr"])

        def ret_fin(l):
            if not (0 <= l < L):
                return
            s = l % 2
            c = l % 3
            R3 = RET3[s]
            for h in range(4):
                hs = slice(h * 128, (h + 1) * 128)
                P.add("dve", lambda e, h=h, hs=hs, c=c, R3=R3: e.scalar_tensor_tensor(
                    out=CAT[c][:, 512 + h * 128:512 + (h + 1) * 128], in0=RN[:, hs], scalar=ST3[:, 44 + h:45 + h],
                    in1=R3[:, 3072 + h * 128:3072 + (h + 1) * 128], op0=ALU.mult, op1=ALU.mult),
                      reads=[("RN", h), "rsr", ("RET3", s)], writes=[("CAT", c, 1)])

        def state_upd(l):
            if not (0 <= l < L) or l + 1 >= L:
                return
            s = l % 2
            R3 = RET3[s]

            def fkf(e, R3=R3):
                for h in range(4):
                    ins = e.matmul(PS(4)[:, h * 128:(h + 1) * 128], R3[:, 1536 + h * 128:1536 + (h + 1) * 128],
                                   R3[:, 2560 + h * 128:2560 + (h + 1) * 128], start=True, stop=True)
                return ins
            P.add("pe", fkf, reads=[("RET3", s)], writes=[("ps", 4)])
            for h in range(4):
                hs = slice(h * 128, (h + 1) * 128)
                P.add("dve", lambda e, h=h, hs=hs: e.scalar_tensor_tensor(out=SF[:, hs], in0=SF[:, hs], scalar=cdf[:, h:h + 1], in1=PS(4)[:, hs],
                                                                     op0=ALU.mult, op1=ALU.add),
                      reads=[("ps", 4), "cdf", "SF"], writes=["SF"])
            P.add("act", lambda e, s=s: e.activation(out=SFB[1 - s], in_=SF, func=AF.Copy), reads=["SF"], writes=[("SFB", 1 - s)])

        def outp(l):
            if not (0 <= l < L):
                return
            n = c0 + l
            s = l % 2
            c = l % 3

            def ftc(e, c=c):
                for f in range(8):
                    ins = e.transpose(PSB(6)[:, f * 128:(f + 1) * 128], CAT[c][:, f * 128:(f + 1) * 128], identb)
                return ins
            P.add("pe", ftc, reads=[("CAT", c, 0), ("CAT", c, 1), "identb"], writes=[("ps", 6)], cost=0.9)
            P.add("dve", lambda e: e.tensor_tensor(out=CATT[:, 0:4, :], in0=v3(PSB(6), 8)[:, 0:4, :],
                                                in1=gn[:, 24:28].unsqueeze(2).to_broadcast([128, 4, 128]), op=ALU.mult),
                  reads=[("ps", 6), "gn"], writes=[("CATT", 0)])
            P.add("act", lambda e: e.activation(out=CATT[:, 4:8, :], in_=v3(PSB(6), 8)[:, 4:8, :], func=AF.Copy), reads=[("ps", 6)], writes=[("CATT", 1)])
            for half in range(2):
                def fo(e, half=half):
                    for f in range(8):
                        ins = e.matmul(PS(7), CATT[:, f, :], Wout[:, f, half * 512:(half + 1) * 512], start=(f == 0), stop=(f == 7))
                    return ins
                P.add("pe", fo, reads=[("CATT", 0), ("CATT", 1)] + [("Wd", q) for q in range(4)], writes=[("ps", 7)], cost=2.0)
                P.add("dve", lambda e, s=s, half=half: e.tensor_tensor(out=HRES[s][:, half * 512:(half + 1) * 512], in0=PS(7),
                                                                  in1=HRES[s][:, half * 512:(half + 1) * 512], op=ALU.add),
                      reads=[("ps", 7), ("HRES", s, half)], writes=[("HRES", s, half)])
            P.add("pool", lambda e, n=n, s=s: e.dma_start(out=h_s[n * 128:(n + 1) * 128, :], in_=HRES[s]),
                  reads=[("HRES", s, 0), ("HRES", s, 1)], writes=[("h_s", n)], dma=("p3", "hst", s))

        if exch:
            load_kv(-1)
        load_kv(0)
        load_attn(0)
        load_ret(0)
        load_hres(0)
        load_attn(1)
        attn_scores(0, 0); attn_pv(0, 0); attn_scores(0, 1); attn_pv(0, 1); attn_norm(0)
        load_attn(2)
        for i in range(0, L + 1):
            load_ret(i + 1)
            attn_scores(i + 1, 0)
            ret_mm(i)
            attn_pv(i + 1, 0)
            attn_scores(i + 1, 1)
            ret_stats(i)
            attn_pv(i + 1, 1)
            state_upd(i)
            ret_fin(i)
            outp(i - 1)
            attn_norm(i + 1)
            load_attn(i + 3)
            load_hres(i + 1)

    mixer_seq(NPC, NSC, False)
    mixer_seq(0, NPC, True)

    if stop_after == 3:
        return finish_copy()
    P.sched = False
    P.extra_reads = []
    ffn_phase("p4", h_s, yout, "h_s", "yout", 16, True, None, wd2, [("h_s", NPC - 1), ("h_s", NPC - 2), ("h_s", NCH - 1), ("h_s", NCH - 2)])

    P.emit()
    return nc


def _w_in_perm():
    aq = []
    for hp in range(4):
        for h in (hp, hp + 4):
            aq += list(range(h * 64, (h + 1) * 64))
    o1 = 512; o2 = 640; o3 = 768; o4 = 1280; o5 = 1792; o6 = 2304
    return np.array(aq + list(range(o3, o4)) + list(range(o4, o5)) + list(range(o5, o6)) + list(range(o6, 2816))
                    + list(range(o1, o2)) + list(range(o2, o3)), dtype=np.int64)


def _consts():
    p = np.arange(128, dtype=np.float32)
    j = p[:, None]; i = p[None, :]
    cst = np.zeros((128, 772), np.float32)
    cst[:, 0:128] = np.eye(128, dtype=np.float32)
    cst[:, 128:256] = (j <= i)
    cst[:, 256:384] = (j >= i)
    cst[:, 384:512] = j - i
    cst[:, 512:640] = np.broadcast_to(i + 1.0, (128, 128))
    cst[:, 640:768] = np.broadcast_to(128.0 - i, (128, 128))
    cst[:, 768] = p + 1.0
    cst[:, 769] = 127.0 - p
    cst[:, 770] = p
    return cst


def _tables(pos):
    pos = pos.astype(np.float32)
    inv_a = (np.float32(500000.0) ** (-np.arange(0, 16, 2, dtype=np.float32) / np.float32(16))).astype(np.float32)
    inv_r = (np.float32(10000.0) ** (-np.linspace(0.0, 1.0, 64, dtype=np.float32))).astype(np.float32)
    ang_r = (pos[:, None] * inv_r[None, :]).astype(np.float32)
    ang_a = (pos[:, None] * inv_a[None, :]).astype(np.float32)
    return np.concatenate([np.cos(ang_r), np.sin(ang_r), np.cos(ang_a), np.sin(ang_a)], axis=1).astype(np.float32)


_NC_CACHE = {}


def run_cores(x_prompt, x_sample, W, NPC, NSC, stop_after=9):
    key = (NPC, NSC, stop_after)
    if key not in _NC_CACHE:
        _NC_CACHE[key] = build(NPC, NSC, stop_after)
    nc = _NC_CACHE[key]
    perm = _w_in_perm()
    f = lambda a: np.ascontiguousarray(np.asarray(a, dtype=np.float32))
    gains = np.zeros((128, 36), np.float32)
    gains[:, 0:8] = f(W["ffn1_norm"])[0].reshape(8, 128).T
    gains[:, 8:16] = f(W["mix_norm"])[0].reshape(8, 128).T
    gains[:, 16:24] = f(W["ffn2_norm"])[0].reshape(8, 128).T
    gains[:, 24:28] = f(W["attn_out_norm"])[0].reshape(4, 128).T
    common = dict(
        wg1=f(W["ffn1_w_gate"])[0], wu1=f(W["ffn1_w_up"])[0], wd1=f(W["ffn1_w_down"])[0],
        wg2=f(W["ffn2_w_gate"])[0], wu2=f(W["ffn2_w_up"])[0], wd2=f(W["ffn2_w_down"])[0],
        win=np.ascontiguousarray(f(W["w_in"])[0][:, perm]), wout=f(W["w_out"])[0],
        gains=gains, gfin=f(W["final_norm"]), sink=f(W["attn_sink"])[0],
        ldf=f(W["ret_log_decay_fwd"])[0], ldb=f(W["ret_log_decay_bwd"])[0], cst=_consts(),
    )
    HL = NPC * 128
    in_maps = []
    for i in range(8):
        p, half = i // 2, i % 2
        xin = np.concatenate([x_prompt[p, half * HL:(half + 1) * HL], x_sample[i]], axis=0)
        pos = np.concatenate([np.arange(half * HL, (half + 1) * HL), np.arange(NSC * 128)])
        tab = _tables(pos).reshape(NPC + NSC, 128, 144)
        flg = np.zeros((128, 2), np.float32)
        flg[:, 0] = 1.0 if half == 1 else 0.0
        flg[:, 1] = 1.0 if half == 0 else 0.0
        m = dict(common)
        m.update(xin=np.ascontiguousarray(xin, dtype=np.float32), tab=np.ascontiguousarray(tab), flg=flg)
        in_maps.append(m)
    res = run_bass_kernel_spmd(nc, in_maps, core_ids=list(range(8)))
    yp = np.zeros_like(np.asarray(x_prompt, dtype=np.float32))
    ys = np.zeros_like(np.asarray(x_sample, dtype=np.float32))
    for i in range(8):
        y = np.asarray(res.results[i]["yout"])
        p, half = i // 2, i % 2
        yp[p, half * HL:(half + 1) * HL] = y[:HL]
        ys[i] = y[HL:]
    return yp, ys


def kernel(x_prompt, x_sample, **W):
    x_prompt = np.asarray(x_prompt, dtype=np.float32)
    x_sample = np.asarray(x_sample, dtype=np.float32)
    NPC = x_prompt.shape[1] // 256
    NSC = x_sample.shape[1] // 128
    yp, ys = run_cores(x_prompt, x_sample, W, NPC, NSC)
    return (yp, ys)
```
